# Optimizing a Trainium2 kernel written in Bass

```python
import jax, jax.numpy as jnp
from jax import lax
import numpy as np

D_MODEL = 1024
BATCH = 8
SEQ = 4096
DEPTH = 2

HEAD_DIM = 64
SWA_Q_HEADS = 8
SWA_KV_HEADS = 2
SWA_G = SWA_Q_HEADS // SWA_KV_HEADS
DSA_Q_HEADS = 8
DSA_KV_HEADS = 1
DSA_G = DSA_Q_HEADS // DSA_KV_HEADS
IDX_HEADS = 8
IDX_DIM = 32
TOPK_MAX = 256
WINDOW = 128
BLOCK = 128
ROPE_THETA = 10000.0
PLE_DIM = 256
MIX_WIDTH = (SWA_Q_HEADS + DSA_Q_HEADS) * HEAD_DIM
D_FF = -(-(8 * D_MODEL) // (3 * 256)) * 256
DN_ALPHA = (2 * DEPTH) ** 0.25
DN_BETA = (8 * DEPTH) ** -0.25
LN_EPS = 1e-5
NEG = -1e30
IN_SIZES = (SWA_Q_HEADS * HEAD_DIM, SWA_KV_HEADS * HEAD_DIM, SWA_KV_HEADS * HEAD_DIM,
            DSA_Q_HEADS * HEAD_DIM, DSA_KV_HEADS * HEAD_DIM, DSA_KV_HEADS * HEAD_DIM,
            IDX_HEADS * IDX_DIM, IDX_DIM, IDX_HEADS)
IN_COLS = sum(IN_SIZES)

kernel_name = "hymba_swa_sink_dsa_deepnorm_ple"


def layer_norm(x, g, b):
    xf = x.astype(jnp.float32)
    mu = xf.mean(-1, keepdims=True)
    var = jnp.square(xf - mu).mean(-1, keepdims=True)
    y = (xf - mu) * lax.rsqrt(var + LN_EPS)
    return (y * g.astype(jnp.float32) + b.astype(jnp.float32)).astype(x.dtype)


def rope(x, positions):
    half = x.shape[-1] // 2
    inv = ROPE_THETA ** (-jnp.arange(half, dtype=jnp.float32) / half)
    ang = positions.astype(jnp.float32)[..., None] * inv
    ang = ang.reshape(ang.shape[:2] + (1,) * (x.ndim - 3) + (half,))
    cos, sin = jnp.cos(ang), jnp.sin(ang)
    x1 = x[..., :half].astype(jnp.float32)
    x2 = x[..., half:].astype(jnp.float32)
    return jnp.concatenate([x1 * cos - x2 * sin, x2 * cos + x1 * sin], -1).astype(x.dtype)


def swa_sink_attention(q, k, v, sinks):
    B, S, Hkv, G, D = q.shape
    nb = S // BLOCK
    qb = q.reshape(B, nb, BLOCK, Hkv, G, D)
    kb = k.reshape(B, nb, BLOCK, Hkv, D)
    vb = v.reshape(B, nb, BLOCK, Hkv, D)
    prev = lambda t: jnp.concatenate([jnp.zeros_like(t[:, :1]), t[:, :-1]], axis=1)
    kk = jnp.concatenate([prev(kb), kb], axis=2)
    vv = jnp.concatenate([prev(vb), vb], axis=2)
    s = jnp.einsum('bnqhgd,bnkhd->bnhgqk', qb, kk).astype(jnp.float32) * (D ** -0.5)
    i = jnp.arange(BLOCK)[:, None]
    j = jnp.arange(2 * BLOCK)[None, :]
    diff = BLOCK + i - j
    band = (diff >= 0) & (diff < WINDOW)
    not_pad = (jnp.arange(nb)[:, None, None] > 0) | (j[None] >= BLOCK)
    mask = band[None] & not_pad
    s = jnp.where(mask[None, :, None, None], s, NEG)
    sink = sinks.astype(jnp.float32)[None, None, :, :, None, None]
    m = jnp.maximum(s.max(-1, keepdims=True), sink)
    pr = jnp.exp(s - m)
    denom = pr.sum(-1, keepdims=True) + jnp.exp(sink - m)
    o = jnp.einsum('bnhgqk,bnkhd->bnqhgd', (pr / denom).astype(v.dtype), vv)
    return o.reshape(B, S, Hkv * G * D)


def dsa_sparse_attention(q, k, v, qi, ki, wi, k_top):
    B, S, Hkv, G, D = q.shape
    nb = S // BLOCK
    to_blocks = lambda t: jnp.moveaxis(t.reshape((B, nb, BLOCK) + t.shape[2:]), 1, 0)
    gather = jax.vmap(lambda kv, ix: kv[ix])
    key_pos = jnp.arange(S)

    def one_block(args):
        n, qb, qib, wib = args
        t = n * BLOCK + jnp.arange(BLOCK)
        sc = jnp.einsum('bqhd,bsd->bqhs', qib, ki).astype(jnp.float32)
        idx_score = jnp.einsum('bqhs,bqh->bqs', jax.nn.relu(sc), wib.astype(jnp.float32))
        causal = key_pos[None, :] <= t[:, None]
        idx_score = jnp.where(causal[None], idx_score, NEG)
        _, sel = lax.top_k(idx_score, k_top)
        ks = gather(k, sel)
        vs = gather(v, sel)
        s = jnp.einsum('bqhgd,bqjhd->bqhgj', qb, ks).astype(jnp.float32) * (D ** -0.5)
        valid = sel <= t[None, :, None]
        s = jnp.where(valid[:, :, None, None, :], s, NEG)
        pr = jax.nn.softmax(s, axis=-1)
        o = jnp.einsum('bqhgj,bqjhd->bqhgd', pr.astype(v.dtype), vs)
        return o.reshape(B, BLOCK, Hkv * G * D)

    out = lax.map(one_block, (jnp.arange(nb), to_blocks(q), to_blocks(qi), to_blocks(wi)))
    return jnp.moveaxis(out, 0, 1).reshape(B, S, Hkv * G * D)


def hybrid_layer(x, p_i, positions, w_in, sinks, idx_k_g, idx_k_b, w_o, ln1_g, ln1_b,
                 w_gu, w_down, w_pg, w_pp, ln2_g, ln2_b):
    B, S, _ = x.shape
    k_top = min(TOPK_MAX, S // 4)
    h = x @ w_in
    cuts = [int(c) for c in np.cumsum(IN_SIZES)[:-1]]
    qa, ka, va, qb, kb, vb, qi, ki, wi = jnp.split(h, cuts, axis=-1)
    qa = rope(qa.reshape(B, S, SWA_KV_HEADS, SWA_G, HEAD_DIM), positions)
    ka = rope(ka.reshape(B, S, SWA_KV_HEADS, HEAD_DIM), positions)
    va = va.reshape(B, S, SWA_KV_HEADS, HEAD_DIM)
    qb = rope(qb.reshape(B, S, DSA_KV_HEADS, DSA_G, HEAD_DIM), positions)
    kb = rope(kb.reshape(B, S, DSA_KV_HEADS, HEAD_DIM), positions)
    vb = vb.reshape(B, S, DSA_KV_HEADS, HEAD_DIM)
    qi = rope(qi.reshape(B, S, IDX_HEADS, IDX_DIM), positions)
    ki = rope(layer_norm(ki, idx_k_g, idx_k_b), positions)
    wi = wi * (IDX_HEADS ** -0.5 * IDX_DIM ** -0.5)
    oa = swa_sink_attention(qa, ka, va, sinks.reshape(SWA_KV_HEADS, SWA_G))
    ob = dsa_sparse_attention(qb, kb, vb, qi, ki, wi, k_top)
    mix = jnp.concatenate([oa, ob], axis=-1) @ w_o
    x = layer_norm(DN_ALPHA * x + mix, ln1_g, ln1_b)
    g, u = jnp.split(x @ w_gu, 2, axis=-1)
    ffn = (jax.nn.silu(g) * u) @ w_down
    ple = (p_i @ w_pp) * jax.nn.sigmoid(x @ w_pg)
    return layer_norm(DN_ALPHA * x + ffn + ple, ln2_g, ln2_b)


def setup_inputs(seed: int = 0) -> dict:
    key = jax.random.key(seed)
    ks = jax.random.split(key, 20)
    nrm = lambda k, shape, scale: jax.random.normal(k, shape, jnp.float32) * scale
    L = DEPTH
    offs = jax.random.randint(ks[2], (BATCH, 1), 0, 1024)
    positions = (offs + jnp.arange(SEQ)[None, :]).astype(jnp.int32)
    return {
        "x": nrm(ks[0], (BATCH, SEQ, D_MODEL), 1.0),
        "p": nrm(ks[1], (DEPTH, BATCH, SEQ, PLE_DIM), 1.0),
        "positions": positions,
        "ln_in_g": 1.0 + nrm(ks[3], (D_MODEL,), 0.02),
        "ln_in_b": nrm(ks[4], (D_MODEL,), 0.02),
        "w_in": nrm(ks[5], (L, D_MODEL, IN_COLS), D_MODEL ** -0.5),
        "attn_sinks": nrm(ks[6], (L, SWA_Q_HEADS), 0.5),
        "idx_k_g": 1.0 + nrm(ks[7], (L, IDX_DIM), 0.02),
        "idx_k_b": nrm(ks[8], (L, IDX_DIM), 0.02),
        "w_o": nrm(ks[9], (L, MIX_WIDTH, D_MODEL), DN_BETA * MIX_WIDTH ** -0.5),
        "ln1_g": 1.0 + nrm(ks[10], (L, D_MODEL), 0.02),
        "ln1_b": nrm(ks[11], (L, D_MODEL), 0.02),
        "w_gu": nrm(ks[12], (L, D_MODEL, 2 * D_FF), D_MODEL ** -0.5),
        "w_down": nrm(ks[13], (L, D_FF, D_MODEL), DN_BETA * D_FF ** -0.5),
        "w_pg": nrm(ks[14], (L, D_MODEL, D_MODEL), D_MODEL ** -0.5),
        "w_pp": nrm(ks[15], (L, PLE_DIM, D_MODEL), DN_BETA * PLE_DIM ** -0.5),
        "ln2_g": 1.0 + nrm(ks[16], (L, D_MODEL), 0.02),
        "ln2_b": nrm(ks[17], (L, D_MODEL), 0.02),
    }


def reference(x, p, positions, ln_in_g, ln_in_b, w_in, attn_sinks, idx_k_g, idx_k_b, w_o,
              ln1_g, ln1_b, w_gu, w_down, w_pg, w_pp, ln2_g, ln2_b):
    x = layer_norm(x, ln_in_g, ln_in_b)
    for i in range(DEPTH):
        x = hybrid_layer(x, p[i], positions, w_in[i], attn_sinks[i], idx_k_g[i], idx_k_b[i],
                         w_o[i], ln1_g[i], ln1_b[i], w_gu[i], w_down[i], w_pg[i], w_pp[i],
                         ln2_g[i], ln2_b[i])
    return x
```

```python
import numpy as np
import concourse.bass as bass
import concourse.mybir as mybir
from concourse.bass_utils import run_bass_kernel_spmd

F32 = mybir.dt.float32
BF16 = mybir.dt.bfloat16
I32 = mybir.dt.int32
AF = mybir.ActivationFunctionType
ALU = mybir.AluOpType
AX = mybir.AxisListType

ENGS = ["pe", "act", "dve", "pool", "sp"]
NDMASEM = 14
EPOCH = 4000
ILV_BIAS = 0.7
STRICT = True

D = 1024
DFF = 2816
INC = 1704
ALPHA = float(4 ** 0.25)
EPS = 1e-5
TWO_PI = 6.283185307179586
C1 = 6.28125
C2 = TWO_PI - C1


class Buf:
    __slots__ = ("name", "last_w", "readers")

    def __init__(self, name):
        self.name = name
        self.last_w = None
        self.readers = []


class Sched:
    def __init__(self, nc):
        self.nc = nc
        self.ops = {e: [] for e in ENGS}
        self.count = {e: 0 for e in ENGS}
        self.epoch = {e: 0 for e in ENGS}
        self.esems = {e: [nc.alloc_semaphore(f"s_{e}_0")] for e in ENGS}
        self.dq = {"sp": 0, "pool": 1, "act": 2}
        self.dsems = [nc.alloc_semaphore(f"s_dma_{i}") for i in range(3 * NDMASEM)]
        self.dma_n = [0, 0, 0]
        self.dma_last = [0] * (3 * NDMASEM)
        self.waited = {e: {} for e in ENGS}

    def _tok(self, eng):
        if self.count[eng] >= EPOCH:
            self.epoch[eng] += 1
            self.count[eng] = 0
            self.esems[eng].append(self.nc.alloc_semaphore(f"s_{eng}_{self.epoch[eng]}"))
        self.count[eng] += 1
        return ("e", eng, self.epoch[eng], self.count[eng])

    def op(self, eng, fn, reads=(), writes=(), dma=False, extra=()):
        if getattr(self, "dry", False):
            return None
        deps = set(extra)
        for b in reads:
            if b.last_w is not None:
                deps.add(b.last_w)
        for b in writes:
            if b.last_w is not None and (STRICT or dma or b.last_w[0] != "e" or b.last_w[1] != eng):
                deps.add(b.last_w)
            for r in b.readers:
                if STRICT or dma or r[0] != "e" or r[1] != eng:
                    deps.add(r)
        if eng == "pe" and not dma:
            deps = {d for d in deps if not (d[0] == "e" and d[1] == "pe")}
        if dma:
            q = self.dq[eng]
            i = self.dma_n[q]
            self.dma_n[q] += 1
            si = q * NDMASEM + (i % NDMASEM)
            val = self.dma_last[si] + 16
            if self.dma_last[si] > 0:
                deps.add(("d", si, 0, self.dma_last[si]))
            self.dma_last[si] = val
            tok = ("d", si, 0, val)
        else:
            tok = self._tok(eng)
        need = {}
        for d in deps:
            key = d[:3]
            need[key] = max(need.get(key, 0), d[3])
        waits = []
        w = self.waited[eng]
        for key, v in need.items():
            if key[0] == "e":
                done = False
                for k2, v2 in w.items():
                    if k2[0] == "e" and k2[1] == key[1] and (k2[2] > key[2] or (k2[2] == key[2] and v2 >= v)):
                        done = True
                        break
                if done:
                    continue
            else:
                if w.get(key, 0) >= v:
                    continue
            w[key] = max(w.get(key, 0), v)
            waits.append((key, v))
        self.ops[eng].append((waits, fn, tok))
        for b in reads:
            b.readers.append(tok)
        for b in writes:
            b.last_w = tok
            b.readers = []
        return tok

    def all_tokens(self):
        toks = []
        for e in ENGS:
            if self.count[e] > 0 or self.epoch[e] > 0:
                toks.append(("e", e, self.epoch[e], self.count[e]))
        for si in range(3 * NDMASEM):
            if self.dma_last[si] > 0:
                toks.append(("d", si, 0, self.dma_last[si]))
        return toks

    def barrier(self):
        toks = self.all_tokens()
        for e in ENGS:
            waits = []
            for t in toks:
                if t[0] == "e" and t[1] == e:
                    continue
                waits.append((t[:3], t[3]))
                self.waited[e][t[:3]] = max(self.waited[e].get(t[:3], 0), t[3])
            self.ops[e].append((waits, None, None))

    def _sem(self, key):
        if key[0] == "e":
            return self.esems[key[1]][key[2]]
        return self.dsems[key[1]]

    def emit(self):
        nc = self.nc
        handles = {"pe": "tensor", "act": "scalar", "dve": "vector", "pool": "gpsimd", "sp": "sync"}
        with nc.Block() as block:
            for e in ENGS:
                ops = self.ops[e]

                def body(engh, ops=ops):
                    for waits, fn, tok in ops:
                        for key, v in waits:
                            engh.wait_ge(self._sem(key), v)
                        if fn is None:
                            continue
                        ins = fn(engh)
                        ins.then_inc(self._sem(tok[:3]), 1 if tok[0] == "e" else 16)

                getattr(block, handles[e])(body)


class T:
    def __init__(self, ap, name):
        self.ap = ap
        self.b = Buf(name)

    def __getitem__(self, k):
        return self.ap[k]


class Arena:
    def __init__(self, nc, nbytes, name):
        self.t = nc.alloc_sbuf_tensor(name, [128, nbytes // 2], BF16)
        self.n = nbytes
        self.off = 0

    def reset(self):
        self.off = 0

    def alloc_at(self, name, shape, dtype, off):
        save, lim = self.off, getattr(self, "limit", self.n)
        self.off, self.limit = off, self.n
        t = self.alloc(name, shape, dtype)
        self.off, self.limit = save, lim
        return t

    def alloc(self, name, shape, dtype):
        size = 2 if dtype == BF16 else 4
        nel = int(np.prod(shape))
        nb = nel * size
        off = (self.off + 31) // 32 * 32
        self.off = off + nb
        assert self.off <= getattr(self, "limit", self.n), (name, self.off, getattr(self, "limit", self.n))
        ap = self.t[:, off // 2: off // 2 + nb // 2]
        if dtype != BF16:
            ap = ap.bitcast(dtype)
        if len(shape) == 2:
            ap = ap.rearrange("p (a b) -> p a b", b=shape[1])
        elif len(shape) == 3:
            ap = ap.rearrange("p (a b c) -> p a b c", b=shape[1], c=shape[2])
        return T(ap, name)


def bcast(ap, n):
    s = list(ap.shape)
    return ap.unsqueeze(1).to_broadcast([s[0], n] + s[1:])


class Builder:
    def __init__(self, NT=32, KTOP=256, DEPTH=2, NITER=16):
        self.NT, self.KTOP, self.DEPTH, self.NITER = NT, KTOP, DEPTH, NITER
        self.S_ = NT * 128
        nc = bass.Bass("TRN2", target_bir_lowering=False)
        self.nc = nc
        self.S = Sched(nc)
        S_ = self.S_
        dt = nc.dram_tensor
        self.x_d = dt("x", [S_, D], F32, kind="ExternalInput").ap()
        self.p_d = dt("p", [DEPTH, S_, 256], F32, kind="ExternalInput").ap()
        self.pos_d = dt("pos", [128, NT], I32, kind="ExternalInput").ap()
        self.inv_d = dt("inv", [48], F32, kind="ExternalInput").ap()
        self.lnp_d = dt("lnp", [1 + 2 * DEPTH, 2 * D], F32, kind="ExternalInput").ap()
        self.win_d = dt("w_in", [DEPTH, D, INC], F32, kind="ExternalInput").ap()
        self.wo_d = dt("w_o", [DEPTH, D, D], F32, kind="ExternalInput").ap()
        self.wgu_d = dt("w_gu", [DEPTH, D, 2 * DFF], F32, kind="ExternalInput").ap()
        self.wdn_d = dt("w_down", [DEPTH, DFF, D], F32, kind="ExternalInput").ap()
        self.wpg_d = dt("w_pg", [DEPTH, D, D], F32, kind="ExternalInput").ap()
        self.wpp_d = dt("w_pp", [DEPTH, 256, D], F32, kind="ExternalInput").ap()
        self.sk_d = dt("sinks", [DEPTH, 8], F32, kind="ExternalInput").ap()
        self.ik_d = dt("ikgb", [DEPTH, 64], F32, kind="ExternalInput").ap()
        self.y_d = dt("y", [S_, D], F32, kind="ExternalOutput").ap()
        self.xs_d = T(dt("xs", [S_, D], F32).ap(), "xs_d")
        self.x1_d = T(dt("x1s", [S_, D], F32).ap(), "x1_d")
        self.hid_d = T(dt("hids", [22, 128, S_], BF16).ap(), "hid_d")
        self.yb = Buf("y_d")
        self.ps = []
        self.psfull = []
        for i in range(4):
            pt = nc.alloc_psum_tensor(f"pp{i}", [128, 1024], F32)
            self.psfull.append(pt[:])
            self.ps.append(T(pt[:, 0:512], f"ps{2 * i}"))
            self.ps.append(T(pt[:, 512:1024], f"ps{2 * i + 1}"))
        self.pers = Arena(nc, 22720, "pers")
        self.ar = Arena(nc, 189760, "arena")
        self.build()

    def V(self, fn, r, w):
        return self.S.op("dve", fn, [t.b if isinstance(t, T) else t for t in r], [t.b if isinstance(t, T) else t for t in w])

    def A(self, fn, r, w):
        return self.S.op("act", fn, [t.b if isinstance(t, T) else t for t in r], [t.b if isinstance(t, T) else t for t in w])

    def G(self, fn, r, w):
        return self.S.op("pool", fn, [t.b if isinstance(t, T) else t for t in r], [t.b if isinstance(t, T) else t for t in w])

    def P(self, fn, r, w):
        return self.S.op("pe", fn, [t.b if isinstance(t, T) else t for t in r], [t.b if isinstance(t, T) else t for t in w])

    def DMA(self, eng, fn, r, w):
        return self.S.op(eng, fn, [t.b if isinstance(t, T) else t for t in r], [t.b if isinstance(t, T) else t for t in w], dma=True)

    def layer_norm(self, xt, yt, gb, width, g_ap, b_ap, scr, gb_eng="pool"):
        st, mv, sc = scr
        nchunk = (width + 511) // 512
        for c in range(nchunk):
            lo, hi = c * 512, min(width, (c + 1) * 512)
            self.V(lambda e, c=c, lo=lo, hi=hi: e.bn_stats(out=st[:, c, :], in_=xt[:, lo:hi]), [xt], [st])
        self.V(lambda e: e.bn_aggr(out=mv[:, 0:2], in_=st[:, 0:nchunk, :].rearrange("p a b -> p (a b)")), [st], [mv])
        self.V(lambda e: e.tensor_scalar(out=sc[:, 0:1], in0=mv[:, 1:2], scalar1=EPS, scalar2=None, op0=ALU.add), [mv], [sc])
        self.A(lambda e: e.activation(out=sc[:, 1:2], in_=sc[:, 0:1], func=AF.Ln), [sc], [sc])
        self.A(lambda e: e.activation(out=sc[:, 2:3], in_=sc[:, 1:2], func=AF.Exp, scale=-0.5), [sc], [sc])
        self.V(lambda e: e.tensor_scalar(out=sc[:, 3:4], in0=mv[:, 0:1], scalar1=sc[:, 2:3], scalar2=-1.0, op0=ALU.mult, op1=ALU.mult), [mv, sc], [sc])
        self.V(lambda e: e.tensor_scalar(out=yt[:, 0:width], in0=xt[:, 0:width], scalar1=sc[:, 2:3], scalar2=sc[:, 3:4], op0=ALU.mult, op1=ALU.add), [xt, sc], [yt])
        E_ = self.G if gb_eng == "pool" else self.V
        E_(lambda e: e.tensor_tensor(out=yt[:, 0:width], in0=yt[:, 0:width], in1=g_ap, op=ALU.mult), [yt, gb], [yt])
        E_(lambda e: e.tensor_tensor(out=yt[:, 0:width], in0=yt[:, 0:width], in1=b_ap, op=ALU.add), [yt, gb], [yt])

    def ln_scratch(self, alloc, tag):
        return (alloc("ln_st" + tag, [2, 6], F32), alloc("ln_mv" + tag, [4], F32), alloc("ln_sc" + tag, [8], F32))

    def freg(self, e, val):
        if not hasattr(self, "_fregs"):
            self._fregs = {}
        if val not in self._fregs:
            self._fregs[val] = e.to_reg(val)
        return self._fregs[val]

    def load_w(self, dst, src_ap):
        self.DMA("pool", lambda e: e.dma_start(out=dst.ap, in_=src_ap), [], [dst])

    def transposes(self, bank, specs, col0=0):
        bv = bank.ap.bitcast(BF16)
        for i, (srcT, sap, F) in enumerate(specs):
            c = col0 + i * 128
            self.P(lambda e, c=c, sap=sap, F=F: e.transpose(out=bv[0:F, c:c + 128], in_=sap, identity=self.ident[:]),
                   [srcT, self.ident], [bank])
        return bv

    @staticmethod
    def interleave(gx, nx, gy, ny):
        ix = iy = 0
        ax, ay = gx is not None, gy is not None
        while ax or ay:
            stepx = ax and (not ay or ix * max(ny, 1) * ILV_BIAS <= iy * max(nx, 1))
            if stepx:
                try:
                    next(gx)
                    ix += 1
                except StopIteration:
                    ax = False
            else:
                try:
                    next(gy)
                    iy += 1
                except StopIteration:
                    ay = False

    @staticmethod
    def merge(g1, n1, g2, n2):
        i1 = i2 = 0
        a1, a2 = g1 is not None, g2 is not None
        while a1 or a2:
            step1 = a1 and (not a2 or i1 * max(n2, 1) <= i2 * max(n1, 1))
            if step1:
                try:
                    next(g1)
                    i1 += 1
                    yield
                except StopIteration:
                    a1 = False
            else:
                try:
                    next(g2)
                    i2 += 1
                    yield
                except StopIteration:
                    a2 = False

    def count(self, gen):
        self.S.dry = True
        n = sum(1 for _ in gen)
        self.S.dry = False
        return n

    def build(self):
        NT, S_, DEPTH = self.NT, self.S_, self.DEPTH
        pa = self.pers.alloc
        self.ident = pa("ident", [128], BF16)
        self.m12 = pa("m12", [2, 128], BF16)
        self.ones64 = pa("ones64", [64], BF16)
        self.cs64 = pa("cs64", [NT, 2, 32], F32)
        self.cs32 = pa("cs32", [NT, 2, 16], F32)
        self.lnb = pa("lnb", [2, D], F32)
        self.esink = pa("esink", [8], F32)
        self.ikgb = pa("ikgb", [2, 32], F32)
        self.mb12 = pa("mb12", [2, 128], BF16)
        self.cb = pa("cb", [NT], F32)
        self.p2n = pa("p2n", [24], F32)
        self.negbig = pa("negbig", [1], F32)
        self.scrA = self.ln_scratch(pa, "A")
        self.scrB = self.ln_scratch(pa, "B")
        top = (self.ar.n - (8 * INC * 2 + 8 * D * 2)) // 32 * 32
        self.ar.limit = top
        self.w_in_T = self.ar.alloc_at("w_in", [8, INC], BF16, top)
        self.w_o_T = self.ar.alloc_at("w_o", [8, D], BF16, top + 8 * INC * 2)
        self.setup_consts()
        self.load_mixer_weights(0)
        self.phase_ln_in()
        for l in range(DEPTH):
            self.phase_mixer(l)
            self.phase_ffn_a(l)
            self.phase_ffn_b(l)
        self.S.barrier()
        self.S.emit()

    def setup_consts(self):
        NT = self.NT
        ident, m12, ones64 = self.ident, self.m12, self.ones64
        self.G(lambda e: e.memset(ident[:], 1.0), [], [ident])
        self.G(lambda e: e.affine_select(out=ident[:], in_=ident[:], pattern=[[1, 128]], compare_op=ALU.is_equal, fill=self.freg(e, 0.0), base=0, channel_multiplier=-1), [ident], [ident])
        self.G(lambda e: e.memset(m12[:], 1.0), [], [m12])
        self.G(lambda e: e.affine_select(out=m12[:, 0, :], in_=m12[:, 0, :], pattern=[[1, 128]], compare_op=ALU.is_ge, fill=self.freg(e, 0.0), base=0, channel_multiplier=-1), [m12], [m12])
        self.G(lambda e: e.affine_select(out=m12[:, 1, :], in_=m12[:, 1, :], pattern=[[-1, 128]], compare_op=ALU.is_ge, fill=self.freg(e, 0.0), base=-1, channel_multiplier=1), [m12], [m12])
        self.G(lambda e: e.memset(ones64[:], 1.0), [], [ones64])
        self.V(lambda e: e.tensor_scalar(out=self.mb12[:], in0=m12[:], scalar1=30000.0, scalar2=-30000.0, op0=ALU.mult, op1=ALU.add), [m12], [self.mb12])
        self.G(lambda e: e.memset(self.negbig[:], -30000.0), [], [self.negbig])
        for n_ in range(NT):
            self.G(lambda e, n_=n_: e.memset(self.cb[:, n_:n_ + 1], float(-(2 * self.KTOP - (n_ + 1) * 128) + 0.5)), [], [self.cb])
        for k_ in range(24):
            self.G(lambda e, k_=k_: e.memset(self.p2n[:, k_:k_ + 1], float(-(2.0 ** (-k_)))), [], [self.p2n])
        self.ar.reset()
        aa = self.ar.alloc
        posi = aa("posi", [NT], I32)
        posf = aa("posf", [NT], F32)
        invb = aa("invb", [48], F32)
        ang = aa("ang", [NT, 48], F32)
        a2 = aa("a2", [NT, 48], F32)
        ki = aa("ki", [NT, 48], I32)
        kf = aa("kf", [NT, 48], F32)
        self.DMA("sp", lambda e: e.dma_start(out=posi.ap, in_=self.pos_d), [], [posi])
        self.DMA("sp", lambda e: e.dma_start(out=invb.ap, in_=self.inv_d.partition_broadcast(128)), [], [invb])
        self.V(lambda e: e.tensor_copy(out=posf[:], in_=posi[:]), [posi], [posf])
        self.V(lambda e: e.tensor_tensor(out=ang[:], in0=posf[:].unsqueeze(2).to_broadcast([128, NT, 48]), in1=bcast(invb[:], NT), op=ALU.mult), [posf, invb], [ang])
        for which in range(2):
            if which == 0:
                self.V(lambda e: e.tensor_scalar(out=a2[:], in0=ang[:], scalar1=float(np.pi / 2), scalar2=None, op0=ALU.add), [ang], [a2])
            else:
                self.V(lambda e: e.tensor_copy(out=a2[:], in_=ang[:]), [ang], [a2])
            self.V(lambda e: e.tensor_scalar(out=ki[:], in0=a2[:], scalar1=float(1.0 / TWO_PI), scalar2=None, op0=ALU.mult), [a2], [ki])
            self.V(lambda e: e.tensor_copy(out=kf[:], in_=ki[:]), [ki], [kf])
            self.V(lambda e: e.scalar_tensor_tensor(out=a2[:], in0=kf[:], scalar=-C1, in1=a2[:], op0=ALU.mult, op1=ALU.add), [kf, a2], [a2])
            self.V(lambda e: e.scalar_tensor_tensor(out=a2[:], in0=kf[:], scalar=-C2, in1=a2[:], op0=ALU.mult, op1=ALU.add), [kf, a2], [a2])
            self.V(lambda e: e.tensor_scalar(out=a2[:], in0=a2[:], scalar1=-3.1415925, scalar2=3.1415925, op0=ALU.max, op1=ALU.min), [a2], [a2])
            self.A(lambda e, which=which: e.activation(out=self.cs64[:, :, which, :], in_=a2[:, :, 0:32], func=AF.Sin), [a2], [self.cs64])
            self.A(lambda e, which=which: e.activation(out=self.cs32[:, :, which, :], in_=a2[:, :, 32:48], func=AF.Sin), [a2], [self.cs32])

    def load_mixer_weights(self, l):
        self.load_w(self.w_in_T, self.win_d[l].rearrange("(k p) c -> p k c", p=128))
        self.load_w(self.w_o_T, self.wo_d[l].rearrange("(k p) c -> p k c", p=128))

    def load_ln(self, idx):
        self.DMA("sp", lambda e: e.dma_start(out=self.lnb.ap.rearrange("p a b -> p (a b)"), in_=self.lnp_d[idx:idx + 1, :].to_broadcast([128, 2 * D])), [], [self.lnb])

    def phase_ln_in(self):
        self.S.barrier()
        self.ar.reset()
        self.load_ln(0)
        xt = [self.ar.alloc(f"xt{i}", [D], F32) for i in range(3)]

        def load(n):
            x = xt[n % 3]
            self.DMA("sp", lambda e: e.dma_start(out=x.ap, in_=self.x_d[n * 128:(n + 1) * 128, :]), [], [x])

        for n in range(min(2, self.NT)):
            load(n)
        for n in range(self.NT):
            x = xt[n % 3]
            if n + 2 < self.NT:
                load(n + 2)
            self.layer_norm(x, x, self.lnb, D, self.lnb[:, 0, :], self.lnb[:, 1, :], self.scrA if n % 2 == 0 else self.scrB)
            self.DMA("sp", lambda e, n=n, x=x: e.dma_start(out=self.xs_d.ap[n * 128:(n + 1) * 128, :], in_=x.ap), [x], [self.xs_d])

    def rope(self, hsb, c0, H, Dh, cs, n, dstT, dst4, t1, t2, perm=False):
        half = Dh // 2
        if perm:
            src = hsb[:, c0:c0 + H * Dh].rearrange("p (hi lo d) -> p hi lo d", hi=2, d=Dh)
            dst = dst4.rearrange("p (lo hi) d -> p hi lo d", hi=2)
            tv = lambda t: t[:, 0:H * half].rearrange("p (hi lo d) -> p hi lo d", hi=2, d=half)
            bc = lambda ap: ap.unsqueeze(1).unsqueeze(1).to_broadcast([128, 2, H // 2, half])
            sl = lambda ap, a, b: ap[:, :, :, a:b]
        else:
            src = hsb[:, c0:c0 + H * Dh].rearrange("p (h d) -> p h d", d=Dh)
            dst = dst4
            tv = lambda t: t[:, 0:H * half].rearrange("p (h d) -> p h d", d=half)
            bc = lambda ap: ap.unsqueeze(1).to_broadcast([128, H, half])
            sl = lambda ap, a, b: ap[:, :, a:b]
        x1, x2 = sl(src, 0, half), sl(src, half, Dh)
        cosb, sinb = bc(cs[:, n, 0, :]), bc(cs[:, n, 1, :])
        a, b = tv(t1), tv(t2)
        self.V(lambda e: e.tensor_tensor(out=a, in0=x1, in1=cosb, op=ALU.mult), [hsb, cs], [t1])
        self.V(lambda e: e.tensor_tensor(out=b, in0=x2, in1=sinb, op=ALU.mult), [hsb, cs], [t2])
        self.V(lambda e: e.tensor_tensor(out=sl(dst, 0, half), in0=a, in1=b, op=ALU.subtract), [t1, t2], [dstT])
        self.V(lambda e: e.tensor_tensor(out=a, in0=x2, in1=cosb, op=ALU.mult), [hsb, cs], [t1])
        self.V(lambda e: e.tensor_tensor(out=b, in0=x1, in1=sinb, op=ALU.mult), [hsb, cs], [t2])
        self.V(lambda e: e.tensor_tensor(out=sl(dst, half, Dh), in0=a, in1=b, op=ALU.add), [t1, t2], [dstT])

    def phase_mixer(self, l):
        NT, S_ = self.NT, self.S_
        self.S.barrier()
        self.ar.reset()
        aa = self.ar.alloc
        B = {}
        B["w_in"] = self.w_in_T
        B["w_o"] = self.w_o_T
        B["kaT"] = aa("kaT", [S_], BF16)
        B["kbT"] = aa("kbT", [S_], BF16)
        B["kiT"] = aa("kiT", [S_], BF16)
        B["va"] = aa("va", [NT, 128], BF16)
        B["vbx"] = aa("vbx", [NT, 128], BF16)
        B["idx"] = aa("idx", [S_], F32)
        B["mask"] = [aa(f"mask{i}", [S_], BF16) for i in range(2)]
        B["xb"] = [aa(f"xb{i}", [D], BF16) for i in range(2)]
        B["xT"] = aa("xT", [8, 128], BF16)
        B["hsb"] = aa("hsb", [INC], F32)
        B["t1"] = aa("t1", [256], F32)
        B["t2"] = aa("t2", [256], F32)
        B["qa_r"] = aa("qa_r", [8, 64], BF16)
        B["qb_r"] = aa("qb_r", [8, 64], BF16)
        B["ka_r"] = aa("ka_r", [2, 64], BF16)
        B["kb_r"] = aa("kb_r", [2, 64], BF16)
        B["qi_f"] = T(B["hsb"].ap[:, 0:256].rearrange("p (h d) -> p h d", d=32), "qi_f")
        B["qi_f"].b = B["hsb"].b
        B["qi_r"] = aa("qi_r", [8, 32], BF16)
        B["ki_l"] = aa("ki_l", [32], F32)
        B["ki_r"] = aa("ki_r", [1, 32], BF16)
        B["wsm"] = [aa(f"wsm{i}", [3, 8], F32) for i in range(2)]
        B["qaTz"] = [aa(f"qaTz{i}", [4, 128], BF16) for i in range(2)]
        B["qbTz"] = [[aa(f"qbTz{i}{j}", [4, 128], BF16) for j in range(2)] for i in range(3)]
        B["qiT"] = [aa(f"qiT{i}", [8, 128], BF16) for i in range(2)]
        B["E_f"] = aa("E_f", [1024], BF16)
        B["E_b"] = [aa(f"E_b{i}", [1024], BF16) for i in range(2)]
        B["mT"] = [aa(f"mT{i}", [8, 128], BF16) for i in range(2)]
        B["R"] = [aa(f"R{i}", [2, 512], F32) for i in range(2)]
        B["oT2"] = [aa(f"oT2{i}", [8, 128], BF16) for i in range(3)]
        B["ostg_f"] = aa("ostg_f", [2, 128], BF16)
        B["ostg_b"] = aa("ostg_b", [2, 128], BF16)
        B["rec_f"] = aa("rec_f", [512], F32)
        rec2 = aa("rec2", [1024], F32)
        B["rec_hi"] = T(rec2.ap, "rec_hi")
        B["rec_lo"] = T(rec2.ap, "rec_lo")
        B["xres"] = aa("xres", [D], F32)
        B["yt"] = aa("yt", [D], F32)
        B["bs"] = aa("bs", [40], F32)
        B["m8"] = aa("m8", [8], F32)

        for nm in ("kaT", "kbT", "kiT", "va", "vbx"):
            B[nm + "_b"] = [Buf(f"{nm}_{j}") for j in range(NT)]
        w_in, w_o = B["w_in"], B["w_o"]
        self.load_ln(1 + 2 * l)
        self.DMA("sp", lambda e: e.dma_start(out=self.esink.ap, in_=self.sk_d[l:l + 1, :].to_broadcast([128, 8])), [], [self.esink])
        self.A(lambda e: e.activation(out=self.esink[:], in_=self.esink[:], func=AF.Exp), [self.esink], [self.esink])
        self.DMA("sp", lambda e: e.dma_start(out=self.ikgb.ap.rearrange("p a b -> p (a b)"), in_=self.ik_d[l:l + 1, :].to_broadcast([128, 64])), [], [self.ikgb])
        self.G(lambda e: e.memset(B["vbx"][:], 1.0), [], [B["vbx"]])
        self.G(lambda e: e.memset(B["kiT"][:], 0.0), [], [B["kiT"]])
        for t_ in B["qaTz"] + B["qbTz"][0] + B["qbTz"][1] + B["qbTz"][2] + B["qiT"]:
            self.G(lambda e, t_=t_: e.memset(t_[:], 0.0), [], [t_])

        self.S.barrier()
        cFE = [self.count(self.gen_FE(l, n, B)) for n in range(NT)]
        cIX = [self.count(self.gen_IDX(l, n, B)) for n in range(NT)]
        cBS = [self.count(self.gen_BIS(l, n, B)) for n in range(NT)]
        cY = [self.count(self.gen_Y(l, n, B)) for n in range(NT)]

        def chainA(n):
            if n + 1 < NT:
                yield from self.gen_IDX(l, n + 1, B)
                g1, c1 = self.gen_BIS(l, n + 1, B), cBS[n + 1]
            else:
                g1, c1 = None, 0
            if n + 2 < NT:
                g2, c2 = self.gen_FE(l, n + 2, B), cFE[n + 2]
            else:
                g2, c2 = None, 0
            yield from self.merge(g1, c1, g2, c2)

        def lenA(n):
            return (cIX[n + 1] + cBS[n + 1] if n + 1 < NT else 0) + (cFE[n + 2] if n + 2 < NT else 0)

        for g in (self.gen_FE(l, 0, B), self.gen_IDX(l, 0, B), self.gen_BIS(l, 0, B)):
            for _ in g:
                pass
        if NT > 1:
            for _ in self.gen_FE(l, 1, B):
                pass
        for n in range(NT):
            self.interleave(chainA(n), lenA(n), self.gen_Y(l, n, B), cY[n])

    def gen_FE(self, l, n, B):
        NT, S_, KTOP = self.NT, self.S_, self.KTOP
        ps = self.ps
        par = n % 2
        nb = slice(n * 128, (n + 1) * 128)
        w_in, kaT, kbT, kiT, va, vbx, idx = B["w_in"], B["kaT"], B["kbT"], B["kiT"], B["va"], B["vbx"], B["idx"]
        mask, xb, xT, hsb, t1, t2 = B["mask"][par], B["xb"][par], B["xT"], B["hsb"], B["t1"], B["t2"]
        qa_r, qb_r, ka_r, kb_r, qi_f, qi_r, ki_l, ki_r = B["qa_r"], B["qb_r"], B["ka_r"], B["kb_r"], B["qi_f"], B["qi_r"], B["ki_l"], B["ki_r"]
        wsm, qaTz, qbTz, qiT = B["wsm"][par], B["qaTz"], B["qbTz"][n % 3], B["qiT"][par]
        E, R, oT2, ostg, rec = B["E_f"], B["R"], B["oT2"][n % 3], B["ostg_f"], B["rec_f"]
        psf = self.psfull
        bs, m8 = B["bs"], B["m8"]
        self.DMA("pool", lambda e: e.dma_start(out=xb.ap, in_=self.xs_d.ap[n * 128:(n + 1) * 128, :]), [self.xs_d], [xb])
        yield
        bv = self.transposes(ps[2], [(xb, xb[:, k * 128:(k + 1) * 128], 128) for k in range(8)])
        self.V(lambda e, bv=bv: e.tensor_copy(out=xT[:].rearrange("p a b -> p (a b)"), in_=bv[:, 0:1024]), [ps[2]], [xT])
        yield
        groups = [(0, 512), (512, 256), (768, 512), (1280, 128), (1408, 296)]
        for gi, (c0, w) in enumerate(groups):
            bank = ps[gi % 2]
            for k in range(8):
                self.P(lambda e, bank=bank, k=k, c0=c0, w=w: e.matmul(bank[:, 0:w], lhsT=xT[:, k, :], rhs=w_in[:, k, c0:c0 + w], start=(k == 0), stop=(k == 7)),
                       [xT, w_in], [bank])
            self.V(lambda e, bank=bank, c0=c0, w=w: e.tensor_copy(out=hsb[:, c0:c0 + w], in_=bank[:, 0:w]), [bank], [hsb])
            yield
        self.rope(hsb, 0, 8, 64, self.cs64, n, qa_r, qa_r[:], t1, t2, perm=True)
        yield
        self.rope(hsb, 512, 2, 64, self.cs64, n, ka_r, ka_r[:], t1, t2)
        self.V(lambda e: e.tensor_copy(out=va[:, n, :], in_=hsb[:, 640:768]), [hsb], [B["va_b"][n]])
        yield
        self.rope(hsb, 768, 8, 64, self.cs64, n, qb_r, qb_r[:], t1, t2, perm=True)
        yield
        self.rope(hsb, 1280, 1, 64, self.cs64, n, kb_r, kb_r[:, 0:1, :], t1, t2)
        self.V(lambda e: e.tensor_copy(out=kb_r[:, 1, :], in_=kb_r[:, 0, :]), [kb_r], [kb_r])
        self.V(lambda e: e.tensor_copy(out=vbx[:, n, 0:64], in_=hsb[:, 1344:1408]), [hsb], [B["vbx_b"][n]])
        self.V(lambda e: e.tensor_scalar(out=wsm[:, 0, :], in0=hsb[:, 1696:1704], scalar1=0.0625, scalar2=None, op0=ALU.mult), [hsb], [wsm])
        self.V(lambda e: e.tensor_scalar(out=wsm[:, 2, :], in0=wsm[:, 0, :], scalar1=0.0, scalar2=2.0, op0=ALU.is_ge, op1=ALU.mult), [wsm], [wsm])
        self.V(lambda e: e.tensor_scalar(out=wsm[:, 2, :], in0=wsm[:, 2, :], scalar1=-1.0, scalar2=None, op0=ALU.add), [wsm], [wsm])
        self.V(lambda e: e.tensor_tensor(out=wsm[:, 1, :], in0=wsm[:, 0, :], in1=wsm[:, 2, :], op=ALU.mult), [wsm], [wsm])
        yield
        self.rope(hsb, 1408, 8, 32, self.cs32, n, qi_f, qi_f[:], t1, t2)
        self.V(lambda e: e.tensor_tensor(out=qi_r[:], in0=qi_f[:], in1=wsm[:, 1, :].unsqueeze(2).to_broadcast([128, 8, 32]), op=ALU.mult), [qi_f, wsm], [qi_r])
        yield
        kiv = T(hsb[:, 1664:1696], "kiv")
        kiv.b = hsb.b
        self.layer_norm(kiv, ki_l, self.ikgb, 32, self.ikgb[:, 0, :], self.ikgb[:, 1, :], self.scrA, gb_eng="dve")
        self.V(lambda e: e.tensor_copy(out=hsb[:, 1664:1696], in_=ki_l[:]), [ki_l], [hsb])
        self.rope(hsb, 1664, 1, 32, self.cs32, n, ki_r, ki_r[:], t1, t2)
        yield
        bv = self.transposes(ps[2], [(qa_r, qa_r[:].rearrange("p a b -> p (a b)")[:, i * 128:(i + 1) * 128], 128) for i in range(4)])
        for g in range(2):
            pp = slice(64 * g, 64 * g + 64)
            self.V(lambda e, bv=bv, g=g, pp=pp: e.tensor_copy(out=qaTz[g][pp].rearrange("p a b -> p (a b)"), in_=bv[pp, 0:512]), [ps[2]], [qaTz[g]])
        bv = self.transposes(ps[2], [(qb_r, qb_r[:].rearrange("p a b -> p (a b)")[:, i * 128:(i + 1) * 128], 128) for i in range(4)], col0=512)
        for g in range(2):
            pp = slice(64 * g, 64 * g + 64)
            self.V(lambda e, bv=bv, g=g, pp=pp: e.tensor_copy(out=qbTz[g][pp].rearrange("p a b -> p (a b)"), in_=bv[pp, 512:1024]), [ps[2]], [qbTz[g]])
        yield
        bv = self.transposes(ps[0], [(qi_r, qi_r[:, h, :], 32) for h in range(8)])
        self.V(lambda e, bv=bv: e.tensor_copy(out=qiT[0:32].rearrange("p a b -> p (a b)"), in_=bv[0:32, 0:1024]), [ps[0]], [qiT])
        bv = self.transposes(ps[1], [(ka_r, ka_r[:].rearrange("p a b -> p (a b)"), 128), (kb_r, kb_r[:].rearrange("p a b -> p (a b)"), 128), (ki_r, ki_r[:, 0, :], 32)])
        self.V(lambda e, bv=bv: e.tensor_copy(out=kaT[:, nb], in_=bv[:, 0:128]), [ps[1]], [B["kaT_b"][n]])
        self.V(lambda e, bv=bv: e.tensor_copy(out=kbT[:, nb], in_=bv[:, 128:256]), [ps[1]], [B["kbT_b"][n]])
        self.V(lambda e, bv=bv: e.tensor_copy(out=kiT[0:32, nb], in_=bv[0:32, 256:384]), [ps[1]], [B["kiT_b"][n]])
        yield
        for g in range(2):
            rhs_q = qaTz[g][:].rearrange("p a b -> p (a b)")
            self.P(lambda e, rhs_q=rhs_q: e.matmul(ps[0][:, 0:512], lhsT=kaT[:, nb], rhs=rhs_q, start=True, stop=False), [B["kaT_b"][n], qaTz[g]], [ps[0]])
            self.P(lambda e: e.matmul(ps[0][:, 0:512], lhsT=self.ident[:], rhs=bcast(self.mb12[:, 0, :], 4), start=False, stop=True), [self.ident, self.mb12], [ps[0]])
            if n > 0:
                pb = slice((n - 1) * 128, n * 128)
                self.P(lambda e, rhs_q=rhs_q, pb=pb: e.matmul(ps[1][:, 0:512], lhsT=kaT[:, pb], rhs=rhs_q, start=True, stop=False), [B["kaT_b"][n - 1], qaTz[g]], [ps[1]])
                self.P(lambda e: e.matmul(ps[1][:, 0:512], lhsT=self.ident[:], rhs=bcast(self.mb12[:, 1, :], 4), start=False, stop=True), [self.ident, self.mb12], [ps[1]])
                self.A(lambda e: e.activation(out=E[:, 0:1024], in_=psf[0][:, 0:1024], func=AF.Exp, scale=0.125), [ps[0], ps[1]], [E])
            else:
                self.A(lambda e: e.activation(out=E[:, 0:512], in_=ps[0][:, 0:512], func=AF.Exp, scale=0.125), [ps[0]], [E])
            self.P(lambda e, g=g: e.matmul(ps[2][0:64, 0:512], lhsT=va[:, n, g * 64:(g + 1) * 64], rhs=E[:, 0:512], start=True, stop=(n == 0)), [B["va_b"][n], E], [ps[2]])
            self.P(lambda e: e.matmul(ps[0][0:64, 0:512], lhsT=self.ones64[:], rhs=E[:, 0:512], start=True, stop=(n == 0)), [self.ones64, E], [ps[0]])
            if n > 0:
                self.P(lambda e, g=g: e.matmul(ps[2][0:64, 0:512], lhsT=va[:, n - 1, g * 64:(g + 1) * 64], rhs=E[:, 512:1024], start=False, stop=True), [B["va_b"][n - 1], E], [ps[2]])
                self.P(lambda e: e.matmul(ps[0][0:64, 0:512], lhsT=self.ones64[:], rhs=E[:, 512:1024], start=False, stop=True), [self.ones64, E], [ps[0]])
            for hh in range(4):
                self.V(lambda e, g=g, hh=hh: e.tensor_scalar(out=rec[0:64, hh * 128:(hh + 1) * 128], in0=ps[0][0:64, hh * 128:(hh + 1) * 128], scalar1=self.esink[0:64, 4 * g + hh:4 * g + hh + 1], scalar2=None, op0=ALU.add), [ps[0], self.esink], [rec])
            self.A(lambda e: e.activation(out=rec[0:64, 0:512], in_=rec[0:64, 0:512], func=AF.Ln), [rec], [rec])
            self.A(lambda e: e.activation(out=rec[0:64, 0:512], in_=rec[0:64, 0:512], func=AF.Exp, scale=-1.0), [rec], [rec])
            ev = lambda ap: ap.rearrange("p (a two b) -> p a two b", two=2, b=128)
            self.V(lambda e, g=g: e.tensor_tensor(out=oT2[0:64, 2 * g:2 * g + 2, :], in0=ev(ps[2][0:64, 0:512])[:, :, 0, :], in1=ev(rec[0:64, 0:512])[:, :, 0, :], op=ALU.mult), [ps[2], rec], [oT2])
            self.V(lambda e: e.tensor_tensor(out=ostg[0:64, :, :], in0=ev(ps[2][0:64, 0:512])[:, :, 1, :], in1=ev(rec[0:64, 0:512])[:, :, 1, :], op=ALU.mult), [ps[2], rec], [ostg])
            self.DMA("sp", lambda e, g=g: e.dma_start(out=oT2[64:128, 2 * g:2 * g + 2, :], in_=ostg[0:64, :, :]), [ostg], [oT2])
            yield

    def gen_IDX(self, l, n, B):
        NT, S_, KTOP = self.NT, self.S_, self.KTOP
        ps = self.ps
        par = n % 2
        nb = slice(n * 128, (n + 1) * 128)
        w_in, kaT, kbT, kiT, va, vbx, idx = B["w_in"], B["kaT"], B["kbT"], B["kiT"], B["va"], B["vbx"], B["idx"]
        mask, xb, xT, hsb, t1, t2 = B["mask"][par], B["xb"][par], B["xT"], B["hsb"], B["t1"], B["t2"]
        qa_r, qb_r, ka_r, kb_r, qi_f, qi_r, ki_l, ki_r = B["qa_r"], B["qb_r"], B["ka_r"], B["kb_r"], B["qi_f"], B["qi_r"], B["ki_l"], B["ki_r"]
        wsm, qaTz, qbTz, qiT = B["wsm"][par], B["qaTz"], B["qbTz"][n % 3], B["qiT"][par]
        E, R, oT2, ostg, rec = B["E_f"], B["R"], B["oT2"][n % 3], B["ostg_f"], B["rec_f"]
        psf = self.psfull
        bs, m8 = B["bs"], B["m8"]
        Nn = (n + 1) * 128
        nblk = (Nn + 511) // 512
        cnt = 0
        for jb in range(nblk):
            Wb = min(512, Nn - jb * 512)
            cs_ = slice(jb * 512, jb * 512 + Wb)
            for h2 in range(4):
                pr = cnt % 2
                Rb = R[pr]
                cnt += 1
                for u in range(2):
                    h = 2 * h2 + u
                    bank = ps[2 * pr + u]
                    self.P(lambda e, bank=bank, h=h, cs_=cs_, Wb=Wb: e.matmul(bank[:, 0:Wb], lhsT=qiT[:, h, :], rhs=kiT[:, cs_], start=True, stop=True), [qiT] + B["kiT_b"][4 * jb:min(n + 1, 4 * jb + 4)], [bank])
                if False and h2 == 3:
                    for u in range(2):
                        h = 2 * h2 + u
                        bank = ps[2 * pr + u]
                        self.V(lambda e, bank=bank, Rb=Rb, u=u, h=h, Wb=Wb: e.tensor_scalar(out=Rb[:, u, 0:Wb], in0=bank[:, 0:Wb], scalar1=0.0, scalar2=wsm[:, 2, h:h + 1], op0=ALU.max, op1=ALU.mult), [bank, wsm], [Rb])
                        self.V(lambda e, Rb=Rb, u=u, cs_=cs_, Wb=Wb: e.tensor_tensor(out=idx[:, cs_], in0=idx[:, cs_], in1=Rb[:, u, 0:Wb], op=ALU.add), [Rb, idx], [idx])
                    yield
                    continue
                self.A(lambda e, pr=pr, Rb=Rb, Wb=Wb: e.activation(out=Rb[:, :, 0:Wb], in_=psf[pr].rearrange("p (a b) -> p a b", b=512)[:, :, 0:Wb], func=AF.Relu), [ps[2 * pr], ps[2 * pr + 1]], [Rb])
                for u in range(2):
                    h = 2 * h2 + u
                    if h == 0:
                        self.V(lambda e, Rb=Rb, cs_=cs_, Wb=Wb: e.tensor_scalar(out=idx[:, cs_], in0=Rb[:, 0, 0:Wb], scalar1=wsm[:, 2, 0:1], scalar2=None, op0=ALU.mult), [Rb, wsm], [idx])
                    else:
                        self.V(lambda e, Rb=Rb, cs_=cs_, Wb=Wb, h=h, u=u: e.scalar_tensor_tensor(out=idx[:, cs_], in0=Rb[:, u, 0:Wb], scalar=wsm[:, 2, h:h + 1], in1=idx[:, cs_], op0=ALU.mult, op1=ALU.add), [Rb, wsm, idx], [idx])
                yield
        self.G(lambda e: e.affine_select(out=idx[:, nb], in_=idx[:, nb], pattern=[[-1, 128]], compare_op=ALU.is_ge, fill=self.freg(e, -1e30), base=0, channel_multiplier=1), [idx], [idx])

    def gen_BIS(self, l, n, B):
        NT, S_, KTOP = self.NT, self.S_, self.KTOP
        ps = self.ps
        par = n % 2
        nb = slice(n * 128, (n + 1) * 128)
        w_in, kaT, kbT, kiT, va, vbx, idx = B["w_in"], B["kaT"], B["kbT"], B["kiT"], B["va"], B["vbx"], B["idx"]
        mask, xb, xT, hsb, t1, t2 = B["mask"][par], B["xb"][par], B["xT"], B["hsb"], B["t1"], B["t2"]
        qa_r, qb_r, ka_r, kb_r, qi_f, qi_r, ki_l, ki_r = B["qa_r"], B["qb_r"], B["ka_r"], B["kb_r"], B["qi_f"], B["qi_r"], B["ki_l"], B["ki_r"]
        wsm, qaTz, qbTz, qiT = B["wsm"][par], B["qaTz"], B["qbTz"][n % 3], B["qiT"][par]
        E, R, oT2, ostg, rec = B["E_f"], B["R"], B["oT2"][n % 3], B["ostg_f"], B["rec_f"]
        psf = self.psfull
        bs, m8 = B["bs"], B["m8"]
        Nn = (n + 1) * 128
        if Nn <= KTOP:
            self.V(lambda e: e.tensor_scalar(out=mask[:, 0:Nn], in0=idx[:, 0:Nn], scalar1=-1e29, scalar2=None, op0=ALU.is_gt), [idx], [mask])
            yield
        else:
            NI = self.NITER
            self.V(lambda e: e.max(out=m8[:, 0:8], in_=idx[:, 0:Nn]), [idx], [m8])
            self.V(lambda e: e.tensor_reduce(out=bs[:, 0:1], in_=idx[:, 0:KTOP], axis=AX.X, op=ALU.min), [idx], [bs])
            self.V(lambda e: e.tensor_scalar(out=bs[:, 0:1], in0=bs[:, 0:1], scalar1=-1e-3, scalar2=None, op0=ALU.add), [bs], [bs])
            self.V(lambda e: e.tensor_scalar(out=m8[:, 7:8], in0=m8[:, 7:8], scalar1=1e-3, scalar2=None, op0=ALU.add), [m8], [m8])
            self.V(lambda e: e.tensor_tensor(out=bs[:, 1:2], in0=m8[:, 7:8], in1=bs[:, 0:1], op=ALU.subtract), [m8, bs], [bs])
            self.V(lambda e: e.tensor_scalar(out=bs[:, 8:8 + NI + 2], in0=self.p2n[:, 0:NI + 2], scalar1=bs[:, 1:2], scalar2=None, op0=ALU.mult), [bs, self.p2n], [bs])
            self.V(lambda e: e.scalar_tensor_tensor(out=bs[:, 2:3], in0=bs[:, 0:1], scalar=-1.0, in1=bs[:, 9:10], op0=ALU.mult, op1=ALU.add), [bs], [bs])
            yield
            for it in range(1, NI + 1):
                self.A(lambda e: e.activation(out=mask[:, 0:Nn], in_=idx[:, 0:Nn], func=AF.Sign, bias=bs[:, 2:3], scale=1.0, accum_out=bs[:, 3:4]), [idx, bs], [mask, bs])
                self.A(lambda e: e.activation(out=bs[:, 5:6], in_=bs[:, 2:3], func=AF.Copy), [bs], [bs])
                self.A(lambda e: e.activation(out=bs[:, 4:5], in_=bs[:, 3:4], func=AF.Sign, bias=self.cb[:, n:n + 1], scale=1.0), [bs, self.cb], [bs])
                self.A(lambda e, it=it: e.activation(out=bs[:, 2:3], in_=bs[:, 4:5], func=AF.Identity, scale=bs[:, 8 + it + 1:8 + it + 2], bias=bs[:, 2:3]), [bs], [bs])
                yield
            self.V(lambda e: e.scalar_tensor_tensor(out=bs[:, 7:8], in0=bs[:, 2:3], scalar=-1.0, in1=bs[:, 8 + NI + 1:8 + NI + 2], op0=ALU.mult, op1=ALU.add), [bs], [bs])
            self.V(lambda e: e.tensor_scalar(out=mask[:, 0:Nn], in0=idx[:, 0:Nn], scalar1=bs[:, 7:8], scalar2=None, op0=ALU.is_gt), [idx, bs], [mask])
            yield

    def gen_Y(self, l, n, B):
        ps = self.ps
        par = n % 2
        w_o, kbT, vbx = B["w_o"], B["kbT"], B["vbx"]
        mask, qbTz, oT2, ostg = B["mask"][par], B["qbTz"][n % 3], B["oT2"][n % 3], B["ostg_b"]
        psf = self.psfull
        rec_hi, rec_lo, xres, yt = B["rec_hi"], B["rec_lo"], B["xres"], B["yt"]
        self.DMA("sp", lambda e: e.dma_start(out=xres.ap, in_=self.xs_d.ap[n * 128:(n + 1) * 128, :]), [self.xs_d], [xres])
        yield
        cnt = 0
        for j0 in range(0, n + 1, 8):
            js = list(range(j0, min(n + 1, j0 + 8)))
            mT = B["mT"][(j0 // 8) % 2]
            bv = self.transposes(ps[3], [(mask, mask[:, j * 128:(j + 1) * 128], 128) for j in js])
            nj = len(js)
            self.A(lambda e, bv=bv, mT=mT, nj=nj: e.activation(out=mT[:].rearrange("p a b -> p (a b)")[:, 0:nj * 128], in_=bv[:, 0:nj * 128], func=AF.Identity, scale=30000.0, bias=self.negbig[:, 0:1]), [ps[3], self.negbig], [mT])
            yield
            for j in js:
                jb_ = slice(j * 128, (j + 1) * 128)
                E = B["E_b"][cnt % 2]
                cnt += 1
                for hf in range(2):
                    self.P(lambda e, hf=hf, jb_=jb_: e.matmul(ps[4 + hf][:, 0:512], lhsT=kbT[:, jb_], rhs=qbTz[hf][:].rearrange("p a b -> p (a b)"), start=True, stop=False), [B["kbT_b"][j], qbTz[hf]], [ps[4 + hf]])
                    self.P(lambda e, hf=hf, mT=mT, j=j, j0=j0: e.matmul(ps[4 + hf][:, 0:512], lhsT=self.ident[:], rhs=bcast(mT[:, j - j0, :], 4), start=False, stop=True), [self.ident, mT], [ps[4 + hf]])
                self.A(lambda e, E=E: e.activation(out=E[:, 0:1024], in_=psf[2][:, 0:1024], func=AF.Exp, scale=0.125), [ps[4], ps[5]], [E])
                for hf in range(2):
                    self.P(lambda e, hf=hf, j=j, E=E: e.matmul(ps[6 + hf][:, 0:512], lhsT=vbx[:, j, :], rhs=E[:, hf * 512:(hf + 1) * 512], start=(j == 0), stop=(j == n)), [B["vbx_b"][j], E], [ps[6 + hf]])
                yield
        ev = lambda ap: ap.rearrange("p (a two b) -> p a two b", two=2, b=128)
        for hf in range(2):
            hs = slice(hf * 512, (hf + 1) * 512)
            self.A(lambda e, hf=hf, hs=hs: e.activation(out=rec_hi[64:128, hs], in_=ps[6 + hf][64:128, 0:512], func=AF.Ln), [ps[6 + hf]], [rec_hi])
            self.A(lambda e, hs=hs: e.activation(out=rec_hi[64:128, hs], in_=rec_hi[64:128, hs], func=AF.Exp, scale=-1.0), [rec_hi], [rec_hi])
            self.DMA("sp", lambda e, hs=hs: e.dma_start(out=rec_lo[0:64, hs], in_=rec_hi[64:128, hs]), [rec_hi], [rec_lo])
        yield
        for hf in range(2):
            hs = slice(hf * 512, (hf + 1) * 512)
            self.V(lambda e, hf=hf, hs=hs: e.tensor_tensor(out=oT2[0:64, 4 + 2 * hf:6 + 2 * hf, :], in0=ev(ps[6 + hf][0:64, 0:512])[:, :, 0, :], in1=ev(rec_lo[0:64, hs])[:, :, 0, :], op=ALU.mult), [ps[6 + hf], rec_lo], [oT2])
            self.V(lambda e, hf=hf, hs=hs: e.tensor_tensor(out=ostg[0:64, :, :], in0=ev(ps[6 + hf][0:64, 0:512])[:, :, 1, :], in1=ev(rec_lo[0:64, hs])[:, :, 1, :], op=ALU.mult), [ps[6 + hf], rec_lo], [ostg])
            self.DMA("sp", lambda e, hf=hf: e.dma_start(out=oT2[64:128, 4 + 2 * hf:6 + 2 * hf, :], in_=ostg[0:64, :, :]), [ostg], [oT2])
        yield
        for hf in range(2):
            hs = slice(hf * 512, (hf + 1) * 512)
            for k in range(8):
                self.P(lambda e, hf=hf, k=k, hs=hs: e.matmul(ps[4 + hf][:, 0:512], lhsT=oT2[:, k, :], rhs=w_o[:, k, hs], start=(k == 0), stop=(k == 7)), [oT2, w_o], [ps[4 + hf]])
            self.V(lambda e, hf=hf, hs=hs: e.scalar_tensor_tensor(out=yt[:, hs], in0=xres[:, hs], scalar=ALPHA, in1=ps[4 + hf][:, 0:512], op0=ALU.mult, op1=ALU.add), [xres, ps[4 + hf]], [yt])
            yield
        self.layer_norm(yt, yt, self.lnb, D, self.lnb[:, 0, :], self.lnb[:, 1, :], self.scrB)
        self.DMA("sp", lambda e: e.dma_start(out=self.x1_d.ap[n * 128:(n + 1) * 128, :], in_=yt.ap), [yt], [self.x1_d])
        yield

    def phase_ffn_a(self, l):
        NT = self.NT
        ps = self.ps
        self.S.barrier()
        self.ar.reset()
        aa = self.ar.alloc
        TG_T = min(4, NT)
        TG = TG_T * 128
        NG = NT // TG_T
        w_gu = aa("w_gu", [8, 2 * DFF], BF16)
        xbg = aa("xbg", [TG_T, D], BF16)
        xTg = [aa(f"xTg{i}", [8, TG], BF16) for i in range(2)]
        sgt = [aa(f"sgt{i}", [TG], F32) for i in range(2)]
        hidg = aa("hidg", [22, TG], BF16)
        hbuf = [Buf("hid_lo"), Buf("hid_hi")]
        rng = [(0, 6), (6, 12), (12, 18), (18, 22)]
        wsrc = self.wgu_d[l].rearrange("(k p) c -> p k c", p=128)
        wb = {}
        for (c0, c1) in rng:
            for part, base in (("g", 0), ("u", DFF)):
                b_ = Buf(f"wgu_{part}{c0}")
                for c in range(c0, c1):
                    wb[(part, c)] = b_
                self.DMA("pool", lambda e, c0=c0, c1=c1, base=base: e.dma_start(out=w_gu.ap[:, :, base + c0 * 128:base + c1 * 128], in_=wsrc[:, :, base + c0 * 128:base + c1 * 128]), [], [b_])

        def prologue(G):
            xT = xTg[G % 2]
            self.DMA("pool", lambda e: e.dma_start(out=xbg.ap, in_=self.x1_d.ap[G * TG:(G + 1) * TG, :].rearrange("(t p) c -> p t c", p=128)), [self.x1_d], [xbg])
            for ti in range(TG_T):
                bv = self.transposes(ps[4 + (ti % 2)], [(xbg, xbg[:, ti, k * 128:(k + 1) * 128], 128) for k in range(8)])
                self.A(lambda e, bv=bv, ti=ti: e.copy(out=xT[:, :, ti * 128:(ti + 1) * 128], in_=bv[:, 0:1024].rearrange("p (a b) -> p a b", b=128)), [ps[4 + (ti % 2)]], [xT])

        prologue(0)
        for G in range(NG):
            xT = xTg[G % 2]
            for c in range(22):
                if c == 4 and G + 1 < NG:
                    prologue(G + 1)
                bg, bu, sg = ps[c % 2], ps[2 + (c % 2)], sgt[c % 2]
                hb = hbuf[0 if c < 11 else 1]
                for k in range(8):
                    self.P(lambda e, bg=bg, k=k, c=c, xT=xT: e.matmul(bg[:, 0:TG], lhsT=w_gu[:, k, c * 128:(c + 1) * 128], rhs=xT[:, k, :], start=(k == 0), stop=(k == 7)), [wb[("g", c)], xT], [bg])
                for k in range(8):
                    self.P(lambda e, bu=bu, k=k, c=c, xT=xT: e.matmul(bu[:, 0:TG], lhsT=w_gu[:, k, DFF + c * 128:DFF + (c + 1) * 128], rhs=xT[:, k, :], start=(k == 0), stop=(k == 7)), [wb[("u", c)], xT], [bu])
                self.A(lambda e, bg=bg, sg=sg: e.activation(out=sg[:, 0:TG], in_=bg[:, 0:TG], func=AF.Silu), [bg], [sg])
                self.V(lambda e, bu=bu, c=c, sg=sg: e.tensor_tensor(out=hidg[:, c, :], in0=sg[:, 0:TG], in1=bu[:, 0:TG], op=ALU.mult), [sg, bu], [hb])
                if c == 10 or c == 21:
                    c0 = 0 if c == 10 else 11
                    self.DMA("sp", lambda e, G=G, c0=c0, c=c: e.dma_start(out=self.hid_d.ap[c0:c + 1, :, G * TG:(G + 1) * TG].rearrange("c p t -> p c t"), in_=hidg.ap[:, c0:c + 1, :]), [hb], [self.hid_d])

    def phase_ffn_b(self, l):
        NT = self.NT
        ps = self.ps
        self.S.barrier()
        self.ar.reset()
        aa = self.ar.alloc
        w_dn = aa("w_dn", [22, D], BF16)
        w_pg = aa("w_pg", [8, D], BF16)
        w_pp = aa("w_pp", [2, D], BF16)
        xres = [aa(f"xres{i}", [D], F32) for i in range(2)]
        xb = [aa(f"xb{i}", [D], BF16) for i in range(2)]
        xT = [aa(f"xT{i}", [8, 128], BF16) for i in range(2)]
        hidT = [aa(f"hidT{i}", [22, 128], BF16) for i in range(2)]
        pb = [aa(f"pb{i}", [256], BF16) for i in range(2)]
        pT = [aa(f"pT{i}", [2, 128], BF16) for i in range(2)]
        sgm = [aa(f"sgm{i}", [D], F32) for i in range(2)]
        yt = [aa(f"yt{i}", [D], F32) for i in range(2)]
        if l + 1 < self.DEPTH:
            self.load_mixer_weights(l + 1)
        self.load_w(w_dn, self.wdn_d[l].rearrange("(c p) m -> p c m", p=128))
        self.load_w(w_pg, self.wpg_d[l].rearrange("(k p) m -> p k m", p=128))
        self.load_w(w_pp, self.wpp_d[l].rearrange("(k p) m -> p k m", p=128))
        self.load_ln(2 + 2 * l)
        last = (l == self.DEPTH - 1)

        def prologue(n):
            i = n % 2
            self.DMA("sp", lambda e: e.dma_start(out=xres[i].ap, in_=self.x1_d.ap[n * 128:(n + 1) * 128, :]), [self.x1_d], [xres[i]])
            self.DMA("sp", lambda e: e.dma_start(out=hidT[i].ap, in_=self.hid_d.ap[:, :, n * 128:(n + 1) * 128].rearrange("c p t -> p c t")), [self.hid_d], [hidT[i]])
            self.DMA("pool", lambda e: e.dma_start(out=xb[i].ap, in_=self.x1_d.ap[n * 128:(n + 1) * 128, :]), [self.x1_d], [xb[i]])
            self.DMA("pool", lambda e: e.dma_start(out=pb[i].ap, in_=self.p_d[l, n * 128:(n + 1) * 128, :]), [], [pb[i]])
            bv = self.transposes(ps[6], [(xb[i], xb[i][:, k * 128:(k + 1) * 128], 128) for k in range(8)])
            self.A(lambda e, bv=bv: e.copy(out=xT[i][:].rearrange("p a b -> p (a b)"), in_=bv[:, 0:1024]), [ps[6]], [xT[i]])
            bv = self.transposes(ps[7], [(pb[i], pb[i][:, k * 128:(k + 1) * 128], 128) for k in range(2)])
            self.A(lambda e, bv=bv: e.copy(out=pT[i][:].rearrange("p a b -> p (a b)"), in_=bv[:, 0:256]), [ps[7]], [pT[i]])

        prologue(0)
        for n in range(NT):
            i = n % 2
            if n + 1 < NT:
                prologue(n + 1)
            for hf in range(2):
                hs = slice(hf * 512, (hf + 1) * 512)
                for c in range(22):
                    self.P(lambda e, hf=hf, c=c, hs=hs, i=i: e.matmul(ps[0 + hf][:, 0:512], lhsT=hidT[i][:, c, :], rhs=w_dn[:, c, hs], start=(c == 0), stop=(c == 21)), [hidT[i], w_dn], [ps[0 + hf]])
                for k in range(8):
                    self.P(lambda e, hf=hf, k=k, hs=hs, i=i: e.matmul(ps[2 + hf][:, 0:512], lhsT=xT[i][:, k, :], rhs=w_pg[:, k, hs], start=(k == 0), stop=(k == 7)), [xT[i], w_pg], [ps[2 + hf]])
                for k in range(2):
                    self.P(lambda e, hf=hf, k=k, hs=hs, i=i: e.matmul(ps[4 + hf][:, 0:512], lhsT=pT[i][:, k, :], rhs=w_pp[:, k, hs], start=(k == 0), stop=(k == 1)), [pT[i], w_pp], [ps[4 + hf]])
                self.A(lambda e, hf=hf, hs=hs, i=i: e.activation(out=sgm[i][:, hs], in_=ps[2 + hf][:, 0:512], func=AF.Sigmoid), [ps[2 + hf]], [sgm[i]])
                self.V(lambda e, hf=hf, hs=hs, i=i: e.tensor_tensor(out=sgm[i][:, hs], in0=sgm[i][:, hs], in1=ps[4 + hf][:, 0:512], op=ALU.mult), [sgm[i], ps[4 + hf]], [sgm[i]])
                self.V(lambda e, hf=hf, hs=hs, i=i: e.scalar_tensor_tensor(out=yt[i][:, hs], in0=xres[i][:, hs], scalar=ALPHA, in1=ps[0 + hf][:, 0:512], op0=ALU.mult, op1=ALU.add), [xres[i], ps[0 + hf]], [yt[i]])
                self.V(lambda e, hs=hs, i=i: e.tensor_tensor(out=yt[i][:, hs], in0=yt[i][:, hs], in1=sgm[i][:, hs], op=ALU.add), [yt[i], sgm[i]], [yt[i]])
            self.layer_norm(yt[i], yt[i], self.lnb, D, self.lnb[:, 0, :], self.lnb[:, 1, :], self.scrA if i == 0 else self.scrB, gb_eng="dve")
            if last:
                self.DMA("sp", lambda e, n=n, i=i: e.dma_start(out=self.y_d[n * 128:(n + 1) * 128, :], in_=yt[i].ap), [yt[i]], [self.yb])
            else:
                self.DMA("sp", lambda e, n=n, i=i: e.dma_start(out=self.xs_d.ap[n * 128:(n + 1) * 128, :], in_=yt[i].ap), [yt[i]], [self.xs_d])


def make_in_maps(x, p, positions, ln_in_g, ln_in_b, w_in, attn_sinks, idx_k_g, idx_k_b, w_o,
                 ln1_g, ln1_b, w_gu, w_down, w_pg, w_pp, ln2_g, ln2_b, NT, DEPTH):
    f = lambda a: np.ascontiguousarray(np.asarray(a), dtype=np.float32)
    B = x.shape[0]
    rows = [np.concatenate([f(ln_in_g), f(ln_in_b)])]
    for l in range(DEPTH):
        rows.append(np.concatenate([f(ln1_g)[l], f(ln1_b)[l]]))
        rows.append(np.concatenate([f(ln2_g)[l], f(ln2_b)[l]]))
    lnp = np.stack(rows).astype(np.float32)
    inv = np.concatenate([
        (10000.0 ** (-np.arange(32, dtype=np.float32) / np.float32(32))).astype(np.float32),
        (10000.0 ** (-np.arange(16, dtype=np.float32) / np.float32(16))).astype(np.float32)]).astype(np.float32)
    ikgb = np.concatenate([f(idx_k_g), f(idx_k_b)], axis=1).astype(np.float32)
    shared = {"inv": inv, "lnp": lnp, "w_in": f(w_in), "w_o": f(w_o), "w_gu": f(w_gu), "w_down": f(w_down),
              "w_pg": f(w_pg), "w_pp": f(w_pp), "sinks": f(attn_sinks), "ikgb": ikgb}
    maps = []
    pos = np.asarray(positions).astype(np.int32)
    for b in range(B):
        m = dict(shared)
        m["x"] = f(x[b])
        m["p"] = f(np.asarray(p)[:, b])
        m["pos"] = np.ascontiguousarray(pos[b].reshape(NT, 128).T)
        maps.append(m)
    return maps


_CACHE = {}


def kernel(**inputs):
    x = np.asarray(inputs["x"])
    B, S_, _ = x.shape
    NT = S_ // 128
    DEPTH = np.asarray(inputs["w_in"]).shape[0]
    KTOP = min(256, S_ // 4)
    key = (NT, KTOP, DEPTH)
    if key not in _CACHE:
        _CACHE[key] = Builder(NT=NT, KTOP=KTOP, DEPTH=DEPTH)
    bld = _CACHE[key]
    maps = make_in_maps(NT=NT, DEPTH=DEPTH, **inputs)
    res = run_bass_kernel_spmd(bld.nc, maps, core_ids=list(range(B)))
    out = np.stack([np.asarray(r["y"]) for r in res.results], axis=0).astype(np.float32)
    return out
```

```python
import numpy as np
import concourse.bass as bass
import concourse.mybir as mybir
from concourse.bass_utils import run_bass_kernel_spmd

F32 = mybir.dt.float32
BF16 = mybir.dt.bfloat16
I32 = mybir.dt.int32
AF = mybir.ActivationFunctionType
ALU = mybir.AluOpType
AX = mybir.AxisListType

ENGS = ["pe", "act", "dve", "pool", "sp"]
NDMASEM = 14
EPOCH = 4000
ILV_BIAS = 1.4
STRICT = True

D = 1024
DFF = 2816
INC = 1704
ALPHA = float(4 ** 0.25)
EPS = 1e-5
TWO_PI = 6.283185307179586
C1 = 6.28125
C2 = TWO_PI - C1


class Buf:
    __slots__ = ("name", "last_w", "readers")

    def __init__(self, name):
        self.name = name
        self.last_w = None
        self.readers = []


class Sched:
    def __init__(self, nc):
        self.nc = nc
        self.ops = {e: [] for e in ENGS}
        self.count = {e: 0 for e in ENGS}
        self.epoch = {e: 0 for e in ENGS}
        self.esems = {e: [nc.alloc_semaphore(f"s_{e}_0")] for e in ENGS}
        self.dq = {"sp": 0, "pool": 1, "act": 2}
        self.dsems = [nc.alloc_semaphore(f"s_dma_{i}") for i in range(3 * NDMASEM)]
        self.dma_n = [0, 0, 0]
        self.dma_last = [0] * (3 * NDMASEM)
        self.waited = {e: {} for e in ENGS}

    def _tok(self, eng):
        if self.count[eng] >= EPOCH:
            self.epoch[eng] += 1
            self.count[eng] = 0
            self.esems[eng].append(self.nc.alloc_semaphore(f"s_{eng}_{self.epoch[eng]}"))
        self.count[eng] += 1
        return ("e", eng, self.epoch[eng], self.count[eng])

    def op(self, eng, fn, reads=(), writes=(), dma=False, extra=()):
        if getattr(self, "dry", False):
            return None
        deps = set(extra)
        for b in reads:
            if b.last_w is not None:
                deps.add(b.last_w)
        for b in writes:
            if b.last_w is not None and (STRICT or dma or b.last_w[0] != "e" or b.last_w[1] != eng):
                deps.add(b.last_w)
            for r in b.readers:
                if STRICT or dma or r[0] != "e" or r[1] != eng:
                    deps.add(r)
        if eng == "pe" and not dma:
            deps = {d for d in deps if not (d[0] == "e" and d[1] == "pe")}
        if dma:
            q = self.dq[eng]
            i = self.dma_n[q]
            self.dma_n[q] += 1
            si = q * NDMASEM + (i % NDMASEM)
            val = self.dma_last[si] + 16
            if self.dma_last[si] > 0:
                deps.add(("d", si, 0, self.dma_last[si]))
            self.dma_last[si] = val
            tok = ("d", si, 0, val)
        else:
            tok = self._tok(eng)
        need = {}
        for d in deps:
            key = d[:3]
            need[key] = max(need.get(key, 0), d[3])
        waits = []
        w = self.waited[eng]
        for key, v in need.items():
            if key[0] == "e":
                done = False
                for k2, v2 in w.items():
                    if k2[0] == "e" and k2[1] == key[1] and (k2[2] > key[2] or (k2[2] == key[2] and v2 >= v)):
                        done = True
                        break
                if done:
                    continue
            else:
                if w.get(key, 0) >= v:
                    continue
            w[key] = max(w.get(key, 0), v)
            waits.append((key, v))
        self.ops[eng].append((waits, fn, tok))
        for b in reads:
            b.readers.append(tok)
        for b in writes:
            b.last_w = tok
            b.readers = []
        return tok

    def all_tokens(self):
        toks = []
        for e in ENGS:
            if self.count[e] > 0 or self.epoch[e] > 0:
                toks.append(("e", e, self.epoch[e], self.count[e]))
        for si in range(3 * NDMASEM):
            if self.dma_last[si] > 0:
                toks.append(("d", si, 0, self.dma_last[si]))
        return toks

    def barrier(self):
        toks = self.all_tokens()
        for e in ENGS:
            waits = []
            for t in toks:
                if t[0] == "e" and t[1] == e:
                    continue
                waits.append((t[:3], t[3]))
                self.waited[e][t[:3]] = max(self.waited[e].get(t[:3], 0), t[3])
            self.ops[e].append((waits, None, None))

    def _sem(self, key):
        if key[0] == "e":
            return self.esems[key[1]][key[2]]
        return self.dsems[key[1]]

    def emit(self):
        nc = self.nc
        handles = {"pe": "tensor", "act": "scalar", "dve": "vector", "pool": "gpsimd", "sp": "sync"}
        with nc.Block() as block:
            for e in ENGS:
                ops = self.ops[e]

                def body(engh, ops=ops):
                    for waits, fn, tok in ops:
                        for key, v in waits:
                            engh.wait_ge(self._sem(key), v)
                        if fn is None:
                            continue
                        ins = fn(engh)
                        ins.then_inc(self._sem(tok[:3]), 1 if tok[0] == "e" else 16)

                getattr(block, handles[e])(body)


class T:
    def __init__(self, ap, name):
        self.ap = ap
        self.b = Buf(name)

    def __getitem__(self, k):
        return self.ap[k]


class Arena:
    def __init__(self, nc, nbytes, name):
        self.t = nc.alloc_sbuf_tensor(name, [128, nbytes // 2], BF16)
        self.n = nbytes
        self.off = 0

    def reset(self):
        self.off = 0

    def alloc_at(self, name, shape, dtype, off):
        save, lim = self.off, getattr(self, "limit", self.n)
        self.off, self.limit = off, self.n
        t = self.alloc(name, shape, dtype)
        self.off, self.limit = save, lim
        return t

    def alloc(self, name, shape, dtype):
        size = 2 if dtype == BF16 else 4
        nel = int(np.prod(shape))
        nb = nel * size
        off = (self.off + 31) // 32 * 32
        self.off = off + nb
        assert self.off <= getattr(self, "limit", self.n), (name, self.off, getattr(self, "limit", self.n))
        ap = self.t[:, off // 2: off // 2 + nb // 2]
        if dtype != BF16:
            ap = ap.bitcast(dtype)
        if len(shape) == 2:
            ap = ap.rearrange("p (a b) -> p a b", b=shape[1])
        elif len(shape) == 3:
            ap = ap.rearrange("p (a b c) -> p a b c", b=shape[1], c=shape[2])
        return T(ap, name)


def bcast(ap, n):
    s = list(ap.shape)
    return ap.unsqueeze(1).to_broadcast([s[0], n] + s[1:])


class Builder:
    def __init__(self, NT=32, KTOP=256, DEPTH=2, NITER=16):
        self.NT, self.KTOP, self.DEPTH, self.NITER = NT, KTOP, DEPTH, NITER
        self.S_ = NT * 128
        nc = bass.Bass("TRN2", target_bir_lowering=False)
        self.nc = nc
        self.S = Sched(nc)
        S_ = self.S_
        dt = nc.dram_tensor
        self.x_d = dt("x", [S_, D], F32, kind="ExternalInput").ap()
        self.p_d = dt("p", [DEPTH, S_, 256], F32, kind="ExternalInput").ap()
        self.pos_d = dt("pos", [128, NT], I32, kind="ExternalInput").ap()
        self.inv_d = dt("inv", [48], F32, kind="ExternalInput").ap()
        self.lnp_d = dt("lnp", [1 + 2 * DEPTH, 2 * D], F32, kind="ExternalInput").ap()
        self.win_d = dt("w_in", [DEPTH, D, INC], F32, kind="ExternalInput").ap()
        self.wo_d = dt("w_o", [DEPTH, D, D], F32, kind="ExternalInput").ap()
        self.wgu_d = dt("w_gu", [DEPTH, D, 2 * DFF], F32, kind="ExternalInput").ap()
        self.wdn_d = dt("w_down", [DEPTH, DFF, D], F32, kind="ExternalInput").ap()
        self.wpg_d = dt("w_pg", [DEPTH, D, D], F32, kind="ExternalInput").ap()
        self.wpp_d = dt("w_pp", [DEPTH, 256, D], F32, kind="ExternalInput").ap()
        self.sk_d = dt("sinks", [DEPTH, 8], F32, kind="ExternalInput").ap()
        self.ik_d = dt("ikgb", [DEPTH, 64], F32, kind="ExternalInput").ap()
        self.y_d = dt("y", [S_, D], F32, kind="ExternalOutput").ap()
        self.xs_d = T(dt("xs", [S_, D], F32).ap(), "xs_d")
        self.x1_d = T(dt("x1s", [S_, D], F32).ap(), "x1_d")
        self.hid_d = T(dt("hids", [22, 128, S_], BF16).ap(), "hid_d")
        self.yb = Buf("y_d")
        self.ps = []
        self.psfull = []
        for i in range(4):
            pt = nc.alloc_psum_tensor(f"pp{i}", [128, 1024], F32)
            self.psfull.append(pt[:])
            self.ps.append(T(pt[:, 0:512], f"ps{2 * i}"))
            self.ps.append(T(pt[:, 512:1024], f"ps{2 * i + 1}"))
        self.pers = Arena(nc, 22720, "pers")
        self.ar = Arena(nc, 189760, "arena")
        self.build()

    def V(self, fn, r, w):
        return self.S.op("dve", fn, [t.b if isinstance(t, T) else t for t in r], [t.b if isinstance(t, T) else t for t in w])

    def A(self, fn, r, w):
        return self.S.op("act", fn, [t.b if isinstance(t, T) else t for t in r], [t.b if isinstance(t, T) else t for t in w])

    def G(self, fn, r, w):
        return self.S.op("pool", fn, [t.b if isinstance(t, T) else t for t in r], [t.b if isinstance(t, T) else t for t in w])

    def P(self, fn, r, w):
        return self.S.op("pe", fn, [t.b if isinstance(t, T) else t for t in r], [t.b if isinstance(t, T) else t for t in w])

    def DMA(self, eng, fn, r, w):
        return self.S.op(eng, fn, [t.b if isinstance(t, T) else t for t in r], [t.b if isinstance(t, T) else t for t in w], dma=True)

    def layer_norm(self, xt, yt, gb, width, g_ap, b_ap, scr, gb_eng="pool"):
        st, mv, sc = scr
        nchunk = (width + 511) // 512
        for c in range(nchunk):
            lo, hi = c * 512, min(width, (c + 1) * 512)
            self.V(lambda e, c=c, lo=lo, hi=hi: e.bn_stats(out=st[:, c, :], in_=xt[:, lo:hi]), [xt], [st])
        self.V(lambda e: e.bn_aggr(out=mv[:, 0:2], in_=st[:, 0:nchunk, :].rearrange("p a b -> p (a b)")), [st], [mv])
        self.V(lambda e: e.tensor_scalar(out=sc[:, 0:1], in0=mv[:, 1:2], scalar1=EPS, scalar2=None, op0=ALU.add), [mv], [sc])
        self.A(lambda e: e.activation(out=sc[:, 1:2], in_=sc[:, 0:1], func=AF.Ln), [sc], [sc])
        self.A(lambda e: e.activation(out=sc[:, 2:3], in_=sc[:, 1:2], func=AF.Exp, scale=-0.5), [sc], [sc])
        self.V(lambda e: e.tensor_scalar(out=sc[:, 3:4], in0=mv[:, 0:1], scalar1=sc[:, 2:3], scalar2=-1.0, op0=ALU.mult, op1=ALU.mult), [mv, sc], [sc])
        self.V(lambda e: e.tensor_scalar(out=yt[:, 0:width], in0=xt[:, 0:width], scalar1=sc[:, 2:3], scalar2=sc[:, 3:4], op0=ALU.mult, op1=ALU.add), [xt, sc], [yt])
        E_ = self.G if gb_eng == "pool" else self.V
        E_(lambda e: e.tensor_tensor(out=yt[:, 0:width], in0=yt[:, 0:width], in1=g_ap, op=ALU.mult), [yt, gb], [yt])
        E_(lambda e: e.tensor_tensor(out=yt[:, 0:width], in0=yt[:, 0:width], in1=b_ap, op=ALU.add), [yt, gb], [yt])

    def ln_scratch(self, alloc, tag):
        return (alloc("ln_st" + tag, [2, 6], F32), alloc("ln_mv" + tag, [4], F32), alloc("ln_sc" + tag, [8], F32))

    def freg(self, e, val):
        if not hasattr(self, "_fregs"):
            self._fregs = {}
        if val not in self._fregs:
            self._fregs[val] = e.to_reg(val)
        return self._fregs[val]

    def load_w(self, dst, src_ap):
        self.DMA("pool", lambda e: e.dma_start(out=dst.ap, in_=src_ap), [], [dst])

    def transposes(self, bank, specs, col0=0):
        bv = bank.ap.bitcast(BF16)
        for i, (srcT, sap, F) in enumerate(specs):
            c = col0 + i * 128
            self.P(lambda e, c=c, sap=sap, F=F: e.transpose(out=bv[0:F, c:c + 128], in_=sap, identity=self.ident[:]),
                   [srcT, self.ident], [bank])
        return bv

    @staticmethod
    def interleave(gx, nx, gy, ny):
        ix = iy = 0
        ax, ay = gx is not None, gy is not None
        while ax or ay:
            stepx = ax and (not ay or ix * max(ny, 1) * ILV_BIAS <= iy * max(nx, 1))
            if stepx:
                try:
                    next(gx)
                    ix += 1
                except StopIteration:
                    ax = False
            else:
                try:
                    next(gy)
                    iy += 1
                except StopIteration:
                    ay = False

    @staticmethod
    def merge(g1, n1, g2, n2):
        i1 = i2 = 0
        a1, a2 = g1 is not None, g2 is not None
        while a1 or a2:
            step1 = a1 and (not a2 or i1 * max(n2, 1) <= i2 * max(n1, 1))
            if step1:
                try:
                    next(g1)
                    i1 += 1
                    yield
                except StopIteration:
                    a1 = False
            else:
                try:
                    next(g2)
                    i2 += 1
                    yield
                except StopIteration:
                    a2 = False

    def count(self, gen):
        self.S.dry = True
        n = sum(1 for _ in gen)
        self.S.dry = False
        return n

    def build(self):
        NT, S_, DEPTH = self.NT, self.S_, self.DEPTH
        pa = self.pers.alloc
        self.ident = pa("ident", [128], BF16)
        self.m12 = pa("m12", [2, 128], BF16)
        self.ones64 = pa("ones64", [64], BF16)
        self.cs64 = pa("cs64", [NT, 2, 32], F32)
        self.cs32 = pa("cs32", [NT, 2, 16], F32)
        self.lnb = pa("lnb", [2, D], F32)
        self.esink = pa("esink", [8], F32)
        self.ikgb = pa("ikgb", [2, 32], F32)
        self.mb12 = pa("mb12", [2, 128], BF16)
        self.cb = pa("cb", [NT], F32)
        self.p2n = pa("p2n", [24], F32)
        self.negbig = pa("negbig", [1], F32)
        self.scrA = self.ln_scratch(pa, "A")
        self.scrB = self.ln_scratch(pa, "B")
        top = (self.ar.n - (8 * INC * 2 + 8 * D * 2)) // 32 * 32
        self.ar.limit = top
        self.w_in_T = self.ar.alloc_at("w_in", [8, INC], BF16, top)
        self.w_o_T = self.ar.alloc_at("w_o", [8, D], BF16, top + 8 * INC * 2)
        self.setup_consts()
        self.load_mixer_weights(0)
        self.phase_ln_in()
        for l in range(DEPTH):
            self.phase_mixer(l)
            self.phase_ffn_a(l)
            self.phase_ffn_b(l)
        self.S.barrier()
        self.S.emit()

    def setup_consts(self):
        NT = self.NT
        ident, m12, ones64 = self.ident, self.m12, self.ones64
        self.G(lambda e: e.memset(ident[:], 1.0), [], [ident])
        self.G(lambda e: e.affine_select(out=ident[:], in_=ident[:], pattern=[[1, 128]], compare_op=ALU.is_equal, fill=self.freg(e, 0.0), base=0, channel_multiplier=-1), [ident], [ident])
        self.G(lambda e: e.memset(m12[:], 1.0), [], [m12])
        self.G(lambda e: e.affine_select(out=m12[:, 0, :], in_=m12[:, 0, :], pattern=[[1, 128]], compare_op=ALU.is_ge, fill=self.freg(e, 0.0), base=0, channel_multiplier=-1), [m12], [m12])
        self.G(lambda e: e.affine_select(out=m12[:, 1, :], in_=m12[:, 1, :], pattern=[[-1, 128]], compare_op=ALU.is_ge, fill=self.freg(e, 0.0), base=-1, channel_multiplier=1), [m12], [m12])
        self.G(lambda e: e.memset(ones64[:], 1.0), [], [ones64])
        self.V(lambda e: e.tensor_scalar(out=self.mb12[:], in0=m12[:], scalar1=30000.0, scalar2=-30000.0, op0=ALU.mult, op1=ALU.add), [m12], [self.mb12])
        self.G(lambda e: e.memset(self.negbig[:], -30000.0), [], [self.negbig])
        for n_ in range(NT):
            self.G(lambda e, n_=n_: e.memset(self.cb[:, n_:n_ + 1], float(-(2 * self.KTOP - (n_ + 1) * 128) + 0.5)), [], [self.cb])
        for k_ in range(24):
            self.G(lambda e, k_=k_: e.memset(self.p2n[:, k_:k_ + 1], float(-(2.0 ** (-k_)))), [], [self.p2n])
        self.ar.reset()
        aa = self.ar.alloc
        posi = aa("posi", [NT], I32)
        posf = aa("posf", [NT], F32)
        invb = aa("invb", [48], F32)
        ang = aa("ang", [NT, 48], F32)
        a2 = aa("a2", [NT, 48], F32)
        ki = aa("ki", [NT, 48], I32)
        kf = aa("kf", [NT, 48], F32)
        self.DMA("sp", lambda e: e.dma_start(out=posi.ap, in_=self.pos_d), [], [posi])
        self.DMA("sp", lambda e: e.dma_start(out=invb.ap, in_=self.inv_d.partition_broadcast(128)), [], [invb])
        self.V(lambda e: e.tensor_copy(out=posf[:], in_=posi[:]), [posi], [posf])
        self.V(lambda e: e.tensor_tensor(out=ang[:], in0=posf[:].unsqueeze(2).to_broadcast([128, NT, 48]), in1=bcast(invb[:], NT), op=ALU.mult), [posf, invb], [ang])
        for which in range(2):
            if which == 0:
                self.V(lambda e: e.tensor_scalar(out=a2[:], in0=ang[:], scalar1=float(np.pi / 2), scalar2=None, op0=ALU.add), [ang], [a2])
            else:
                self.V(lambda e: e.tensor_copy(out=a2[:], in_=ang[:]), [ang], [a2])
            self.V(lambda e: e.tensor_scalar(out=ki[:], in0=a2[:], scalar1=float(1.0 / TWO_PI), scalar2=None, op0=ALU.mult), [a2], [ki])
            self.V(lambda e: e.tensor_copy(out=kf[:], in_=ki[:]), [ki], [kf])
            self.V(lambda e: e.scalar_tensor_tensor(out=a2[:], in0=kf[:], scalar=-C1, in1=a2[:], op0=ALU.mult, op1=ALU.add), [kf, a2], [a2])
            self.V(lambda e: e.scalar_tensor_tensor(out=a2[:], in0=kf[:], scalar=-C2, in1=a2[:], op0=ALU.mult, op1=ALU.add), [kf, a2], [a2])
            self.V(lambda e: e.tensor_scalar(out=a2[:], in0=a2[:], scalar1=-3.1415925, scalar2=3.1415925, op0=ALU.max, op1=ALU.min), [a2], [a2])
            self.A(lambda e, which=which: e.activation(out=self.cs64[:, :, which, :], in_=a2[:, :, 0:32], func=AF.Sin), [a2], [self.cs64])
            self.A(lambda e, which=which: e.activation(out=self.cs32[:, :, which, :], in_=a2[:, :, 32:48], func=AF.Sin), [a2], [self.cs32])

    def load_mixer_weights(self, l):
        self.load_w(self.w_in_T, self.win_d[l].rearrange("(k p) c -> p k c", p=128))
        self.load_w(self.w_o_T, self.wo_d[l].rearrange("(k p) c -> p k c", p=128))

    def load_ln(self, idx):
        self.DMA("sp", lambda e: e.dma_start(out=self.lnb.ap.rearrange("p a b -> p (a b)"), in_=self.lnp_d[idx:idx + 1, :].to_broadcast([128, 2 * D])), [], [self.lnb])

    def phase_ln_in(self):
        self.S.barrier()
        self.ar.reset()
        self.load_ln(0)
        xt = [self.ar.alloc(f"xt{i}", [D], F32) for i in range(3)]

        def load(n):
            x = xt[n % 3]
            self.DMA("sp", lambda e: e.dma_start(out=x.ap, in_=self.x_d[n * 128:(n + 1) * 128, :]), [], [x])

        for n in range(min(2, self.NT)):
            load(n)
        for n in range(self.NT):
            x = xt[n % 3]
            if n + 2 < self.NT:
                load(n + 2)
            self.layer_norm(x, x, self.lnb, D, self.lnb[:, 0, :], self.lnb[:, 1, :], self.scrA if n % 2 == 0 else self.scrB)
            self.DMA("sp", lambda e, n=n, x=x: e.dma_start(out=self.xs_d.ap[n * 128:(n + 1) * 128, :], in_=x.ap), [x], [self.xs_d])

    def rope(self, hsb, c0, H, Dh, cs, n, dstT, dst4, t1, t2, perm=False):
        half = Dh // 2
        if perm:
            src = hsb[:, c0:c0 + H * Dh].rearrange("p (hi lo d) -> p hi lo d", hi=2, d=Dh)
            dst = dst4.rearrange("p (lo hi) d -> p hi lo d", hi=2)
            tv = lambda t: t[:, 0:H * half].rearrange("p (hi lo d) -> p hi lo d", hi=2, d=half)
            bc = lambda ap: ap.unsqueeze(1).unsqueeze(1).to_broadcast([128, 2, H // 2, half])
            sl = lambda ap, a, b: ap[:, :, :, a:b]
        else:
            src = hsb[:, c0:c0 + H * Dh].rearrange("p (h d) -> p h d", d=Dh)
            dst = dst4
            tv = lambda t: t[:, 0:H * half].rearrange("p (h d) -> p h d", d=half)
            bc = lambda ap: ap.unsqueeze(1).to_broadcast([128, H, half])
            sl = lambda ap, a, b: ap[:, :, a:b]
        x1, x2 = sl(src, 0, half), sl(src, half, Dh)
        cosb, sinb = bc(cs[:, n, 0, :]), bc(cs[:, n, 1, :])
        a, b = tv(t1), tv(t2)
        self.V(lambda e: e.tensor_tensor(out=a, in0=x1, in1=cosb, op=ALU.mult), [hsb, cs], [t1])
        self.V(lambda e: e.tensor_tensor(out=b, in0=x2, in1=sinb, op=ALU.mult), [hsb, cs], [t2])
        self.V(lambda e: e.tensor_tensor(out=sl(dst, 0, half), in0=a, in1=b, op=ALU.subtract), [t1, t2], [dstT])
        self.V(lambda e: e.tensor_tensor(out=a, in0=x2, in1=cosb, op=ALU.mult), [hsb, cs], [t1])
        self.V(lambda e: e.tensor_tensor(out=b, in0=x1, in1=sinb, op=ALU.mult), [hsb, cs], [t2])
        self.V(lambda e: e.tensor_tensor(out=sl(dst, half, Dh), in0=a, in1=b, op=ALU.add), [t1, t2], [dstT])

    def phase_mixer(self, l):
        NT, S_ = self.NT, self.S_
        self.S.barrier()
        self.ar.reset()
        aa = self.ar.alloc
        B = {}
        B["w_in"] = self.w_in_T
        B["w_o"] = self.w_o_T
        B["kaT"] = aa("kaT", [S_], BF16)
        B["kbT"] = aa("kbT", [S_], BF16)
        B["kiT"] = aa("kiT", [S_], BF16)
        B["va"] = aa("va", [NT, 128], BF16)
        B["vbx"] = aa("vbx", [NT, 128], BF16)
        B["idx"] = aa("idx", [S_], F32)
        B["mask"] = [aa(f"mask{i}", [S_], BF16) for i in range(2)]
        B["xb"] = [aa(f"xb{i}", [D], BF16) for i in range(2)]
        B["xT"] = aa("xT", [8, 128], BF16)
        B["hsb"] = aa("hsb", [INC], F32)
        B["t1"] = aa("t1", [256], F32)
        B["t2"] = aa("t2", [256], F32)
        B["qa_r"] = aa("qa_r", [8, 64], BF16)
        B["qb_r"] = aa("qb_r", [8, 64], BF16)
        B["ka_r"] = aa("ka_r", [2, 64], BF16)
        B["kb_r"] = aa("kb_r", [2, 64], BF16)
        B["qi_f"] = T(B["hsb"].ap[:, 0:256].rearrange("p (h d) -> p h d", d=32), "qi_f")
        B["qi_f"].b = B["hsb"].b
        B["qi_r"] = aa("qi_r", [8, 32], BF16)
        B["ki_l"] = aa("ki_l", [32], F32)
        B["ki_r"] = aa("ki_r", [1, 32], BF16)
        B["wsm"] = [aa(f"wsm{i}", [3, 8], F32) for i in range(2)]
        B["qaTz"] = [aa(f"qaTz{i}", [4, 128], BF16) for i in range(2)]
        B["qbTz"] = [[aa(f"qbTz{i}{j}", [4, 128], BF16) for j in range(2)] for i in range(3)]
        B["qiT"] = [aa(f"qiT{i}", [8, 128], BF16) for i in range(2)]
        B["E_f"] = aa("E_f", [1024], BF16)
        B["E_b"] = [aa(f"E_b{i}", [1024], BF16) for i in range(2)]
        B["mT"] = [aa(f"mT{i}", [8, 128], BF16) for i in range(2)]
        B["R"] = [aa(f"R{i}", [2, 512], F32) for i in range(2)]
        B["oT2"] = [aa(f"oT2{i}", [8, 128], BF16) for i in range(3)]
        B["ostg_f"] = aa("ostg_f", [2, 128], BF16)
        B["ostg_b"] = aa("ostg_b", [2, 128], BF16)
        B["rec_f"] = aa("rec_f", [512], F32)
        rec2 = aa("rec2", [1024], F32)
        B["rec_hi"] = T(rec2.ap, "rec_hi")
        B["rec_lo"] = T(rec2.ap, "rec_lo")
        B["xres"] = aa("xres", [D], F32)
        B["yt"] = aa("yt", [D], F32)
        B["bs"] = aa("bs", [40], F32)
        B["m8"] = aa("m8", [8], F32)

        for nm in ("kaT", "kbT", "kiT", "va", "vbx"):
            B[nm + "_b"] = [Buf(f"{nm}_{j}") for j in range(NT)]
        w_in, w_o = B["w_in"], B["w_o"]
        self.load_ln(1 + 2 * l)
        self.DMA("sp", lambda e: e.dma_start(out=self.esink.ap, in_=self.sk_d[l:l + 1, :].to_broadcast([128, 8])), [], [self.esink])
        self.A(lambda e: e.activation(out=self.esink[:], in_=self.esink[:], func=AF.Exp), [self.esink], [self.esink])
        self.DMA("sp", lambda e: e.dma_start(out=self.ikgb.ap.rearrange("p a b -> p (a b)"), in_=self.ik_d[l:l + 1, :].to_broadcast([128, 64])), [], [self.ikgb])
        self.G(lambda e: e.memset(B["vbx"][:], 1.0), [], [B["vbx"]])
        self.G(lambda e: e.memset(B["kiT"][:], 0.0), [], [B["kiT"]])
        for t_ in B["qaTz"] + B["qbTz"][0] + B["qbTz"][1] + B["qbTz"][2] + B["qiT"]:
            self.G(lambda e, t_=t_: e.memset(t_[:], 0.0), [], [t_])

        self.S.barrier()
        cFE = [self.count(self.gen_FE(l, n, B)) for n in range(NT)]
        cIX = [self.count(self.gen_IDX(l, n, B)) for n in range(NT)]
        cBS = [self.count(self.gen_BIS(l, n, B)) for n in range(NT)]
        cY = [self.count(self.gen_Y(l, n, B)) for n in range(NT)]

        def chainA(n):
            if n + 1 < NT:
                yield from self.gen_IDX(l, n + 1, B)
                g1, c1 = self.gen_BIS(l, n + 1, B), cBS[n + 1]
            else:
                g1, c1 = None, 0
            if n + 2 < NT:
                g2, c2 = self.gen_FE(l, n + 2, B), cFE[n + 2]
            else:
                g2, c2 = None, 0
            yield from self.merge(g1, c1, g2, c2)

        def lenA(n):
            return (cIX[n + 1] + cBS[n + 1] if n + 1 < NT else 0) + (cFE[n + 2] if n + 2 < NT else 0)

        for g in (self.gen_FE(l, 0, B), self.gen_IDX(l, 0, B), self.gen_BIS(l, 0, B)):
            for _ in g:
                pass
        if NT > 1:
            for _ in self.gen_FE(l, 1, B):
                pass
        for n in range(NT):
            self.interleave(chainA(n), lenA(n), self.gen_Y(l, n, B), cY[n])

    def gen_FE(self, l, n, B):
        NT, S_, KTOP = self.NT, self.S_, self.KTOP
        ps = self.ps
        par = n % 2
        nb = slice(n * 128, (n + 1) * 128)
        w_in, kaT, kbT, kiT, va, vbx, idx = B["w_in"], B["kaT"], B["kbT"], B["kiT"], B["va"], B["vbx"], B["idx"]
        mask, xb, xT, hsb, t1, t2 = B["mask"][par], B["xb"][par], B["xT"], B["hsb"], B["t1"], B["t2"]
        qa_r, qb_r, ka_r, kb_r, qi_f, qi_r, ki_l, ki_r = B["qa_r"], B["qb_r"], B["ka_r"], B["kb_r"], B["qi_f"], B["qi_r"], B["ki_l"], B["ki_r"]
        wsm, qaTz, qbTz, qiT = B["wsm"][par], B["qaTz"], B["qbTz"][n % 3], B["qiT"][par]
        E, R, oT2, ostg, rec = B["E_f"], B["R"], B["oT2"][n % 3], B["ostg_f"], B["rec_f"]
        psf = self.psfull
        bs, m8 = B["bs"], B["m8"]
        self.DMA("pool", lambda e: e.dma_start(out=xb.ap, in_=self.xs_d.ap[n * 128:(n + 1) * 128, :]), [self.xs_d], [xb])
        yield
        bv = self.transposes(ps[2], [(xb, xb[:, k * 128:(k + 1) * 128], 128) for k in range(8)])
        self.V(lambda e, bv=bv: e.tensor_copy(out=xT[:].rearrange("p a b -> p (a b)"), in_=bv[:, 0:1024]), [ps[2]], [xT])
        yield
        groups = [(0, 512), (512, 256), (768, 512), (1280, 128), (1408, 296)]
        for gi, (c0, w) in enumerate(groups):
            bank = ps[gi % 2]
            for k in range(8):
                self.P(lambda e, bank=bank, k=k, c0=c0, w=w: e.matmul(bank[:, 0:w], lhsT=xT[:, k, :], rhs=w_in[:, k, c0:c0 + w], start=(k == 0), stop=(k == 7)),
                       [xT, w_in], [bank])
            self.V(lambda e, bank=bank, c0=c0, w=w: e.tensor_copy(out=hsb[:, c0:c0 + w], in_=bank[:, 0:w]), [bank], [hsb])
            yield
        self.rope(hsb, 0, 8, 64, self.cs64, n, qa_r, qa_r[:], t1, t2, perm=True)
        yield
        self.rope(hsb, 512, 2, 64, self.cs64, n, ka_r, ka_r[:], t1, t2)
        self.V(lambda e: e.tensor_copy(out=va[:, n, :], in_=hsb[:, 640:768]), [hsb], [B["va_b"][n]])
        yield
        self.rope(hsb, 768, 8, 64, self.cs64, n, qb_r, qb_r[:], t1, t2, perm=True)
        yield
        self.rope(hsb, 1280, 1, 64, self.cs64, n, kb_r, kb_r[:, 0:1, :], t1, t2)
        self.V(lambda e: e.tensor_copy(out=kb_r[:, 1, :], in_=kb_r[:, 0, :]), [kb_r], [kb_r])
        self.V(lambda e: e.tensor_copy(out=vbx[:, n, 0:64], in_=hsb[:, 1344:1408]), [hsb], [B["vbx_b"][n]])
        self.V(lambda e: e.tensor_scalar(out=wsm[:, 0, :], in0=hsb[:, 1696:1704], scalar1=0.0625, scalar2=None, op0=ALU.mult), [hsb], [wsm])
        self.V(lambda e: e.tensor_scalar(out=wsm[:, 2, :], in0=wsm[:, 0, :], scalar1=0.0, scalar2=2.0, op0=ALU.is_ge, op1=ALU.mult), [wsm], [wsm])
        self.V(lambda e: e.tensor_scalar(out=wsm[:, 2, :], in0=wsm[:, 2, :], scalar1=-1.0, scalar2=None, op0=ALU.add), [wsm], [wsm])
        self.V(lambda e: e.tensor_tensor(out=wsm[:, 1, :], in0=wsm[:, 0, :], in1=wsm[:, 2, :], op=ALU.mult), [wsm], [wsm])
        yield
        self.rope(hsb, 1408, 8, 32, self.cs32, n, qi_f, qi_f[:], t1, t2)
        self.V(lambda e: e.tensor_tensor(out=qi_r[:], in0=qi_f[:], in1=wsm[:, 1, :].unsqueeze(2).to_broadcast([128, 8, 32]), op=ALU.mult), [qi_f, wsm], [qi_r])
        yield
        kiv = T(hsb[:, 1664:1696], "kiv")
        kiv.b = hsb.b
        self.layer_norm(kiv, ki_l, self.ikgb, 32, self.ikgb[:, 0, :], self.ikgb[:, 1, :], self.scrA, gb_eng="dve")
        self.V(lambda e: e.tensor_copy(out=hsb[:, 1664:1696], in_=ki_l[:]), [ki_l], [hsb])
        self.rope(hsb, 1664, 1, 32, self.cs32, n, ki_r, ki_r[:], t1, t2)
        yield
        bv = self.transposes(ps[2], [(qa_r, qa_r[:].rearrange("p a b -> p (a b)")[:, i * 128:(i + 1) * 128], 128) for i in range(4)])
        for g in range(2):
            pp = slice(64 * g, 64 * g + 64)
            self.V(lambda e, bv=bv, g=g, pp=pp: e.tensor_copy(out=qaTz[g][pp].rearrange("p a b -> p (a b)"), in_=bv[pp, 0:512]), [ps[2]], [qaTz[g]])
        bv = self.transposes(ps[2], [(qb_r, qb_r[:].rearrange("p a b -> p (a b)")[:, i * 128:(i + 1) * 128], 128) for i in range(4)], col0=512)
        for g in range(2):
            pp = slice(64 * g, 64 * g + 64)
            self.V(lambda e, bv=bv, g=g, pp=pp: e.tensor_copy(out=qbTz[g][pp].rearrange("p a b -> p (a b)"), in_=bv[pp, 512:1024]), [ps[2]], [qbTz[g]])
        yield
        bv = self.transposes(ps[0], [(qi_r, qi_r[:, h, :], 32) for h in range(8)])
        self.V(lambda e, bv=bv: e.tensor_copy(out=qiT[0:32].rearrange("p a b -> p (a b)"), in_=bv[0:32, 0:1024]), [ps[0]], [qiT])
        bv = self.transposes(ps[1], [(ka_r, ka_r[:].rearrange("p a b -> p (a b)"), 128), (kb_r, kb_r[:].rearrange("p a b -> p (a b)"), 128), (ki_r, ki_r[:, 0, :], 32)])
        self.V(lambda e, bv=bv: e.tensor_copy(out=kaT[:, nb], in_=bv[:, 0:128]), [ps[1]], [B["kaT_b"][n]])
        self.V(lambda e, bv=bv: e.tensor_copy(out=kbT[:, nb], in_=bv[:, 128:256]), [ps[1]], [B["kbT_b"][n]])
        self.V(lambda e, bv=bv: e.tensor_copy(out=kiT[0:32, nb], in_=bv[0:32, 256:384]), [ps[1]], [B["kiT_b"][n]])
        yield
        for g in range(2):
            rhs_q = qaTz[g][:].rearrange("p a b -> p (a b)")
            self.P(lambda e, rhs_q=rhs_q: e.matmul(ps[0][:, 0:512], lhsT=kaT[:, nb], rhs=rhs_q, start=True, stop=False), [B["kaT_b"][n], qaTz[g]], [ps[0]])
            self.P(lambda e: e.matmul(ps[0][:, 0:512], lhsT=self.ident[:], rhs=bcast(self.mb12[:, 0, :], 4), start=False, stop=True), [self.ident, self.mb12], [ps[0]])
            if n > 0:
                pb = slice((n - 1) * 128, n * 128)
                self.P(lambda e, rhs_q=rhs_q, pb=pb: e.matmul(ps[1][:, 0:512], lhsT=kaT[:, pb], rhs=rhs_q, start=True, stop=False), [B["kaT_b"][n - 1], qaTz[g]], [ps[1]])
                self.P(lambda e: e.matmul(ps[1][:, 0:512], lhsT=self.ident[:], rhs=bcast(self.mb12[:, 1, :], 4), start=False, stop=True), [self.ident, self.mb12], [ps[1]])
                self.A(lambda e: e.activation(out=E[:, 0:1024], in_=psf[0][:, 0:1024], func=AF.Exp, scale=0.125), [ps[0], ps[1]], [E])
            else:
                self.A(lambda e: e.activation(out=E[:, 0:512], in_=ps[0][:, 0:512], func=AF.Exp, scale=0.125), [ps[0]], [E])
            self.P(lambda e, g=g: e.matmul(ps[2][0:64, 0:512], lhsT=va[:, n, g * 64:(g + 1) * 64], rhs=E[:, 0:512], start=True, stop=(n == 0)), [B["va_b"][n], E], [ps[2]])
            self.P(lambda e: e.matmul(ps[0][0:64, 0:512], lhsT=self.ones64[:], rhs=E[:, 0:512], start=True, stop=(n == 0)), [self.ones64, E], [ps[0]])
            if n > 0:
                self.P(lambda e, g=g: e.matmul(ps[2][0:64, 0:512], lhsT=va[:, n - 1, g * 64:(g + 1) * 64], rhs=E[:, 512:1024], start=False, stop=True), [B["va_b"][n - 1], E], [ps[2]])
                self.P(lambda e: e.matmul(ps[0][0:64, 0:512], lhsT=self.ones64[:], rhs=E[:, 512:1024], start=False, stop=True), [self.ones64, E], [ps[0]])
            for hh in range(4):
                self.V(lambda e, g=g, hh=hh: e.tensor_scalar(out=rec[0:64, hh * 128:(hh + 1) * 128], in0=ps[0][0:64, hh * 128:(hh + 1) * 128], scalar1=self.esink[0:64, 4 * g + hh:4 * g + hh + 1], scalar2=None, op0=ALU.add), [ps[0], self.esink], [rec])
            self.A(lambda e: e.activation(out=rec[0:64, 0:512], in_=rec[0:64, 0:512], func=AF.Ln), [rec], [rec])
            self.A(lambda e: e.activation(out=rec[0:64, 0:512], in_=rec[0:64, 0:512], func=AF.Exp, scale=-1.0), [rec], [rec])
            ev = lambda ap: ap.rearrange("p (a two b) -> p a two b", two=2, b=128)
            self.V(lambda e, g=g: e.tensor_tensor(out=oT2[0:64, 2 * g:2 * g + 2, :], in0=ev(ps[2][0:64, 0:512])[:, :, 0, :], in1=ev(rec[0:64, 0:512])[:, :, 0, :], op=ALU.mult), [ps[2], rec], [oT2])
            self.V(lambda e: e.tensor_tensor(out=ostg[0:64, :, :], in0=ev(ps[2][0:64, 0:512])[:, :, 1, :], in1=ev(rec[0:64, 0:512])[:, :, 1, :], op=ALU.mult), [ps[2], rec], [ostg])
            self.DMA("sp", lambda e, g=g: e.dma_start(out=oT2[64:128, 2 * g:2 * g + 2, :], in_=ostg[0:64, :, :]), [ostg], [oT2])
            yield

    def gen_IDX(self, l, n, B):
        NT, S_, KTOP = self.NT, self.S_, self.KTOP
        ps = self.ps
        par = n % 2
        nb = slice(n * 128, (n + 1) * 128)
        w_in, kaT, kbT, kiT, va, vbx, idx = B["w_in"], B["kaT"], B["kbT"], B["kiT"], B["va"], B["vbx"], B["idx"]
        mask, xb, xT, hsb, t1, t2 = B["mask"][par], B["xb"][par], B["xT"], B["hsb"], B["t1"], B["t2"]
        qa_r, qb_r, ka_r, kb_r, qi_f, qi_r, ki_l, ki_r = B["qa_r"], B["qb_r"], B["ka_r"], B["kb_r"], B["qi_f"], B["qi_r"], B["ki_l"], B["ki_r"]
        wsm, qaTz, qbTz, qiT = B["wsm"][par], B["qaTz"], B["qbTz"][n % 3], B["qiT"][par]
        E, R, oT2, ostg, rec = B["E_f"], B["R"], B["oT2"][n % 3], B["ostg_f"], B["rec_f"]
        psf = self.psfull
        bs, m8 = B["bs"], B["m8"]
        Nn = (n + 1) * 128
        nblk = (Nn + 511) // 512
        cnt = 0
        for jb in range(nblk):
            Wb = min(512, Nn - jb * 512)
            cs_ = slice(jb * 512, jb * 512 + Wb)
            for h2 in range(4):
                pr = cnt % 2
                Rb = R[pr]
                cnt += 1
                for u in range(2):
                    h = 2 * h2 + u
                    bank = ps[2 * pr + u]
                    self.P(lambda e, bank=bank, h=h, cs_=cs_, Wb=Wb: e.matmul(bank[:, 0:Wb], lhsT=qiT[:, h, :], rhs=kiT[:, cs_], start=True, stop=True), [qiT] + B["kiT_b"][4 * jb:min(n + 1, 4 * jb + 4)], [bank])
                if False and h2 == 3:
                    for u in range(2):
                        h = 2 * h2 + u
                        bank = ps[2 * pr + u]
                        self.V(lambda e, bank=bank, Rb=Rb, u=u, h=h, Wb=Wb: e.tensor_scalar(out=Rb[:, u, 0:Wb], in0=bank[:, 0:Wb], scalar1=0.0, scalar2=wsm[:, 2, h:h + 1], op0=ALU.max, op1=ALU.mult), [bank, wsm], [Rb])
                        self.V(lambda e, Rb=Rb, u=u, cs_=cs_, Wb=Wb: e.tensor_tensor(out=idx[:, cs_], in0=idx[:, cs_], in1=Rb[:, u, 0:Wb], op=ALU.add), [Rb, idx], [idx])
                    yield
                    continue
                self.A(lambda e, pr=pr, Rb=Rb, Wb=Wb: e.activation(out=Rb[:, :, 0:Wb], in_=psf[pr].rearrange("p (a b) -> p a b", b=512)[:, :, 0:Wb], func=AF.Relu), [ps[2 * pr], ps[2 * pr + 1]], [Rb])
                for u in range(2):
                    h = 2 * h2 + u
                    if h == 0:
                        self.V(lambda e, Rb=Rb, cs_=cs_, Wb=Wb: e.tensor_scalar(out=idx[:, cs_], in0=Rb[:, 0, 0:Wb], scalar1=wsm[:, 2, 0:1], scalar2=None, op0=ALU.mult), [Rb, wsm], [idx])
                    else:
                        self.V(lambda e, Rb=Rb, cs_=cs_, Wb=Wb, h=h, u=u: e.scalar_tensor_tensor(out=idx[:, cs_], in0=Rb[:, u, 0:Wb], scalar=wsm[:, 2, h:h + 1], in1=idx[:, cs_], op0=ALU.mult, op1=ALU.add), [Rb, wsm, idx], [idx])
                yield
        self.G(lambda e: e.affine_select(out=idx[:, nb], in_=idx[:, nb], pattern=[[-1, 128]], compare_op=ALU.is_ge, fill=self.freg(e, -1e30), base=0, channel_multiplier=1), [idx], [idx])

    def gen_BIS(self, l, n, B):
        NT, S_, KTOP = self.NT, self.S_, self.KTOP
        ps = self.ps
        par = n % 2
        nb = slice(n * 128, (n + 1) * 128)
        w_in, kaT, kbT, kiT, va, vbx, idx = B["w_in"], B["kaT"], B["kbT"], B["kiT"], B["va"], B["vbx"], B["idx"]
        mask, xb, xT, hsb, t1, t2 = B["mask"][par], B["xb"][par], B["xT"], B["hsb"], B["t1"], B["t2"]
        qa_r, qb_r, ka_r, kb_r, qi_f, qi_r, ki_l, ki_r = B["qa_r"], B["qb_r"], B["ka_r"], B["kb_r"], B["qi_f"], B["qi_r"], B["ki_l"], B["ki_r"]
        wsm, qaTz, qbTz, qiT = B["wsm"][par], B["qaTz"], B["qbTz"][n % 3], B["qiT"][par]
        E, R, oT2, ostg, rec = B["E_f"], B["R"], B["oT2"][n % 3], B["ostg_f"], B["rec_f"]
        psf = self.psfull
        bs, m8 = B["bs"], B["m8"]
        Nn = (n + 1) * 128
        if Nn <= KTOP:
            self.V(lambda e: e.tensor_scalar(out=mask[:, 0:Nn], in0=idx[:, 0:Nn], scalar1=-1e29, scalar2=None, op0=ALU.is_gt), [idx], [mask])
            yield
        else:
            NI = self.NITER
            self.V(lambda e: e.max(out=m8[:, 0:8], in_=idx[:, 0:Nn]), [idx], [m8])
            self.V(lambda e: e.tensor_reduce(out=bs[:, 0:1], in_=idx[:, 0:KTOP], axis=AX.X, op=ALU.min), [idx], [bs])
            self.V(lambda e: e.tensor_scalar(out=bs[:, 0:1], in0=bs[:, 0:1], scalar1=-1e-3, scalar2=None, op0=ALU.add), [bs], [bs])
            self.V(lambda e: e.tensor_scalar(out=m8[:, 7:8], in0=m8[:, 7:8], scalar1=1e-3, scalar2=None, op0=ALU.add), [m8], [m8])
            self.V(lambda e: e.tensor_tensor(out=bs[:, 1:2], in0=m8[:, 7:8], in1=bs[:, 0:1], op=ALU.subtract), [m8, bs], [bs])
            self.V(lambda e: e.tensor_scalar(out=bs[:, 8:8 + NI + 2], in0=self.p2n[:, 0:NI + 2], scalar1=bs[:, 1:2], scalar2=None, op0=ALU.mult), [bs, self.p2n], [bs])
            self.V(lambda e: e.scalar_tensor_tensor(out=bs[:, 2:3], in0=bs[:, 0:1], scalar=-1.0, in1=bs[:, 9:10], op0=ALU.mult, op1=ALU.add), [bs], [bs])
            yield
            for it in range(1, NI + 1):
                self.A(lambda e: e.activation(out=mask[:, 0:Nn], in_=idx[:, 0:Nn], func=AF.Sign, bias=bs[:, 2:3], scale=1.0, accum_out=bs[:, 3:4]), [idx, bs], [mask, bs])
                self.A(lambda e: e.activation(out=bs[:, 5:6], in_=bs[:, 2:3], func=AF.Copy), [bs], [bs])
                self.A(lambda e: e.activation(out=bs[:, 4:5], in_=bs[:, 3:4], func=AF.Sign, bias=self.cb[:, n:n + 1], scale=1.0), [bs, self.cb], [bs])
                self.A(lambda e, it=it: e.activation(out=bs[:, 2:3], in_=bs[:, 4:5], func=AF.Identity, scale=bs[:, 8 + it + 1:8 + it + 2], bias=bs[:, 2:3]), [bs], [bs])
                yield
            self.V(lambda e: e.scalar_tensor_tensor(out=bs[:, 7:8], in0=bs[:, 2:3], scalar=-1.0, in1=bs[:, 8 + NI + 1:8 + NI + 2], op0=ALU.mult, op1=ALU.add), [bs], [bs])
            self.V(lambda e: e.tensor_scalar(out=mask[:, 0:Nn], in0=idx[:, 0:Nn], scalar1=bs[:, 7:8], scalar2=None, op0=ALU.is_gt), [idx, bs], [mask])
            yield

    def gen_Y(self, l, n, B):
        ps = self.ps
        par = n % 2
        w_o, kbT, vbx = B["w_o"], B["kbT"], B["vbx"]
        mask, qbTz, oT2, ostg = B["mask"][par], B["qbTz"][n % 3], B["oT2"][n % 3], B["ostg_b"]
        psf = self.psfull
        rec_hi, rec_lo, xres, yt = B["rec_hi"], B["rec_lo"], B["xres"], B["yt"]
        self.DMA("sp", lambda e: e.dma_start(out=xres.ap, in_=self.xs_d.ap[n * 128:(n + 1) * 128, :]), [self.xs_d], [xres])
        yield
        cnt = 0
        for j0 in range(0, n + 1, 8):
            js = list(range(j0, min(n + 1, j0 + 8)))
            mT = B["mT"][(j0 // 8) % 2]
            bv = self.transposes(ps[3], [(mask, mask[:, j * 128:(j + 1) * 128], 128) for j in js])
            nj = len(js)
            self.A(lambda e, bv=bv, mT=mT, nj=nj: e.activation(out=mT[:].rearrange("p a b -> p (a b)")[:, 0:nj * 128], in_=bv[:, 0:nj * 128], func=AF.Identity, scale=30000.0, bias=self.negbig[:, 0:1]), [ps[3], self.negbig], [mT])
            yield
            for j in js:
                jb_ = slice(j * 128, (j + 1) * 128)
                E = B["E_b"][cnt % 2]
                cnt += 1
                for hf in range(2):
                    self.P(lambda e, hf=hf, jb_=jb_: e.matmul(ps[4 + hf][:, 0:512], lhsT=kbT[:, jb_], rhs=qbTz[hf][:].rearrange("p a b -> p (a b)"), start=True, stop=False), [B["kbT_b"][j], qbTz[hf]], [ps[4 + hf]])
                    self.P(lambda e, hf=hf, mT=mT, j=j, j0=j0: e.matmul(ps[4 + hf][:, 0:512], lhsT=self.ident[:], rhs=bcast(mT[:, j - j0, :], 4), start=False, stop=True), [self.ident, mT], [ps[4 + hf]])
                self.A(lambda e, E=E: e.activation(out=E[:, 0:1024], in_=psf[2][:, 0:1024], func=AF.Exp, scale=0.125), [ps[4], ps[5]], [E])
                for hf in range(2):
                    self.P(lambda e, hf=hf, j=j, E=E: e.matmul(ps[6 + hf][:, 0:512], lhsT=vbx[:, j, :], rhs=E[:, hf * 512:(hf + 1) * 512], start=(j == 0), stop=(j == n)), [B["vbx_b"][j], E], [ps[6 + hf]])
                yield
        ev = lambda ap: ap.rearrange("p (a two b) -> p a two b", two=2, b=128)
        for hf in range(2):
            hs = slice(hf * 512, (hf + 1) * 512)
            self.A(lambda e, hf=hf, hs=hs: e.activation(out=rec_hi[64:128, hs], in_=ps[6 + hf][64:128, 0:512], func=AF.Ln), [ps[6 + hf]], [rec_hi])
            self.A(lambda e, hs=hs: e.activation(out=rec_hi[64:128, hs], in_=rec_hi[64:128, hs], func=AF.Exp, scale=-1.0), [rec_hi], [rec_hi])
            self.DMA("sp", lambda e, hs=hs: e.dma_start(out=rec_lo[0:64, hs], in_=rec_hi[64:128, hs]), [rec_hi], [rec_lo])
        yield
        for hf in range(2):
            hs = slice(hf * 512, (hf + 1) * 512)
            self.V(lambda e, hf=hf, hs=hs: e.tensor_tensor(out=oT2[0:64, 4 + 2 * hf:6 + 2 * hf, :], in0=ev(ps[6 + hf][0:64, 0:512])[:, :, 0, :], in1=ev(rec_lo[0:64, hs])[:, :, 0, :], op=ALU.mult), [ps[6 + hf], rec_lo], [oT2])
            self.V(lambda e, hf=hf, hs=hs: e.tensor_tensor(out=ostg[0:64, :, :], in0=ev(ps[6 + hf][0:64, 0:512])[:, :, 1, :], in1=ev(rec_lo[0:64, hs])[:, :, 1, :], op=ALU.mult), [ps[6 + hf], rec_lo], [ostg])
            self.DMA("sp", lambda e, hf=hf: e.dma_start(out=oT2[64:128, 4 + 2 * hf:6 + 2 * hf, :], in_=ostg[0:64, :, :]), [ostg], [oT2])
        yield
        for hf in range(2):
            hs = slice(hf * 512, (hf + 1) * 512)
            for k in range(8):
                self.P(lambda e, hf=hf, k=k, hs=hs: e.matmul(ps[4 + hf][:, 0:512], lhsT=oT2[:, k, :], rhs=w_o[:, k, hs], start=(k == 0), stop=(k == 7)), [oT2, w_o], [ps[4 + hf]])
            self.V(lambda e, hf=hf, hs=hs: e.scalar_tensor_tensor(out=yt[:, hs], in0=xres[:, hs], scalar=ALPHA, in1=ps[4 + hf][:, 0:512], op0=ALU.mult, op1=ALU.add), [xres, ps[4 + hf]], [yt])
            yield
        self.layer_norm(yt, yt, self.lnb, D, self.lnb[:, 0, :], self.lnb[:, 1, :], self.scrB)
        self.DMA("sp", lambda e: e.dma_start(out=self.x1_d.ap[n * 128:(n + 1) * 128, :], in_=yt.ap), [yt], [self.x1_d])
        yield

    def phase_ffn_a(self, l):
        NT = self.NT
        ps = self.ps
        self.S.barrier()
        self.ar.reset()
        aa = self.ar.alloc
        TG_T = min(4, NT)
        TG = TG_T * 128
        NG = NT // TG_T
        w_gu = aa("w_gu", [8, 2 * DFF], BF16)
        xbg = aa("xbg", [TG_T, D], BF16)
        xTg = [aa(f"xTg{i}", [8, TG], BF16) for i in range(2)]
        sgt = [aa(f"sgt{i}", [TG], F32) for i in range(2)]
        hidg = aa("hidg", [22, TG], BF16)
        hbuf = [Buf("hid_lo"), Buf("hid_hi")]
        rng = [(0, 6), (6, 12), (12, 18), (18, 22)]
        wsrc = self.wgu_d[l].rearrange("(k p) c -> p k c", p=128)
        wb = {}
        for (c0, c1) in rng:
            for part, base in (("g", 0), ("u", DFF)):
                b_ = Buf(f"wgu_{part}{c0}")
                for c in range(c0, c1):
                    wb[(part, c)] = b_
                self.DMA("pool", lambda e, c0=c0, c1=c1, base=base: e.dma_start(out=w_gu.ap[:, :, base + c0 * 128:base + c1 * 128], in_=wsrc[:, :, base + c0 * 128:base + c1 * 128]), [], [b_])

        def prologue(G):
            xT = xTg[G % 2]
            self.DMA("pool", lambda e: e.dma_start(out=xbg.ap, in_=self.x1_d.ap[G * TG:(G + 1) * TG, :].rearrange("(t p) c -> p t c", p=128)), [self.x1_d], [xbg])
            for ti in range(TG_T):
                bv = self.transposes(ps[4 + (ti % 2)], [(xbg, xbg[:, ti, k * 128:(k + 1) * 128], 128) for k in range(8)])
                self.A(lambda e, bv=bv, ti=ti: e.copy(out=xT[:, :, ti * 128:(ti + 1) * 128], in_=bv[:, 0:1024].rearrange("p (a b) -> p a b", b=128)), [ps[4 + (ti % 2)]], [xT])

        prologue(0)
        for G in range(NG):
            xT = xTg[G % 2]
            for c in range(22):
                if c == 4 and G + 1 < NG:
                    prologue(G + 1)
                bg, bu, sg = ps[c % 2], ps[2 + (c % 2)], sgt[c % 2]
                hb = hbuf[0 if c < 11 else 1]
                for k in range(8):
                    self.P(lambda e, bg=bg, k=k, c=c, xT=xT: e.matmul(bg[:, 0:TG], lhsT=w_gu[:, k, c * 128:(c + 1) * 128], rhs=xT[:, k, :], start=(k == 0), stop=(k == 7)), [wb[("g", c)], xT], [bg])
                for k in range(8):
                    self.P(lambda e, bu=bu, k=k, c=c, xT=xT: e.matmul(bu[:, 0:TG], lhsT=w_gu[:, k, DFF + c * 128:DFF + (c + 1) * 128], rhs=xT[:, k, :], start=(k == 0), stop=(k == 7)), [wb[("u", c)], xT], [bu])
                self.A(lambda e, bg=bg, sg=sg: e.activation(out=sg[:, 0:TG], in_=bg[:, 0:TG], func=AF.Silu), [bg], [sg])
                self.V(lambda e, bu=bu, c=c, sg=sg: e.tensor_tensor(out=hidg[:, c, :], in0=sg[:, 0:TG], in1=bu[:, 0:TG], op=ALU.mult), [sg, bu], [hb])
                if c == 10 or c == 21:
                    c0 = 0 if c == 10 else 11
                    self.DMA("sp", lambda e, G=G, c0=c0, c=c: e.dma_start(out=self.hid_d.ap[c0:c + 1, :, G * TG:(G + 1) * TG].rearrange("c p t -> p c t"), in_=hidg.ap[:, c0:c + 1, :]), [hb], [self.hid_d])

    def phase_ffn_b(self, l):
        NT = self.NT
        ps = self.ps
        self.S.barrier()
        self.ar.reset()
        aa = self.ar.alloc
        w_dn = aa("w_dn", [22, D], BF16)
        w_pg = aa("w_pg", [8, D], BF16)
        w_pp = aa("w_pp", [2, D], BF16)
        xres = [aa(f"xres{i}", [D], F32) for i in range(2)]
        xb = [aa(f"xb{i}", [D], BF16) for i in range(2)]
        xT = [aa(f"xT{i}", [8, 128], BF16) for i in range(2)]
        hidT = [aa(f"hidT{i}", [22, 128], BF16) for i in range(2)]
        pb = [aa(f"pb{i}", [256], BF16) for i in range(2)]
        pT = [aa(f"pT{i}", [2, 128], BF16) for i in range(2)]
        sgm = [aa(f"sgm{i}", [D], F32) for i in range(2)]
        yt = [aa(f"yt{i}", [D], F32) for i in range(2)]
        if l + 1 < self.DEPTH:
            self.load_mixer_weights(l + 1)
        self.load_w(w_dn, self.wdn_d[l].rearrange("(c p) m -> p c m", p=128))
        self.load_w(w_pg, self.wpg_d[l].rearrange("(k p) m -> p k m", p=128))
        self.load_w(w_pp, self.wpp_d[l].rearrange("(k p) m -> p k m", p=128))
        self.load_ln(2 + 2 * l)
        last = (l == self.DEPTH - 1)

        def prologue(n):
            i = n % 2
            self.DMA("sp", lambda e: e.dma_start(out=xres[i].ap, in_=self.x1_d.ap[n * 128:(n + 1) * 128, :]), [self.x1_d], [xres[i]])
            self.DMA("sp", lambda e: e.dma_start(out=hidT[i].ap, in_=self.hid_d.ap[:, :, n * 128:(n + 1) * 128].rearrange("c p t -> p c t")), [self.hid_d], [hidT[i]])
            self.DMA("pool", lambda e: e.dma_start(out=xb[i].ap, in_=self.x1_d.ap[n * 128:(n + 1) * 128, :]), [self.x1_d], [xb[i]])
            self.DMA("pool", lambda e: e.dma_start(out=pb[i].ap, in_=self.p_d[l, n * 128:(n + 1) * 128, :]), [], [pb[i]])
            bv = self.transposes(ps[6], [(xb[i], xb[i][:, k * 128:(k + 1) * 128], 128) for k in range(8)])
            self.A(lambda e, bv=bv: e.copy(out=xT[i][:].rearrange("p a b -> p (a b)"), in_=bv[:, 0:1024]), [ps[6]], [xT[i]])
            bv = self.transposes(ps[7], [(pb[i], pb[i][:, k * 128:(k + 1) * 128], 128) for k in range(2)])
            self.A(lambda e, bv=bv: e.copy(out=pT[i][:].rearrange("p a b -> p (a b)"), in_=bv[:, 0:256]), [ps[7]], [pT[i]])

        prologue(0)
        for n in range(NT):
            i = n % 2
            if n + 1 < NT:
                prologue(n + 1)
            for hf in range(2):
                hs = slice(hf * 512, (hf + 1) * 512)
                for c in range(22):
                    self.P(lambda e, hf=hf, c=c, hs=hs, i=i: e.matmul(ps[0 + hf][:, 0:512], lhsT=hidT[i][:, c, :], rhs=w_dn[:, c, hs], start=(c == 0), stop=(c == 21)), [hidT[i], w_dn], [ps[0 + hf]])
                for k in range(8):
                    self.P(lambda e, hf=hf, k=k, hs=hs, i=i: e.matmul(ps[2 + hf][:, 0:512], lhsT=xT[i][:, k, :], rhs=w_pg[:, k, hs], start=(k == 0), stop=(k == 7)), [xT[i], w_pg], [ps[2 + hf]])
                for k in range(2):
                    self.P(lambda e, hf=hf, k=k, hs=hs, i=i: e.matmul(ps[4 + hf][:, 0:512], lhsT=pT[i][:, k, :], rhs=w_pp[:, k, hs], start=(k == 0), stop=(k == 1)), [pT[i], w_pp], [ps[4 + hf]])
                self.A(lambda e, hf=hf, hs=hs, i=i: e.activation(out=sgm[i][:, hs], in_=ps[2 + hf][:, 0:512], func=AF.Sigmoid), [ps[2 + hf]], [sgm[i]])
                self.V(lambda e, hf=hf, hs=hs, i=i: e.tensor_tensor(out=sgm[i][:, hs], in0=sgm[i][:, hs], in1=ps[4 + hf][:, 0:512], op=ALU.mult), [sgm[i], ps[4 + hf]], [sgm[i]])
                self.V(lambda e, hf=hf, hs=hs, i=i: e.scalar_tensor_tensor(out=yt[i][:, hs], in0=xres[i][:, hs], scalar=ALPHA, in1=ps[0 + hf][:, 0:512], op0=ALU.mult, op1=ALU.add), [xres[i], ps[0 + hf]], [yt[i]])
                self.V(lambda e, hs=hs, i=i: e.tensor_tensor(out=yt[i][:, hs], in0=yt[i][:, hs], in1=sgm[i][:, hs], op=ALU.add), [yt[i], sgm[i]], [yt[i]])
            self.layer_norm(yt[i], yt[i], self.lnb, D, self.lnb[:, 0, :], self.lnb[:, 1, :], self.scrA if i == 0 else self.scrB, gb_eng="dve")
            if last:
                self.DMA("sp", lambda e, n=n, i=i: e.dma_start(out=self.y_d[n * 128:(n + 1) * 128, :], in_=yt[i].ap), [yt[i]], [self.yb])
            else:
                self.DMA("sp", lambda e, n=n, i=i: e.dma_start(out=self.xs_d.ap[n * 128:(n + 1) * 128, :], in_=yt[i].ap), [yt[i]], [self.xs_d])


def make_in_maps(x, p, positions, ln_in_g, ln_in_b, w_in, attn_sinks, idx_k_g, idx_k_b, w_o,
                 ln1_g, ln1_b, w_gu, w_down, w_pg, w_pp, ln2_g, ln2_b, NT, DEPTH):
    f = lambda a: np.ascontiguousarray(np.asarray(a), dtype=np.float32)
    B = x.shape[0]
    rows = [np.concatenate([f(ln_in_g), f(ln_in_b)])]
    for l in range(DEPTH):
        rows.append(np.concatenate([f(ln1_g)[l], f(ln1_b)[l]]))
        rows.append(np.concatenate([f(ln2_g)[l], f(ln2_b)[l]]))
    lnp = np.stack(rows).astype(np.float32)
    inv = np.concatenate([
        (10000.0 ** (-np.arange(32, dtype=np.float32) / np.float32(32))).astype(np.float32),
        (10000.0 ** (-np.arange(16, dtype=np.float32) / np.float32(16))).astype(np.float32)]).astype(np.float32)
    ikgb = np.concatenate([f(idx_k_g), f(idx_k_b)], axis=1).astype(np.float32)
    shared = {"inv": inv, "lnp": lnp, "w_in": f(w_in), "w_o": f(w_o), "w_gu": f(w_gu), "w_down": f(w_down),
              "w_pg": f(w_pg), "w_pp": f(w_pp), "sinks": f(attn_sinks), "ikgb": ikgb}
    maps = []
    pos = np.asarray(positions).astype(np.int32)
    for b in range(B):
        m = dict(shared)
        m["x"] = f(x[b])
        m["p"] = f(np.asarray(p)[:, b])
        m["pos"] = np.ascontiguousarray(pos[b].reshape(NT, 128).T)
        maps.append(m)
    return maps


_CACHE = {}


def kernel(**inputs):
    x = np.asarray(inputs["x"])
    B, S_, _ = x.shape
    NT = S_ // 128
    DEPTH = np.asarray(inputs["w_in"]).shape[0]
    KTOP = min(256, S_ // 4)
    key = (NT, KTOP, DEPTH)
    if key not in _CACHE:
        _CACHE[key] = Builder(NT=NT, KTOP=KTOP, DEPTH=DEPTH)
    bld = _CACHE[key]
    maps = make_in_maps(NT=NT, DEPTH=DEPTH, **inputs)
    res = run_bass_kernel_spmd(bld.nc, maps, core_ids=list(range(B)))
    out = np.stack([np.asarray(r["y"]) for r in res.results], axis=0).astype(np.float32)
    return out
```

```python
import numpy as np
import concourse.bass as bass
import concourse.mybir as mybir
from concourse.bass_utils import run_bass_kernel_spmd

F32 = mybir.dt.float32
BF16 = mybir.dt.bfloat16
I32 = mybir.dt.int32
AF = mybir.ActivationFunctionType
ALU = mybir.AluOpType
AX = mybir.AxisListType

ENGS = ["pe", "act", "dve", "pool", "sp"]
NDMASEM = 14
EPOCH = 4000
STRICT = False

D = 1024
DFF = 2816
INC = 1704
ALPHA = float(4 ** 0.25)
EPS = 1e-5
TWO_PI = 6.283185307179586
C1 = 6.28125
C2 = TWO_PI - C1


class Buf:
    __slots__ = ("name", "last_w", "readers")

    def __init__(self, name):
        self.name = name
        self.last_w = None
        self.readers = []


class Sched:
    def __init__(self, nc):
        self.nc = nc
        self.ops = {e: [] for e in ENGS}
        self.count = {e: 0 for e in ENGS}
        self.epoch = {e: 0 for e in ENGS}
        self.esems = {e: [nc.alloc_semaphore(f"s_{e}_0")] for e in ENGS}
        self.dq = {"sp": 0, "pool": 1, "act": 2}
        self.dsems = [nc.alloc_semaphore(f"s_dma_{i}") for i in range(3 * NDMASEM)]
        self.dma_n = [0, 0, 0]
        self.dma_last = [0] * (3 * NDMASEM)
        self.waited = {e: {} for e in ENGS}

    def _tok(self, eng):
        if self.count[eng] >= EPOCH:
            self.epoch[eng] += 1
            self.count[eng] = 0
            self.esems[eng].append(self.nc.alloc_semaphore(f"s_{eng}_{self.epoch[eng]}"))
        self.count[eng] += 1
        return ("e", eng, self.epoch[eng], self.count[eng])

    def op(self, eng, fn, reads=(), writes=(), dma=False, extra=()):
        if getattr(self, "dry", False):
            return None
        deps = set(extra)
        for b in reads:
            if b.last_w is not None:
                deps.add(b.last_w)
        for b in writes:
            if b.last_w is not None and (STRICT or dma or b.last_w[0] != "e" or b.last_w[1] != eng):
                deps.add(b.last_w)
            for r in b.readers:
                if STRICT or dma or r[0] != "e" or r[1] != eng:
                    deps.add(r)
        if eng == "pe" and not dma:
            deps = {d for d in deps if not (d[0] == "e" and d[1] == "pe")}
        if dma:
            q = self.dq[eng]
            i = self.dma_n[q]
            self.dma_n[q] += 1
            si = q * NDMASEM + (i % NDMASEM)
            val = self.dma_last[si] + 16
            if self.dma_last[si] > 0:
                deps.add(("d", si, 0, self.dma_last[si]))
            self.dma_last[si] = val
            tok = ("d", si, 0, val)
        else:
            tok = self._tok(eng)
        need = {}
        for d in deps:
            key = d[:3]
            need[key] = max(need.get(key, 0), d[3])
        waits = []
        w = self.waited[eng]
        for key, v in need.items():
            if key[0] == "e":
                done = False
                for k2, v2 in w.items():
                    if k2[0] == "e" and k2[1] == key[1] and (k2[2] > key[2] or (k2[2] == key[2] and v2 >= v)):
                        done = True
                        break
                if done:
                    continue
            else:
                if w.get(key, 0) >= v:
                    continue
            w[key] = max(w.get(key, 0), v)
            waits.append((key, v))
        self.ops[eng].append((waits, fn, tok))
        for b in reads:
            b.readers.append(tok)
        for b in writes:
            b.last_w = tok
            b.readers = []
        return tok

    def all_tokens(self):
        toks = []
        for e in ENGS:
            if self.count[e] > 0 or self.epoch[e] > 0:
                toks.append(("e", e, self.epoch[e], self.count[e]))
        for si in range(3 * NDMASEM):
            if self.dma_last[si] > 0:
                toks.append(("d", si, 0, self.dma_last[si]))
        return toks

    def barrier(self):
        toks = self.all_tokens()
        for e in ENGS:
            waits = []
            for t in toks:
                if t[0] == "e" and t[1] == e:
                    continue
                waits.append((t[:3], t[3]))
                self.waited[e][t[:3]] = max(self.waited[e].get(t[:3], 0), t[3])
            self.ops[e].append((waits, None, None))

    def _sem(self, key):
        if key[0] == "e":
            return self.esems[key[1]][key[2]]
        return self.dsems[key[1]]

    def emit(self):
        nc = self.nc
        handles = {"pe": "tensor", "act": "scalar", "dve": "vector", "pool": "gpsimd", "sp": "sync"}
        with nc.Block() as block:
            for e in ENGS:
                ops = self.ops[e]

                def body(engh, ops=ops):
                    for waits, fn, tok in ops:
                        for key, v in waits:
                            engh.wait_ge(self._sem(key), v)
                        if fn is None:
                            continue
                        ins = fn(engh)
                        ins.then_inc(self._sem(tok[:3]), 1 if tok[0] == "e" else 16)

                getattr(block, handles[e])(body)


class T:
    def __init__(self, ap, name):
        self.ap = ap
        self.b = Buf(name)

    def __getitem__(self, k):
        return self.ap[k]


class Arena:
    def __init__(self, nc, nbytes, name):
        self.t = nc.alloc_sbuf_tensor(name, [128, nbytes // 2], BF16)
        self.n = nbytes
        self.off = 0

    def reset(self):
        self.off = 0

    def alloc_at(self, name, shape, dtype, off):
        save, lim = self.off, getattr(self, "limit", self.n)
        self.off, self.limit = off, self.n
        t = self.alloc(name, shape, dtype)
        self.off, self.limit = save, lim
        return t

    def alloc(self, name, shape, dtype):
        size = 2 if dtype == BF16 else 4
        nel = int(np.prod(shape))
        nb = nel * size
        off = (self.off + 31) // 32 * 32
        self.off = off + nb
        assert self.off <= getattr(self, "limit", self.n), (name, self.off, getattr(self, "limit", self.n))
        ap = self.t[:, off // 2: off // 2 + nb // 2]
        if dtype != BF16:
            ap = ap.bitcast(dtype)
        if len(shape) == 2:
            ap = ap.rearrange("p (a b) -> p a b", b=shape[1])
        elif len(shape) == 3:
            ap = ap.rearrange("p (a b c) -> p a b c", b=shape[1], c=shape[2])
        return T(ap, name)


def bcast(ap, n):
    s = list(ap.shape)
    return ap.unsqueeze(1).to_broadcast([s[0], n] + s[1:])


class Builder:
    def __init__(self, NT=32, KTOP=256, DEPTH=2, NITER=16):
        self.NT, self.KTOP, self.DEPTH, self.NITER = NT, KTOP, DEPTH, NITER
        self.S_ = NT * 128
        nc = bass.Bass("TRN2", target_bir_lowering=False)
        self.nc = nc
        self.S = Sched(nc)
        S_ = self.S_
        dt = nc.dram_tensor
        self.x_d = dt("x", [S_, D], F32, kind="ExternalInput").ap()
        self.p_d = dt("p", [DEPTH, S_, 256], F32, kind="ExternalInput").ap()
        self.pos_d = dt("pos", [128, NT], I32, kind="ExternalInput").ap()
        self.inv_d = dt("inv", [48], F32, kind="ExternalInput").ap()
        self.lnp_d = dt("lnp", [1 + 2 * DEPTH, 2 * D], F32, kind="ExternalInput").ap()
        self.win_d = dt("w_in", [DEPTH, D, INC], F32, kind="ExternalInput").ap()
        self.wo_d = dt("w_o", [DEPTH, D, D], F32, kind="ExternalInput").ap()
        self.wgu_d = dt("w_gu", [DEPTH, D, 2 * DFF], F32, kind="ExternalInput").ap()
        self.wdn_d = dt("w_down", [DEPTH, DFF, D], F32, kind="ExternalInput").ap()
        self.wpg_d = dt("w_pg", [DEPTH, D, D], F32, kind="ExternalInput").ap()
        self.wpp_d = dt("w_pp", [DEPTH, 256, D], F32, kind="ExternalInput").ap()
        self.sk_d = dt("sinks", [DEPTH, 8], F32, kind="ExternalInput").ap()
        self.ik_d = dt("ikgb", [DEPTH, 64], F32, kind="ExternalInput").ap()
        self.y_d = dt("y", [S_, D], F32, kind="ExternalOutput").ap()
        self.xs_d = T(dt("xs", [S_, D], F32).ap(), "xs_d")
        self.x1_d = T(dt("x1s", [S_, D], F32).ap(), "x1_d")
        self.hid_d = T(dt("hids", [22, 128, S_], BF16).ap(), "hid_d")
        self.yb = Buf("y_d")
        self.ps = []
        self.psfull = []
        for i in range(4):
            pt = nc.alloc_psum_tensor(f"pp{i}", [128, 1024], F32)
            self.psfull.append(pt[:])
            self.ps.append(T(pt[:, 0:512], f"ps{2 * i}"))
            self.ps.append(T(pt[:, 512:1024], f"ps{2 * i + 1}"))
        self.pers = Arena(nc, 22720, "pers")
        self.ar = Arena(nc, 189760, "arena")
        self.build()

    def V(self, fn, r, w):
        return self.S.op("dve", fn, [t.b if isinstance(t, T) else t for t in r], [t.b if isinstance(t, T) else t for t in w])

    def A(self, fn, r, w):
        return self.S.op("act", fn, [t.b if isinstance(t, T) else t for t in r], [t.b if isinstance(t, T) else t for t in w])

    def G(self, fn, r, w):
        return self.S.op("pool", fn, [t.b if isinstance(t, T) else t for t in r], [t.b if isinstance(t, T) else t for t in w])

    def P(self, fn, r, w):
        return self.S.op("pe", fn, [t.b if isinstance(t, T) else t for t in r], [t.b if isinstance(t, T) else t for t in w])

    def DMA(self, eng, fn, r, w):
        return self.S.op(eng, fn, [t.b if isinstance(t, T) else t for t in r], [t.b if isinstance(t, T) else t for t in w], dma=True)

    def layer_norm(self, xt, yt, gb, width, g_ap, b_ap, scr, gb_eng="pool"):
        st, mv, sc = scr
        nchunk = (width + 511) // 512
        for c in range(nchunk):
            lo, hi = c * 512, min(width, (c + 1) * 512)
            self.V(lambda e, c=c, lo=lo, hi=hi: e.bn_stats(out=st[:, c, :], in_=xt[:, lo:hi]), [xt], [st])
        self.V(lambda e: e.bn_aggr(out=mv[:, 0:2], in_=st[:, 0:nchunk, :].rearrange("p a b -> p (a b)")), [st], [mv])
        self.V(lambda e: e.tensor_scalar(out=sc[:, 0:1], in0=mv[:, 1:2], scalar1=EPS, scalar2=None, op0=ALU.add), [mv], [sc])
        self.A(lambda e: e.activation(out=sc[:, 1:2], in_=sc[:, 0:1], func=AF.Ln), [sc], [sc])
        self.A(lambda e: e.activation(out=sc[:, 2:3], in_=sc[:, 1:2], func=AF.Exp, scale=-0.5), [sc], [sc])
        self.V(lambda e: e.tensor_scalar(out=sc[:, 3:4], in0=mv[:, 0:1], scalar1=sc[:, 2:3], scalar2=-1.0, op0=ALU.mult, op1=ALU.mult), [mv, sc], [sc])
        self.V(lambda e: e.tensor_scalar(out=yt[:, 0:width], in0=xt[:, 0:width], scalar1=sc[:, 2:3], scalar2=sc[:, 3:4], op0=ALU.mult, op1=ALU.add), [xt, sc], [yt])
        E_ = self.G if gb_eng == "pool" else self.V
        E_(lambda e: e.tensor_tensor(out=yt[:, 0:width], in0=yt[:, 0:width], in1=g_ap, op=ALU.mult), [yt, gb], [yt])
        E_(lambda e: e.tensor_tensor(out=yt[:, 0:width], in0=yt[:, 0:width], in1=b_ap, op=ALU.add), [yt, gb], [yt])

    def ln_scratch(self, alloc, tag):
        return (alloc("ln_st" + tag, [2, 6], F32), alloc("ln_mv" + tag, [4], F32), alloc("ln_sc" + tag, [8], F32))

    def freg(self, e, val):
        if not hasattr(self, "_fregs"):
            self._fregs = {}
        if val not in self._fregs:
            self._fregs[val] = e.to_reg(val)
        return self._fregs[val]

    def load_w(self, dst, src_ap):
        self.DMA("pool", lambda e: e.dma_start(out=dst.ap, in_=src_ap), [], [dst])

    def transposes(self, bank, specs, col0=0):
        bv = bank.ap.bitcast(BF16)
        for i, (srcT, sap, F) in enumerate(specs):
            c = col0 + i * 128
            self.P(lambda e, c=c, sap=sap, F=F: e.transpose(out=bv[0:F, c:c + 128], in_=sap, identity=self.ident[:]),
                   [srcT, self.ident], [bank])
        return bv

    @staticmethod
    def interleave(gx, nx, gy, ny):
        ix = iy = 0
        ax, ay = gx is not None, gy is not None
        while ax or ay:
            stepx = ax and (not ay or ix * max(ny, 1) <= iy * max(nx, 1))
            if stepx:
                try:
                    next(gx)
                    ix += 1
                except StopIteration:
                    ax = False
            else:
                try:
                    next(gy)
                    iy += 1
                except StopIteration:
                    ay = False

    @staticmethod
    def merge(g1, n1, g2, n2):
        i1 = i2 = 0
        a1, a2 = g1 is not None, g2 is not None
        while a1 or a2:
            step1 = a1 and (not a2 or i1 * max(n2, 1) <= i2 * max(n1, 1))
            if step1:
                try:
                    next(g1)
                    i1 += 1
                    yield
                except StopIteration:
                    a1 = False
            else:
                try:
                    next(g2)
                    i2 += 1
                    yield
                except StopIteration:
                    a2 = False

    def count(self, gen):
        self.S.dry = True
        n = sum(1 for _ in gen)
        self.S.dry = False
        return n

    def build(self):
        NT, S_, DEPTH = self.NT, self.S_, self.DEPTH
        pa = self.pers.alloc
        self.ident = pa("ident", [128], BF16)
        self.m12 = pa("m12", [2, 128], BF16)
        self.ones64 = pa("ones64", [64], BF16)
        self.cs64 = pa("cs64", [NT, 2, 32], F32)
        self.cs32 = pa("cs32", [NT, 2, 16], F32)
        self.lnb = pa("lnb", [2, D], F32)
        self.esink = pa("esink", [8], F32)
        self.ikgb = pa("ikgb", [2, 32], F32)
        self.mb12 = pa("mb12", [2, 128], BF16)
        self.cb = pa("cb", [NT], F32)
        self.p2n = pa("p2n", [24], F32)
        self.negbig = pa("negbig", [1], F32)
        self.scrA = self.ln_scratch(pa, "A")
        self.scrB = self.ln_scratch(pa, "B")
        top = (self.ar.n - (8 * INC * 2 + 8 * D * 2)) // 32 * 32
        self.ar.limit = top
        self.w_in_T = self.ar.alloc_at("w_in", [8, INC], BF16, top)
        self.w_o_T = self.ar.alloc_at("w_o", [8, D], BF16, top + 8 * INC * 2)
        self.setup_consts()
        self.load_mixer_weights(0)
        self.phase_ln_in()
        for l in range(DEPTH):
            self.phase_mixer(l)
            self.phase_ffn_a(l)
            self.phase_ffn_b(l)
        self.S.barrier()
        self.S.emit()

    def setup_consts(self):
        NT = self.NT
        ident, m12, ones64 = self.ident, self.m12, self.ones64
        self.G(lambda e: e.memset(ident[:], 1.0), [], [ident])
        self.G(lambda e: e.affine_select(out=ident[:], in_=ident[:], pattern=[[1, 128]], compare_op=ALU.is_equal, fill=self.freg(e, 0.0), base=0, channel_multiplier=-1), [ident], [ident])
        self.G(lambda e: e.memset(m12[:], 1.0), [], [m12])
        self.G(lambda e: e.affine_select(out=m12[:, 0, :], in_=m12[:, 0, :], pattern=[[1, 128]], compare_op=ALU.is_ge, fill=self.freg(e, 0.0), base=0, channel_multiplier=-1), [m12], [m12])
        self.G(lambda e: e.affine_select(out=m12[:, 1, :], in_=m12[:, 1, :], pattern=[[-1, 128]], compare_op=ALU.is_ge, fill=self.freg(e, 0.0), base=-1, channel_multiplier=1), [m12], [m12])
        self.G(lambda e: e.memset(ones64[:], 1.0), [], [ones64])
        self.V(lambda e: e.tensor_scalar(out=self.mb12[:], in0=m12[:], scalar1=30000.0, scalar2=-30000.0, op0=ALU.mult, op1=ALU.add), [m12], [self.mb12])
        self.G(lambda e: e.memset(self.negbig[:], -30000.0), [], [self.negbig])
        for n_ in range(NT):
            self.G(lambda e, n_=n_: e.memset(self.cb[:, n_:n_ + 1], float(-(2 * self.KTOP - (n_ + 1) * 128) + 0.5)), [], [self.cb])
        for k_ in range(24):
            self.G(lambda e, k_=k_: e.memset(self.p2n[:, k_:k_ + 1], float(-(2.0 ** (-k_)))), [], [self.p2n])
        self.ar.reset()
        aa = self.ar.alloc
        posi = aa("posi", [NT], I32)
        posf = aa("posf", [NT], F32)
        invb = aa("invb", [48], F32)
        ang = aa("ang", [NT, 48], F32)
        a2 = aa("a2", [NT, 48], F32)
        ki = aa("ki", [NT, 48], I32)
        kf = aa("kf", [NT, 48], F32)
        self.DMA("sp", lambda e: e.dma_start(out=posi.ap, in_=self.pos_d), [], [posi])
        self.DMA("sp", lambda e: e.dma_start(out=invb.ap, in_=self.inv_d.partition_broadcast(128)), [], [invb])
        self.V(lambda e: e.tensor_copy(out=posf[:], in_=posi[:]), [posi], [posf])
        self.V(lambda e: e.tensor_tensor(out=ang[:], in0=posf[:].unsqueeze(2).to_broadcast([128, NT, 48]), in1=bcast(invb[:], NT), op=ALU.mult), [posf, invb], [ang])
        for which in range(2):
            if which == 0:
                self.V(lambda e: e.tensor_scalar(out=a2[:], in0=ang[:], scalar1=float(np.pi / 2), scalar2=None, op0=ALU.add), [ang], [a2])
            else:
                self.V(lambda e: e.tensor_copy(out=a2[:], in_=ang[:]), [ang], [a2])
            self.V(lambda e: e.tensor_scalar(out=ki[:], in0=a2[:], scalar1=float(1.0 / TWO_PI), scalar2=None, op0=ALU.mult), [a2], [ki])
            self.V(lambda e: e.tensor_copy(out=kf[:], in_=ki[:]), [ki], [kf])
            self.V(lambda e: e.scalar_tensor_tensor(out=a2[:], in0=kf[:], scalar=-C1, in1=a2[:], op0=ALU.mult, op1=ALU.add), [kf, a2], [a2])
            self.V(lambda e: e.scalar_tensor_tensor(out=a2[:], in0=kf[:], scalar=-C2, in1=a2[:], op0=ALU.mult, op1=ALU.add), [kf, a2], [a2])
            self.V(lambda e: e.tensor_scalar(out=a2[:], in0=a2[:], scalar1=-3.1415925, scalar2=3.1415925, op0=ALU.max, op1=ALU.min), [a2], [a2])
            self.A(lambda e, which=which: e.activation(out=self.cs64[:, :, which, :], in_=a2[:, :, 0:32], func=AF.Sin), [a2], [self.cs64])
            self.A(lambda e, which=which: e.activation(out=self.cs32[:, :, which, :], in_=a2[:, :, 32:48], func=AF.Sin), [a2], [self.cs32])

    def load_mixer_weights(self, l):
        self.load_w(self.w_in_T, self.win_d[l].rearrange("(k p) c -> p k c", p=128))
        self.load_w(self.w_o_T, self.wo_d[l].rearrange("(k p) c -> p k c", p=128))

    def load_ln(self, idx):
        self.DMA("sp", lambda e: e.dma_start(out=self.lnb.ap.rearrange("p a b -> p (a b)"), in_=self.lnp_d[idx:idx + 1, :].to_broadcast([128, 2 * D])), [], [self.lnb])

    def phase_ln_in(self):
        self.S.barrier()
        self.ar.reset()
        self.load_ln(0)
        xt = [self.ar.alloc(f"xt{i}", [D], F32) for i in range(3)]

        def load(n):
            x = xt[n % 3]
            self.DMA("sp", lambda e: e.dma_start(out=x.ap, in_=self.x_d[n * 128:(n + 1) * 128, :]), [], [x])

        for n in range(min(2, self.NT)):
            load(n)
        for n in range(self.NT):
            x = xt[n % 3]
            if n + 2 < self.NT:
                load(n + 2)
            self.layer_norm(x, x, self.lnb, D, self.lnb[:, 0, :], self.lnb[:, 1, :], self.scrA if n % 2 == 0 else self.scrB)
            self.DMA("sp", lambda e, n=n, x=x: e.dma_start(out=self.xs_d.ap[n * 128:(n + 1) * 128, :], in_=x.ap), [x], [self.xs_d])

    def rope(self, hsb, c0, H, Dh, cs, n, dstT, dst4, t1, t2, perm=False):
        half = Dh // 2
        if perm:
            src = hsb[:, c0:c0 + H * Dh].rearrange("p (hi lo d) -> p hi lo d", hi=2, d=Dh)
            dst = dst4.rearrange("p (lo hi) d -> p hi lo d", hi=2)
            tv = lambda t: t[:, 0:H * half].rearrange("p (hi lo d) -> p hi lo d", hi=2, d=half)
            bc = lambda ap: ap.unsqueeze(1).unsqueeze(1).to_broadcast([128, 2, H // 2, half])
            sl = lambda ap, a, b: ap[:, :, :, a:b]
        else:
            src = hsb[:, c0:c0 + H * Dh].rearrange("p (h d) -> p h d", d=Dh)
            dst = dst4
            tv = lambda t: t[:, 0:H * half].rearrange("p (h d) -> p h d", d=half)
            bc = lambda ap: ap.unsqueeze(1).to_broadcast([128, H, half])
            sl = lambda ap, a, b: ap[:, :, a:b]
        x1, x2 = sl(src, 0, half), sl(src, half, Dh)
        cosb, sinb = bc(cs[:, n, 0, :]), bc(cs[:, n, 1, :])
        a, b = tv(t1), tv(t2)
        self.V(lambda e: e.tensor_tensor(out=a, in0=x1, in1=cosb, op=ALU.mult), [hsb, cs], [t1])
        self.V(lambda e: e.tensor_tensor(out=b, in0=x2, in1=sinb, op=ALU.mult), [hsb, cs], [t2])
        self.V(lambda e: e.tensor_tensor(out=sl(dst, 0, half), in0=a, in1=b, op=ALU.subtract), [t1, t2], [dstT])
        self.V(lambda e: e.tensor_tensor(out=a, in0=x2, in1=cosb, op=ALU.mult), [hsb, cs], [t1])
        self.V(lambda e: e.tensor_tensor(out=b, in0=x1, in1=sinb, op=ALU.mult), [hsb, cs], [t2])
        self.V(lambda e: e.tensor_tensor(out=sl(dst, half, Dh), in0=a, in1=b, op=ALU.add), [t1, t2], [dstT])

    def phase_mixer(self, l):
        NT, S_ = self.NT, self.S_
        self.S.barrier()
        self.ar.reset()
        aa = self.ar.alloc
        B = {}
        B["w_in"] = self.w_in_T
        B["w_o"] = self.w_o_T
        B["kaT"] = aa("kaT", [S_], BF16)
        B["kbT"] = aa("kbT", [S_], BF16)
        B["kiT"] = aa("kiT", [S_], BF16)
        B["va"] = aa("va", [NT, 128], BF16)
        B["vbx"] = aa("vbx", [NT, 128], BF16)
        B["idx"] = aa("idx", [S_], F32)
        B["mask"] = [aa(f"mask{i}", [S_], BF16) for i in range(2)]
        B["xb"] = [aa(f"xb{i}", [D], BF16) for i in range(2)]
        B["xT"] = aa("xT", [8, 128], BF16)
        B["hsb"] = aa("hsb", [INC], F32)
        B["t1"] = aa("t1", [256], F32)
        B["t2"] = aa("t2", [256], F32)
        B["qa_r"] = aa("qa_r", [8, 64], BF16)
        B["qb_r"] = aa("qb_r", [8, 64], BF16)
        B["ka_r"] = aa("ka_r", [2, 64], BF16)
        B["kb_r"] = aa("kb_r", [2, 64], BF16)
        B["qi_f"] = T(B["hsb"].ap[:, 0:256].rearrange("p (h d) -> p h d", d=32), "qi_f")
        B["qi_f"].b = B["hsb"].b
        B["qi_r"] = aa("qi_r", [8, 32], BF16)
        B["ki_l"] = aa("ki_l", [32], F32)
        B["ki_r"] = aa("ki_r", [1, 32], BF16)
        B["wsm"] = [aa(f"wsm{i}", [3, 8], F32) for i in range(2)]
        B["qaTz"] = [aa(f"qaTz{i}", [4, 128], BF16) for i in range(2)]
        B["qbTz"] = [[aa(f"qbTz{i}{j}", [4, 128], BF16) for j in range(2)] for i in range(3)]
        B["qiT"] = [aa(f"qiT{i}", [8, 128], BF16) for i in range(2)]
        B["E_f"] = aa("E_f", [1024], BF16)
        B["E_b"] = [aa(f"E_b{i}", [1024], BF16) for i in range(2)]
        B["mT"] = [aa(f"mT{i}", [8, 128], BF16) for i in range(2)]
        B["R"] = [aa(f"R{i}", [2, 512], F32) for i in range(2)]
        B["oT2"] = [aa(f"oT2{i}", [8, 128], BF16) for i in range(3)]
        B["ostg_f"] = aa("ostg_f", [2, 128], BF16)
        B["ostg_b"] = aa("ostg_b", [2, 128], BF16)
        B["rec_f"] = aa("rec_f", [512], F32)
        rec2 = aa("rec2", [1024], F32)
        B["rec_hi"] = T(rec2.ap, "rec_hi")
        B["rec_lo"] = T(rec2.ap, "rec_lo")
        B["xres"] = aa("xres", [D], F32)
        B["yt"] = aa("yt", [D], F32)
        B["bs"] = aa("bs", [40], F32)
        B["m8"] = aa("m8", [8], F32)

        for nm in ("kaT", "kbT", "kiT", "va", "vbx"):
            B[nm + "_b"] = [Buf(f"{nm}_{j}") for j in range(NT)]
        w_in, w_o = B["w_in"], B["w_o"]
        self.load_ln(1 + 2 * l)
        self.DMA("sp", lambda e: e.dma_start(out=self.esink.ap, in_=self.sk_d[l:l + 1, :].to_broadcast([128, 8])), [], [self.esink])
        self.A(lambda e: e.activation(out=self.esink[:], in_=self.esink[:], func=AF.Exp), [self.esink], [self.esink])
        self.DMA("sp", lambda e: e.dma_start(out=self.ikgb.ap.rearrange("p a b -> p (a b)"), in_=self.ik_d[l:l + 1, :].to_broadcast([128, 64])), [], [self.ikgb])
        self.G(lambda e: e.memset(B["vbx"][:], 1.0), [], [B["vbx"]])
        self.G(lambda e: e.memset(B["kiT"][:], 0.0), [], [B["kiT"]])
        for t_ in B["qaTz"] + B["qbTz"][0] + B["qbTz"][1] + B["qbTz"][2] + B["qiT"]:
            self.G(lambda e, t_=t_: e.memset(t_[:], 0.0), [], [t_])

        self.S.barrier()
        cFE = [self.count(self.gen_FE(l, n, B)) for n in range(NT)]
        cIX = [self.count(self.gen_IDX(l, n, B)) for n in range(NT)]
        cBS = [self.count(self.gen_BIS(l, n, B)) for n in range(NT)]
        cY = [self.count(self.gen_Y(l, n, B)) for n in range(NT)]

        def chainA(n):
            if n + 1 < NT:
                yield from self.gen_IDX(l, n + 1, B)
                g1, c1 = self.gen_BIS(l, n + 1, B), cBS[n + 1]
            else:
                g1, c1 = None, 0
            if n + 2 < NT:
                g2, c2 = self.gen_FE(l, n + 2, B), cFE[n + 2]
            else:
                g2, c2 = None, 0
            yield from self.merge(g1, c1, g2, c2)

        def lenA(n):
            return (cIX[n + 1] + cBS[n + 1] if n + 1 < NT else 0) + (cFE[n + 2] if n + 2 < NT else 0)

        for g in (self.gen_FE(l, 0, B), self.gen_IDX(l, 0, B), self.gen_BIS(l, 0, B)):
            for _ in g:
                pass
        if NT > 1:
            for _ in self.gen_FE(l, 1, B):
                pass
        for n in range(NT):
            self.interleave(chainA(n), lenA(n), self.gen_Y(l, n, B), cY[n])

    def gen_FE(self, l, n, B):
        NT, S_, KTOP = self.NT, self.S_, self.KTOP
        ps = self.ps
        par = n % 2
        nb = slice(n * 128, (n + 1) * 128)
        w_in, kaT, kbT, kiT, va, vbx, idx = B["w_in"], B["kaT"], B["kbT"], B["kiT"], B["va"], B["vbx"], B["idx"]
        mask, xb, xT, hsb, t1, t2 = B["mask"][par], B["xb"][par], B["xT"], B["hsb"], B["t1"], B["t2"]
        qa_r, qb_r, ka_r, kb_r, qi_f, qi_r, ki_l, ki_r = B["qa_r"], B["qb_r"], B["ka_r"], B["kb_r"], B["qi_f"], B["qi_r"], B["ki_l"], B["ki_r"]
        wsm, qaTz, qbTz, qiT = B["wsm"][par], B["qaTz"], B["qbTz"][n % 3], B["qiT"][par]
        E, R, oT2, ostg, rec = B["E_f"], B["R"], B["oT2"][n % 3], B["ostg_f"], B["rec_f"]
        psf = self.psfull
        bs, m8 = B["bs"], B["m8"]
        self.DMA("pool", lambda e: e.dma_start(out=xb.ap, in_=self.xs_d.ap[n * 128:(n + 1) * 128, :]), [self.xs_d], [xb])
        yield
        bv = self.transposes(ps[2], [(xb, xb[:, k * 128:(k + 1) * 128], 128) for k in range(8)])
        self.V(lambda e, bv=bv: e.tensor_copy(out=xT[:].rearrange("p a b -> p (a b)"), in_=bv[:, 0:1024]), [ps[2]], [xT])
        yield
        groups = [(0, 512), (512, 256), (768, 512), (1280, 128), (1408, 296)]
        for gi, (c0, w) in enumerate(groups):
            bank = ps[gi % 2]
            for k in range(8):
                self.P(lambda e, bank=bank, k=k, c0=c0, w=w: e.matmul(bank[:, 0:w], lhsT=xT[:, k, :], rhs=w_in[:, k, c0:c0 + w], start=(k == 0), stop=(k == 7)),
                       [xT, w_in], [bank])
            self.V(lambda e, bank=bank, c0=c0, w=w: e.tensor_copy(out=hsb[:, c0:c0 + w], in_=bank[:, 0:w]), [bank], [hsb])
            yield
        self.rope(hsb, 0, 8, 64, self.cs64, n, qa_r, qa_r[:], t1, t2, perm=True)
        yield
        self.rope(hsb, 512, 2, 64, self.cs64, n, ka_r, ka_r[:], t1, t2)
        self.V(lambda e: e.tensor_copy(out=va[:, n, :], in_=hsb[:, 640:768]), [hsb], [B["va_b"][n]])
        yield
        self.rope(hsb, 768, 8, 64, self.cs64, n, qb_r, qb_r[:], t1, t2, perm=True)
        yield
        self.rope(hsb, 1280, 1, 64, self.cs64, n, kb_r, kb_r[:, 0:1, :], t1, t2)
        self.V(lambda e: e.tensor_copy(out=kb_r[:, 1, :], in_=kb_r[:, 0, :]), [kb_r], [kb_r])
        self.V(lambda e: e.tensor_copy(out=vbx[:, n, 0:64], in_=hsb[:, 1344:1408]), [hsb], [B["vbx_b"][n]])
        self.V(lambda e: e.tensor_scalar(out=wsm[:, 0, :], in0=hsb[:, 1696:1704], scalar1=0.0625, scalar2=None, op0=ALU.mult), [hsb], [wsm])
        self.V(lambda e: e.tensor_scalar(out=wsm[:, 2, :], in0=wsm[:, 0, :], scalar1=0.0, scalar2=2.0, op0=ALU.is_ge, op1=ALU.mult), [wsm], [wsm])
        self.V(lambda e: e.tensor_scalar(out=wsm[:, 2, :], in0=wsm[:, 2, :], scalar1=-1.0, scalar2=None, op0=ALU.add), [wsm], [wsm])
        self.V(lambda e: e.tensor_tensor(out=wsm[:, 1, :], in0=wsm[:, 0, :], in1=wsm[:, 2, :], op=ALU.mult), [wsm], [wsm])
        yield
        self.rope(hsb, 1408, 8, 32, self.cs32, n, qi_f, qi_f[:], t1, t2)
        self.V(lambda e: e.tensor_tensor(out=qi_r[:], in0=qi_f[:], in1=wsm[:, 1, :].unsqueeze(2).to_broadcast([128, 8, 32]), op=ALU.mult), [qi_f, wsm], [qi_r])
        yield
        kiv = T(hsb[:, 1664:1696], "kiv")
        kiv.b = hsb.b
        self.layer_norm(kiv, ki_l, self.ikgb, 32, self.ikgb[:, 0, :], self.ikgb[:, 1, :], self.scrA, gb_eng="dve")
        self.V(lambda e: e.tensor_copy(out=hsb[:, 1664:1696], in_=ki_l[:]), [ki_l], [hsb])
        self.rope(hsb, 1664, 1, 32, self.cs32, n, ki_r, ki_r[:], t1, t2)
        yield
        bv = self.transposes(ps[2], [(qa_r, qa_r[:].rearrange("p a b -> p (a b)")[:, i * 128:(i + 1) * 128], 128) for i in range(4)])
        for g in range(2):
            pp = slice(64 * g, 64 * g + 64)
            self.V(lambda e, bv=bv, g=g, pp=pp: e.tensor_copy(out=qaTz[g][pp].rearrange("p a b -> p (a b)"), in_=bv[pp, 0:512]), [ps[2]], [qaTz[g]])
        bv = self.transposes(ps[2], [(qb_r, qb_r[:].rearrange("p a b -> p (a b)")[:, i * 128:(i + 1) * 128], 128) for i in range(4)], col0=512)
        for g in range(2):
            pp = slice(64 * g, 64 * g + 64)
            self.V(lambda e, bv=bv, g=g, pp=pp: e.tensor_copy(out=qbTz[g][pp].rearrange("p a b -> p (a b)"), in_=bv[pp, 512:1024]), [ps[2]], [qbTz[g]])
        yield
        bv = self.transposes(ps[0], [(qi_r, qi_r[:, h, :], 32) for h in range(8)])
        self.V(lambda e, bv=bv: e.tensor_copy(out=qiT[0:32].rearrange("p a b -> p (a b)"), in_=bv[0:32, 0:1024]), [ps[0]], [qiT])
        bv = self.transposes(ps[1], [(ka_r, ka_r[:].rearrange("p a b -> p (a b)"), 128), (kb_r, kb_r[:].rearrange("p a b -> p (a b)"), 128), (ki_r, ki_r[:, 0, :], 32)])
        self.V(lambda e, bv=bv: e.tensor_copy(out=kaT[:, nb], in_=bv[:, 0:128]), [ps[1]], [B["kaT_b"][n]])
        self.V(lambda e, bv=bv: e.tensor_copy(out=kbT[:, nb], in_=bv[:, 128:256]), [ps[1]], [B["kbT_b"][n]])
        self.V(lambda e, bv=bv: e.tensor_copy(out=kiT[0:32, nb], in_=bv[0:32, 256:384]), [ps[1]], [B["kiT_b"][n]])
        yield
        for g in range(2):
            rhs_q = qaTz[g][:].rearrange("p a b -> p (a b)")
            self.P(lambda e, rhs_q=rhs_q: e.matmul(ps[0][:, 0:512], lhsT=kaT[:, nb], rhs=rhs_q, start=True, stop=False), [B["kaT_b"][n], qaTz[g]], [ps[0]])
            self.P(lambda e: e.matmul(ps[0][:, 0:512], lhsT=self.ident[:], rhs=bcast(self.mb12[:, 0, :], 4), start=False, stop=True), [self.ident, self.mb12], [ps[0]])
            if n > 0:
                pb = slice((n - 1) * 128, n * 128)
                self.P(lambda e, rhs_q=rhs_q, pb=pb: e.matmul(ps[1][:, 0:512], lhsT=kaT[:, pb], rhs=rhs_q, start=True, stop=False), [B["kaT_b"][n - 1], qaTz[g]], [ps[1]])
                self.P(lambda e: e.matmul(ps[1][:, 0:512], lhsT=self.ident[:], rhs=bcast(self.mb12[:, 1, :], 4), start=False, stop=True), [self.ident, self.mb12], [ps[1]])
                self.A(lambda e: e.activation(out=E[:, 0:1024], in_=psf[0][:, 0:1024], func=AF.Exp, scale=0.125), [ps[0], ps[1]], [E])
            else:
                self.A(lambda e: e.activation(out=E[:, 0:512], in_=ps[0][:, 0:512], func=AF.Exp, scale=0.125), [ps[0]], [E])
            self.P(lambda e, g=g: e.matmul(ps[2][0:64, 0:512], lhsT=va[:, n, g * 64:(g + 1) * 64], rhs=E[:, 0:512], start=True, stop=(n == 0)), [B["va_b"][n], E], [ps[2]])
            self.P(lambda e: e.matmul(ps[0][0:64, 0:512], lhsT=self.ones64[:], rhs=E[:, 0:512], start=True, stop=(n == 0)), [self.ones64, E], [ps[0]])
            if n > 0:
                self.P(lambda e, g=g: e.matmul(ps[2][0:64, 0:512], lhsT=va[:, n - 1, g * 64:(g + 1) * 64], rhs=E[:, 512:1024], start=False, stop=True), [B["va_b"][n - 1], E], [ps[2]])
                self.P(lambda e: e.matmul(ps[0][0:64, 0:512], lhsT=self.ones64[:], rhs=E[:, 512:1024], start=False, stop=True), [self.ones64, E], [ps[0]])
            for hh in range(4):
                self.V(lambda e, g=g, hh=hh: e.tensor_scalar(out=rec[0:64, hh * 128:(hh + 1) * 128], in0=ps[0][0:64, hh * 128:(hh + 1) * 128], scalar1=self.esink[0:64, 4 * g + hh:4 * g + hh + 1], scalar2=None, op0=ALU.add), [ps[0], self.esink], [rec])
            self.A(lambda e: e.activation(out=rec[0:64, 0:512], in_=rec[0:64, 0:512], func=AF.Ln), [rec], [rec])
            self.A(lambda e: e.activation(out=rec[0:64, 0:512], in_=rec[0:64, 0:512], func=AF.Exp, scale=-1.0), [rec], [rec])
            ev = lambda ap: ap.rearrange("p (a two b) -> p a two b", two=2, b=128)
            self.V(lambda e, g=g: e.tensor_tensor(out=oT2[0:64, 2 * g:2 * g + 2, :], in0=ev(ps[2][0:64, 0:512])[:, :, 0, :], in1=ev(rec[0:64, 0:512])[:, :, 0, :], op=ALU.mult), [ps[2], rec], [oT2])
            self.V(lambda e: e.tensor_tensor(out=ostg[0:64, :, :], in0=ev(ps[2][0:64, 0:512])[:, :, 1, :], in1=ev(rec[0:64, 0:512])[:, :, 1, :], op=ALU.mult), [ps[2], rec], [ostg])
            self.DMA("sp", lambda e, g=g: e.dma_start(out=oT2[64:128, 2 * g:2 * g + 2, :], in_=ostg[0:64, :, :]), [ostg], [oT2])
            yield

    def gen_IDX(self, l, n, B):
        NT, S_, KTOP = self.NT, self.S_, self.KTOP
        ps = self.ps
        par = n % 2
        nb = slice(n * 128, (n + 1) * 128)
        w_in, kaT, kbT, kiT, va, vbx, idx = B["w_in"], B["kaT"], B["kbT"], B["kiT"], B["va"], B["vbx"], B["idx"]
        mask, xb, xT, hsb, t1, t2 = B["mask"][par], B["xb"][par], B["xT"], B["hsb"], B["t1"], B["t2"]
        qa_r, qb_r, ka_r, kb_r, qi_f, qi_r, ki_l, ki_r = B["qa_r"], B["qb_r"], B["ka_r"], B["kb_r"], B["qi_f"], B["qi_r"], B["ki_l"], B["ki_r"]
        wsm, qaTz, qbTz, qiT = B["wsm"][par], B["qaTz"], B["qbTz"][n % 3], B["qiT"][par]
        E, R, oT2, ostg, rec = B["E_f"], B["R"], B["oT2"][n % 3], B["ostg_f"], B["rec_f"]
        psf = self.psfull
        bs, m8 = B["bs"], B["m8"]
        Nn = (n + 1) * 128
        nblk = (Nn + 511) // 512
        cnt = 0
        for jb in range(nblk):
            Wb = min(512, Nn - jb * 512)
            cs_ = slice(jb * 512, jb * 512 + Wb)
            for h2 in range(4):
                pr = cnt % 2
                Rb = R[pr]
                cnt += 1
                for u in range(2):
                    h = 2 * h2 + u
                    bank = ps[2 * pr + u]
                    self.P(lambda e, bank=bank, h=h, cs_=cs_, Wb=Wb: e.matmul(bank[:, 0:Wb], lhsT=qiT[:, h, :], rhs=kiT[:, cs_], start=True, stop=True), [qiT] + B["kiT_b"][4 * jb:min(n + 1, 4 * jb + 4)], [bank])
                if False and h2 == 3:
                    for u in range(2):
                        h = 2 * h2 + u
                        bank = ps[2 * pr + u]
                        self.V(lambda e, bank=bank, Rb=Rb, u=u, h=h, Wb=Wb: e.tensor_scalar(out=Rb[:, u, 0:Wb], in0=bank[:, 0:Wb], scalar1=0.0, scalar2=wsm[:, 2, h:h + 1], op0=ALU.max, op1=ALU.mult), [bank, wsm], [Rb])
                        self.V(lambda e, Rb=Rb, u=u, cs_=cs_, Wb=Wb: e.tensor_tensor(out=idx[:, cs_], in0=idx[:, cs_], in1=Rb[:, u, 0:Wb], op=ALU.add), [Rb, idx], [idx])
                    yield
                    continue
                self.A(lambda e, pr=pr, Rb=Rb, Wb=Wb: e.activation(out=Rb[:, :, 0:Wb], in_=psf[pr].rearrange("p (a b) -> p a b", b=512)[:, :, 0:Wb], func=AF.Relu), [ps[2 * pr], ps[2 * pr + 1]], [Rb])
                for u in range(2):
                    h = 2 * h2 + u
                    if h == 0:
                        self.V(lambda e, Rb=Rb, cs_=cs_, Wb=Wb: e.tensor_scalar(out=idx[:, cs_], in0=Rb[:, 0, 0:Wb], scalar1=wsm[:, 2, 0:1], scalar2=None, op0=ALU.mult), [Rb, wsm], [idx])
                    else:
                        self.V(lambda e, Rb=Rb, cs_=cs_, Wb=Wb, h=h, u=u: e.scalar_tensor_tensor(out=idx[:, cs_], in0=Rb[:, u, 0:Wb], scalar=wsm[:, 2, h:h + 1], in1=idx[:, cs_], op0=ALU.mult, op1=ALU.add), [Rb, wsm, idx], [idx])
                yield
        self.G(lambda e: e.affine_select(out=idx[:, nb], in_=idx[:, nb], pattern=[[-1, 128]], compare_op=ALU.is_ge, fill=self.freg(e, -1e30), base=0, channel_multiplier=1), [idx], [idx])

    def gen_BIS(self, l, n, B):
        NT, S_, KTOP = self.NT, self.S_, self.KTOP
        ps = self.ps
        par = n % 2
        nb = slice(n * 128, (n + 1) * 128)
        w_in, kaT, kbT, kiT, va, vbx, idx = B["w_in"], B["kaT"], B["kbT"], B["kiT"], B["va"], B["vbx"], B["idx"]
        mask, xb, xT, hsb, t1, t2 = B["mask"][par], B["xb"][par], B["xT"], B["hsb"], B["t1"], B["t2"]
        qa_r, qb_r, ka_r, kb_r, qi_f, qi_r, ki_l, ki_r = B["qa_r"], B["qb_r"], B["ka_r"], B["kb_r"], B["qi_f"], B["qi_r"], B["ki_l"], B["ki_r"]
        wsm, qaTz, qbTz, qiT = B["wsm"][par], B["qaTz"], B["qbTz"][n % 3], B["qiT"][par]
        E, R, oT2, ostg, rec = B["E_f"], B["R"], B["oT2"][n % 3], B["ostg_f"], B["rec_f"]
        psf = self.psfull
        bs, m8 = B["bs"], B["m8"]
        Nn = (n + 1) * 128
        if Nn <= KTOP:
            self.V(lambda e: e.tensor_scalar(out=mask[:, 0:Nn], in0=idx[:, 0:Nn], scalar1=-1e29, scalar2=None, op0=ALU.is_gt), [idx], [mask])
            yield
        else:
            NI = self.NITER
            self.V(lambda e: e.max(out=m8[:, 0:8], in_=idx[:, 0:Nn]), [idx], [m8])
            self.V(lambda e: e.tensor_reduce(out=bs[:, 0:1], in_=idx[:, 0:KTOP], axis=AX.X, op=ALU.min), [idx], [bs])
            self.V(lambda e: e.tensor_scalar(out=bs[:, 0:1], in0=bs[:, 0:1], scalar1=-1e-3, scalar2=None, op0=ALU.add), [bs], [bs])
            self.V(lambda e: e.tensor_scalar(out=m8[:, 7:8], in0=m8[:, 7:8], scalar1=1e-3, scalar2=None, op0=ALU.add), [m8], [m8])
            self.V(lambda e: e.tensor_tensor(out=bs[:, 1:2], in0=m8[:, 7:8], in1=bs[:, 0:1], op=ALU.subtract), [m8, bs], [bs])
            self.V(lambda e: e.tensor_scalar(out=bs[:, 8:8 + NI + 2], in0=self.p2n[:, 0:NI + 2], scalar1=bs[:, 1:2], scalar2=None, op0=ALU.mult), [bs, self.p2n], [bs])
            self.V(lambda e: e.scalar_tensor_tensor(out=bs[:, 2:3], in0=bs[:, 0:1], scalar=-1.0, in1=bs[:, 9:10], op0=ALU.mult, op1=ALU.add), [bs], [bs])
            yield
            for it in range(1, NI + 1):
                self.A(lambda e: e.activation(out=mask[:, 0:Nn], in_=idx[:, 0:Nn], func=AF.Sign, bias=bs[:, 2:3], scale=1.0, accum_out=bs[:, 3:4]), [idx, bs], [mask, bs])
                self.A(lambda e: e.activation(out=bs[:, 5:6], in_=bs[:, 2:3], func=AF.Copy), [bs], [bs])
                self.A(lambda e: e.activation(out=bs[:, 4:5], in_=bs[:, 3:4], func=AF.Sign, bias=self.cb[:, n:n + 1], scale=1.0), [bs, self.cb], [bs])
                self.A(lambda e, it=it: e.activation(out=bs[:, 2:3], in_=bs[:, 4:5], func=AF.Identity, scale=bs[:, 8 + it + 1:8 + it + 2], bias=bs[:, 2:3]), [bs], [bs])
                yield
            self.V(lambda e: e.scalar_tensor_tensor(out=bs[:, 7:8], in0=bs[:, 2:3], scalar=-1.0, in1=bs[:, 8 + NI + 1:8 + NI + 2], op0=ALU.mult, op1=ALU.add), [bs], [bs])
            self.V(lambda e: e.tensor_scalar(out=mask[:, 0:Nn], in0=idx[:, 0:Nn], scalar1=bs[:, 7:8], scalar2=None, op0=ALU.is_gt), [idx, bs], [mask])
            yield

    def gen_Y(self, l, n, B):
        ps = self.ps
        par = n % 2
        w_o, kbT, vbx = B["w_o"], B["kbT"], B["vbx"]
        mask, qbTz, oT2, ostg = B["mask"][par], B["qbTz"][n % 3], B["oT2"][n % 3], B["ostg_b"]
        psf = self.psfull
        rec_hi, rec_lo, xres, yt = B["rec_hi"], B["rec_lo"], B["xres"], B["yt"]
        self.DMA("sp", lambda e: e.dma_start(out=xres.ap, in_=self.xs_d.ap[n * 128:(n + 1) * 128, :]), [self.xs_d], [xres])
        yield
        cnt = 0
        for j0 in range(0, n + 1, 8):
            js = list(range(j0, min(n + 1, j0 + 8)))
            mT = B["mT"][(j0 // 8) % 2]
            bv = self.transposes(ps[3], [(mask, mask[:, j * 128:(j + 1) * 128], 128) for j in js])
            nj = len(js)
            self.A(lambda e, bv=bv, mT=mT, nj=nj: e.activation(out=mT[:].rearrange("p a b -> p (a b)")[:, 0:nj * 128], in_=bv[:, 0:nj * 128], func=AF.Identity, scale=30000.0, bias=self.negbig[:, 0:1]), [ps[3], self.negbig], [mT])
            yield
            for j in js:
                jb_ = slice(j * 128, (j + 1) * 128)
                E = B["E_b"][cnt % 2]
                cnt += 1
                for hf in range(2):
                    self.P(lambda e, hf=hf, jb_=jb_: e.matmul(ps[4 + hf][:, 0:512], lhsT=kbT[:, jb_], rhs=qbTz[hf][:].rearrange("p a b -> p (a b)"), start=True, stop=False), [B["kbT_b"][j], qbTz[hf]], [ps[4 + hf]])
                    self.P(lambda e, hf=hf, mT=mT, j=j, j0=j0: e.matmul(ps[4 + hf][:, 0:512], lhsT=self.ident[:], rhs=bcast(mT[:, j - j0, :], 4), start=False, stop=True), [self.ident, mT], [ps[4 + hf]])
                self.A(lambda e, E=E: e.activation(out=E[:, 0:1024], in_=psf[2][:, 0:1024], func=AF.Exp, scale=0.125), [ps[4], ps[5]], [E])
                for hf in range(2):
                    self.P(lambda e, hf=hf, j=j, E=E: e.matmul(ps[6 + hf][:, 0:512], lhsT=vbx[:, j, :], rhs=E[:, hf * 512:(hf + 1) * 512], start=(j == 0), stop=(j == n)), [B["vbx_b"][j], E], [ps[6 + hf]])
                yield
        ev = lambda ap: ap.rearrange("p (a two b) -> p a two b", two=2, b=128)
        for hf in range(2):
            hs = slice(hf * 512, (hf + 1) * 512)
            self.A(lambda e, hf=hf, hs=hs: e.activation(out=rec_hi[64:128, hs], in_=ps[6 + hf][64:128, 0:512], func=AF.Ln), [ps[6 + hf]], [rec_hi])
            self.A(lambda e, hs=hs: e.activation(out=rec_hi[64:128, hs], in_=rec_hi[64:128, hs], func=AF.Exp, scale=-1.0), [rec_hi], [rec_hi])
            self.DMA("sp", lambda e, hs=hs: e.dma_start(out=rec_lo[0:64, hs], in_=rec_hi[64:128, hs]), [rec_hi], [rec_lo])
        yield
        for hf in range(2):
            hs = slice(hf * 512, (hf + 1) * 512)
            self.V(lambda e, hf=hf, hs=hs: e.tensor_tensor(out=oT2[0:64, 4 + 2 * hf:6 + 2 * hf, :], in0=ev(ps[6 + hf][0:64, 0:512])[:, :, 0, :], in1=ev(rec_lo[0:64, hs])[:, :, 0, :], op=ALU.mult), [ps[6 + hf], rec_lo], [oT2])
            self.V(lambda e, hf=hf, hs=hs: e.tensor_tensor(out=ostg[0:64, :, :], in0=ev(ps[6 + hf][0:64, 0:512])[:, :, 1, :], in1=ev(rec_lo[0:64, hs])[:, :, 1, :], op=ALU.mult), [ps[6 + hf], rec_lo], [ostg])
            self.DMA("sp", lambda e, hf=hf: e.dma_start(out=oT2[64:128, 4 + 2 * hf:6 + 2 * hf, :], in_=ostg[0:64, :, :]), [ostg], [oT2])
        yield
        for hf in range(2):
            hs = slice(hf * 512, (hf + 1) * 512)
            for k in range(8):
                self.P(lambda e, hf=hf, k=k, hs=hs: e.matmul(ps[4 + hf][:, 0:512], lhsT=oT2[:, k, :], rhs=w_o[:, k, hs], start=(k == 0), stop=(k == 7)), [oT2, w_o], [ps[4 + hf]])
            self.V(lambda e, hf=hf, hs=hs: e.scalar_tensor_tensor(out=yt[:, hs], in0=xres[:, hs], scalar=ALPHA, in1=ps[4 + hf][:, 0:512], op0=ALU.mult, op1=ALU.add), [xres, ps[4 + hf]], [yt])
            yield
        self.layer_norm(yt, yt, self.lnb, D, self.lnb[:, 0, :], self.lnb[:, 1, :], self.scrB)
        self.DMA("sp", lambda e: e.dma_start(out=self.x1_d.ap[n * 128:(n + 1) * 128, :], in_=yt.ap), [yt], [self.x1_d])
        yield

    def phase_ffn_a(self, l):
        NT = self.NT
        ps = self.ps
        self.S.barrier()
        self.ar.reset()
        aa = self.ar.alloc
        TG_T = min(4, NT)
        TG = TG_T * 128
        NG = NT // TG_T
        w_gu = aa("w_gu", [8, 2 * DFF], BF16)
        xbg = aa("xbg", [TG_T, D], BF16)
        xTg = [aa(f"xTg{i}", [8, TG], BF16) for i in range(2)]
        sgt = [aa(f"sgt{i}", [TG], F32) for i in range(2)]
        hidg = aa("hidg", [22, TG], BF16)
        hbuf = [Buf("hid_lo"), Buf("hid_hi")]
        rng = [(0, 6), (6, 12), (12, 18), (18, 22)]
        wsrc = self.wgu_d[l].rearrange("(k p) c -> p k c", p=128)
        wb = {}
        for (c0, c1) in rng:
            for part, base in (("g", 0), ("u", DFF)):
                b_ = Buf(f"wgu_{part}{c0}")
                for c in range(c0, c1):
                    wb[(part, c)] = b_
                self.DMA("pool", lambda e, c0=c0, c1=c1, base=base: e.dma_start(out=w_gu.ap[:, :, base + c0 * 128:base + c1 * 128], in_=wsrc[:, :, base + c0 * 128:base + c1 * 128]), [], [b_])

        def prologue(G):
            xT = xTg[G % 2]
            self.DMA("pool", lambda e: e.dma_start(out=xbg.ap, in_=self.x1_d.ap[G * TG:(G + 1) * TG, :].rearrange("(t p) c -> p t c", p=128)), [self.x1_d], [xbg])
            for ti in range(TG_T):
                bv = self.transposes(ps[4 + (ti % 2)], [(xbg, xbg[:, ti, k * 128:(k + 1) * 128], 128) for k in range(8)])
                self.A(lambda e, bv=bv, ti=ti: e.copy(out=xT[:, :, ti * 128:(ti + 1) * 128], in_=bv[:, 0:1024].rearrange("p (a b) -> p a b", b=128)), [ps[4 + (ti % 2)]], [xT])

        prologue(0)
        for G in range(NG):
            xT = xTg[G % 2]
            for c in range(22):
                if c == 4 and G + 1 < NG:
                    prologue(G + 1)
                bg, bu, sg = ps[c % 2], ps[2 + (c % 2)], sgt[c % 2]
                hb = hbuf[0 if c < 11 else 1]
                for k in range(8):
                    self.P(lambda e, bg=bg, k=k, c=c, xT=xT: e.matmul(bg[:, 0:TG], lhsT=w_gu[:, k, c * 128:(c + 1) * 128], rhs=xT[:, k, :], start=(k == 0), stop=(k == 7)), [wb[("g", c)], xT], [bg])
                for k in range(8):
                    self.P(lambda e, bu=bu, k=k, c=c, xT=xT: e.matmul(bu[:, 0:TG], lhsT=w_gu[:, k, DFF + c * 128:DFF + (c + 1) * 128], rhs=xT[:, k, :], start=(k == 0), stop=(k == 7)), [wb[("u", c)], xT], [bu])
                self.A(lambda e, bg=bg, sg=sg: e.activation(out=sg[:, 0:TG], in_=bg[:, 0:TG], func=AF.Silu), [bg], [sg])
                self.V(lambda e, bu=bu, c=c, sg=sg: e.tensor_tensor(out=hidg[:, c, :], in0=sg[:, 0:TG], in1=bu[:, 0:TG], op=ALU.mult), [sg, bu], [hb])
                if c == 10 or c == 21:
                    c0 = 0 if c == 10 else 11
                    self.DMA("sp", lambda e, G=G, c0=c0, c=c: e.dma_start(out=self.hid_d.ap[c0:c + 1, :, G * TG:(G + 1) * TG].rearrange("c p t -> p c t"), in_=hidg.ap[:, c0:c + 1, :]), [hb], [self.hid_d])

    def phase_ffn_b(self, l):
        NT = self.NT
        ps = self.ps
        self.S.barrier()
        self.ar.reset()
        aa = self.ar.alloc
        w_dn = aa("w_dn", [22, D], BF16)
        w_pg = aa("w_pg", [8, D], BF16)
        w_pp = aa("w_pp", [2, D], BF16)
        xres = [aa(f"xres{i}", [D], F32) for i in range(2)]
        xb = [aa(f"xb{i}", [D], BF16) for i in range(2)]
        xT = [aa(f"xT{i}", [8, 128], BF16) for i in range(2)]
        hidT = [aa(f"hidT{i}", [22, 128], BF16) for i in range(2)]
        pb = [aa(f"pb{i}", [256], BF16) for i in range(2)]
        pT = [aa(f"pT{i}", [2, 128], BF16) for i in range(2)]
        sgm = [aa(f"sgm{i}", [D], F32) for i in range(2)]
        yt = [aa(f"yt{i}", [D], F32) for i in range(2)]
        if l + 1 < self.DEPTH:
            self.load_mixer_weights(l + 1)
        self.load_w(w_dn, self.wdn_d[l].rearrange("(c p) m -> p c m", p=128))
        self.load_w(w_pg, self.wpg_d[l].rearrange("(k p) m -> p k m", p=128))
        self.load_w(w_pp, self.wpp_d[l].rearrange("(k p) m -> p k m", p=128))
        self.load_ln(2 + 2 * l)
        last = (l == self.DEPTH - 1)

        def prologue(n):
            i = n % 2
            self.DMA("sp", lambda e: e.dma_start(out=xres[i].ap, in_=self.x1_d.ap[n * 128:(n + 1) * 128, :]), [self.x1_d], [xres[i]])
            self.DMA("sp", lambda e: e.dma_start(out=hidT[i].ap, in_=self.hid_d.ap[:, :, n * 128:(n + 1) * 128].rearrange("c p t -> p c t")), [self.hid_d], [hidT[i]])
            self.DMA("pool", lambda e: e.dma_start(out=xb[i].ap, in_=self.x1_d.ap[n * 128:(n + 1) * 128, :]), [self.x1_d], [xb[i]])
            self.DMA("pool", lambda e: e.dma_start(out=pb[i].ap, in_=self.p_d[l, n * 128:(n + 1) * 128, :]), [], [pb[i]])
            bv = self.transposes(ps[6], [(xb[i], xb[i][:, k * 128:(k + 1) * 128], 128) for k in range(8)])
            self.A(lambda e, bv=bv: e.copy(out=xT[i][:].rearrange("p a b -> p (a b)"), in_=bv[:, 0:1024]), [ps[6]], [xT[i]])
            bv = self.transposes(ps[7], [(pb[i], pb[i][:, k * 128:(k + 1) * 128], 128) for k in range(2)])
            self.A(lambda e, bv=bv: e.copy(out=pT[i][:].rearrange("p a b -> p (a b)"), in_=bv[:, 0:256]), [ps[7]], [pT[i]])

        prologue(0)
        for n in range(NT):
            i = n % 2
            if n + 1 < NT:
                prologue(n + 1)
            for hf in range(2):
                hs = slice(hf * 512, (hf + 1) * 512)
                for c in range(22):
                    self.P(lambda e, hf=hf, c=c, hs=hs, i=i: e.matmul(ps[0 + hf][:, 0:512], lhsT=hidT[i][:, c, :], rhs=w_dn[:, c, hs], start=(c == 0), stop=(c == 21)), [hidT[i], w_dn], [ps[0 + hf]])
                for k in range(8):
                    self.P(lambda e, hf=hf, k=k, hs=hs, i=i: e.matmul(ps[2 + hf][:, 0:512], lhsT=xT[i][:, k, :], rhs=w_pg[:, k, hs], start=(k == 0), stop=(k == 7)), [xT[i], w_pg], [ps[2 + hf]])
                for k in range(2):
                    self.P(lambda e, hf=hf, k=k, hs=hs, i=i: e.matmul(ps[4 + hf][:, 0:512], lhsT=pT[i][:, k, :], rhs=w_pp[:, k, hs], start=(k == 0), stop=(k == 1)), [pT[i], w_pp], [ps[4 + hf]])
                self.A(lambda e, hf=hf, hs=hs, i=i: e.activation(out=sgm[i][:, hs], in_=ps[2 + hf][:, 0:512], func=AF.Sigmoid), [ps[2 + hf]], [sgm[i]])
                self.V(lambda e, hf=hf, hs=hs, i=i: e.tensor_tensor(out=sgm[i][:, hs], in0=sgm[i][:, hs], in1=ps[4 + hf][:, 0:512], op=ALU.mult), [sgm[i], ps[4 + hf]], [sgm[i]])
                self.V(lambda e, hf=hf, hs=hs, i=i: e.scalar_tensor_tensor(out=yt[i][:, hs], in0=xres[i][:, hs], scalar=ALPHA, in1=ps[0 + hf][:, 0:512], op0=ALU.mult, op1=ALU.add), [xres[i], ps[0 + hf]], [yt[i]])
                self.V(lambda e, hs=hs, i=i: e.tensor_tensor(out=yt[i][:, hs], in0=yt[i][:, hs], in1=sgm[i][:, hs], op=ALU.add), [yt[i], sgm[i]], [yt[i]])
            self.layer_norm(yt[i], yt[i], self.lnb, D, self.lnb[:, 0, :], self.lnb[:, 1, :], self.scrA if i == 0 else self.scrB, gb_eng="dve")
            if last:
                self.DMA("sp", lambda e, n=n, i=i: e.dma_start(out=self.y_d[n * 128:(n + 1) * 128, :], in_=yt[i].ap), [yt[i]], [self.yb])
            else:
                self.DMA("sp", lambda e, n=n, i=i: e.dma_start(out=self.xs_d.ap[n * 128:(n + 1) * 128, :], in_=yt[i].ap), [yt[i]], [self.xs_d])


def make_in_maps(x, p, positions, ln_in_g, ln_in_b, w_in, attn_sinks, idx_k_g, idx_k_b, w_o,
                 ln1_g, ln1_b, w_gu, w_down, w_pg, w_pp, ln2_g, ln2_b, NT, DEPTH):
    f = lambda a: np.ascontiguousarray(np.asarray(a), dtype=np.float32)
    B = x.shape[0]
    rows = [np.concatenate([f(ln_in_g), f(ln_in_b)])]
    for l in range(DEPTH):
        rows.append(np.concatenate([f(ln1_g)[l], f(ln1_b)[l]]))
        rows.append(np.concatenate([f(ln2_g)[l], f(ln2_b)[l]]))
    lnp = np.stack(rows).astype(np.float32)
    inv = np.concatenate([
        (10000.0 ** (-np.arange(32, dtype=np.float32) / np.float32(32))).astype(np.float32),
        (10000.0 ** (-np.arange(16, dtype=np.float32) / np.float32(16))).astype(np.float32)]).astype(np.float32)
    ikgb = np.concatenate([f(idx_k_g), f(idx_k_b)], axis=1).astype(np.float32)
    shared = {"inv": inv, "lnp": lnp, "w_in": f(w_in), "w_o": f(w_o), "w_gu": f(w_gu), "w_down": f(w_down),
              "w_pg": f(w_pg), "w_pp": f(w_pp), "sinks": f(attn_sinks), "ikgb": ikgb}
    maps = []
    pos = np.asarray(positions).astype(np.int32)
    for b in range(B):
        m = dict(shared)
        m["x"] = f(x[b])
        m["p"] = f(np.asarray(p)[:, b])
        m["pos"] = np.ascontiguousarray(pos[b].reshape(NT, 128).T)
        maps.append(m)
    return maps


_CACHE = {}


def kernel(**inputs):
    x = np.asarray(inputs["x"])
    B, S_, _ = x.shape
    NT = S_ // 128
    DEPTH = np.asarray(inputs["w_in"]).shape[0]
    KTOP = min(256, S_ // 4)
    key = (NT, KTOP, DEPTH)
    if key not in _CACHE:
        _CACHE[key] = Builder(NT=NT, KTOP=KTOP, DEPTH=DEPTH)
    bld = _CACHE[key]
    maps = make_in_maps(NT=NT, DEPTH=DEPTH, **inputs)
    res = run_bass_kernel_spmd(bld.nc, maps, core_ids=list(range(B)))
    out = np.stack([np.asarray(r["y"]) for r in res.results], axis=0).astype(np.float32)
    return out
```
